# Optimizing a Trainium2 kernel written in Bass

```python
import math
import jax, jax.numpy as jnp
from jax import lax
import numpy as np

D_MODEL = 1024
BATCH = 16
SEQ = 2048
DEPTH = 1

N_MEM = 256
HEAD_DIM = 64
ATT_HEADS = 8
ATT_KV_HEADS = 2
ATT_WIDTH = ATT_HEADS * HEAD_DIM
KV_WIDTH = ATT_KV_HEADS * HEAD_DIM
IDX_HEADS = 8
IDX_DIM = 32
TOPK_MAX = 256
Q_BLOCK = 128
SSM_WIDTH = D_MODEL // 4
SSM_GROUP = 16
SSM_GROUPS = SSM_WIDTH // SSM_GROUP
SSM_STATE = 64
DT_MIN = 0.001
DT_MAX = 0.1
MEM_HEADS = 4
MEM_WIDTH = MEM_HEADS * HEAD_DIM
MIX_WIDTH = ATT_WIDTH + SSM_WIDTH + MEM_WIDTH
ROPE_THETA = 500000.0
ROPE_FRAC = 4
LN_EPS = 1e-5
DN_ALPHA = (2.0 * DEPTH) ** 0.25
DN_BETA = (8.0 * DEPTH) ** -0.25

SPLITS = [
    ATT_WIDTH,
    KV_WIDTH,
    KV_WIDTH,
    IDX_HEADS * IDX_DIM,
    IDX_DIM,
    IDX_HEADS,
    ATT_WIDTH,
    SSM_WIDTH,
    SSM_WIDTH,
    MEM_WIDTH,
    MEM_WIDTH,
]
IN_WIDTH = int(sum(SPLITS))
SPLIT_POINTS = [int(s) for s in np.cumsum(SPLITS)[:-1]]

kernel_name = "hymba_dsa_s5_memxattn_deepnorm"


def rope_partial(x, pos):
    d = x.shape[-1]
    r = d // ROPE_FRAC
    half = r // 2
    inv = ROPE_THETA ** (-jnp.arange(0, half, dtype=jnp.float32) * 2.0 / r)
    ang = pos.astype(jnp.float32)[:, None] * inv[None, :]
    cos = jnp.cos(ang)[:, None, :]
    sin = jnp.sin(ang)[:, None, :]
    xf = x.astype(jnp.float32)
    x1, x2, xp = xf[..., :half], xf[..., half:r], xf[..., r:]
    out = jnp.concatenate([x1 * cos - x2 * sin, x2 * cos + x1 * sin, xp], axis=-1)
    return out.astype(x.dtype)


def dsa_attention(q, k, v, q_idx, k_idx, w_idx):
    B, L = q.shape[0], q.shape[1]
    topk = min(TOPK_MAX, L // 4)
    nb = L // Q_BLOCK
    rep = ATT_HEADS // ATT_KV_HEADS
    key_pos = jnp.arange(L)
    k_idx_f = k_idx.astype(jnp.float32)
    gather = jax.vmap(lambda a, i: a[i])

    def to_blocks(a):
        return a.reshape((B, nb, Q_BLOCK) + a.shape[2:]).swapaxes(0, 1)

    def block(args):
        qb, qib, wb, start = args
        qpos = start + jnp.arange(Q_BLOCK)
        s = jnp.einsum('bqhd,bsd->bqhs', qib.astype(jnp.float32), k_idx_f)
        score = jnp.einsum('bqh,bqhs->bqs', wb.astype(jnp.float32), jax.nn.relu(s))
        causal = key_pos[None, :] <= qpos[:, None]
        score = jnp.where(causal[None], score, -jnp.inf)
        _, idx = lax.top_k(score, topk)
        valid = idx <= qpos[None, :, None]
        kg = gather(k, idx)
        vg = gather(v, idx)
        qg = qb.reshape(B, Q_BLOCK, ATT_KV_HEADS, rep, HEAD_DIM)
        logits = jnp.einsum('bqgrd,bqkgd->bqgrk', qg, kg).astype(jnp.float32) * (HEAD_DIM ** -0.5)
        logits = jnp.where(valid[:, :, None, None, :], logits, -jnp.inf)
        p = jax.nn.softmax(logits, axis=-1).astype(v.dtype)
        o = jnp.einsum('bqgrk,bqkgd->bqgrd', p, vg)
        return o.reshape(B, Q_BLOCK, ATT_WIDTH)

    starts = jnp.arange(nb) * Q_BLOCK
    out = lax.map(block, (to_blocks(q), to_blocks(q_idx), to_blocks(w_idx), starts))
    return out.swapaxes(0, 1).reshape(B, L, ATT_WIDTH)


def s5_scan(u, lam_re, lam_im, log_dt, b_re, b_im, c_re, c_im, d_skip):
    f32 = jnp.float32
    lam = lax.complex(lam_re.astype(f32), lam_im.astype(f32))
    dt = jnp.exp(log_dt.astype(f32))[:, None]
    lam_bar = jnp.exp(lam * dt)
    b = lax.complex(b_re.astype(f32), b_im.astype(f32))
    b_bar = ((lam_bar - 1.0) / lam)[:, :, None] * b
    uf = u.astype(f32)
    bu = jnp.einsum('gpn,blgn->blgp', b_bar, uf.astype(jnp.complex64))
    a = jnp.broadcast_to(lam_bar, bu.shape)

    def combine(e1, e2):
        a1, b1 = e1
        a2, b2 = e2
        return a1 * a2, a2 * b1 + b2

    _, states = lax.associative_scan(combine, (a, bu), axis=1)
    c = lax.complex(c_re.astype(f32), c_im.astype(f32))
    y = jnp.einsum('gnp,blgp->blgn', c, states).real + d_skip.astype(f32) * uf
    return y.astype(u.dtype)


def layer_norm(h, g, b):
    hf = h.astype(jnp.float32)
    mu = jnp.mean(hf, axis=-1, keepdims=True)
    var = jnp.mean(jnp.square(hf - mu), axis=-1, keepdims=True)
    return ((hf - mu) * lax.rsqrt(var + LN_EPS) * g.astype(jnp.float32) + b.astype(jnp.float32))


def setup_inputs(seed: int = 0) -> dict:
    key = jax.random.key(seed)
    ks = jax.random.split(key, 20)
    f32 = jnp.float32
    G, P, N = SSM_GROUPS, SSM_STATE, SSM_GROUP
    x = jax.random.normal(ks[0], (BATCH, SEQ, D_MODEL), f32)
    mem = jax.random.normal(ks[1], (BATCH, N_MEM, D_MODEL), f32)
    w_in = jax.random.normal(ks[2], (D_MODEL, IN_WIDTH), f32) * D_MODEL ** -0.5
    w_mem_kv = jax.random.normal(ks[3], (D_MODEL, 2 * MEM_WIDTH), f32) * D_MODEL ** -0.5
    n = jnp.arange(P, dtype=f32)
    lam_re = -0.5 + 0.01 * jax.random.normal(ks[4], (G, P), f32)
    lam_im = math.pi * n[None, :] + 0.01 * jax.random.normal(ks[5], (G, P), f32)
    log_dt = jax.random.uniform(ks[6], (G,), f32, math.log(DT_MIN), math.log(DT_MAX))
    b_re = jax.random.normal(ks[7], (G, P, N), f32) * (2.0 * N) ** -0.5
    b_im = jax.random.normal(ks[8], (G, P, N), f32) * (2.0 * N) ** -0.5
    c_re = jax.random.normal(ks[9], (G, N, P), f32) * (2.0 * P) ** -0.5
    c_im = jax.random.normal(ks[10], (G, N, P), f32) * (2.0 * P) ** -0.5
    d_skip = jax.random.normal(ks[11], (G, N), f32)
    w_glu = jax.random.normal(ks[12], (SSM_WIDTH, SSM_WIDTH), f32) * SSM_WIDTH ** -0.5
    b_glu = 0.01 * jax.random.normal(ks[13], (SSM_WIDTH,), f32)
    w_out = jax.random.normal(ks[14], (MIX_WIDTH, D_MODEL), f32) * (MIX_WIDTH ** -0.5) * DN_BETA
    ln_g = 1.0 + 0.02 * jax.random.normal(ks[15], (D_MODEL,), f32)
    ln_b = 0.02 * jax.random.normal(ks[16], (D_MODEL,), f32)
    return {"x": x, "mem": mem, "w_in": w_in, "w_mem_kv": w_mem_kv,
            "lam_re": lam_re, "lam_im": lam_im, "log_dt": log_dt,
            "b_re": b_re, "b_im": b_im, "c_re": c_re, "c_im": c_im, "d_skip": d_skip,
            "w_glu": w_glu, "b_glu": b_glu, "w_out": w_out, "ln_g": ln_g, "ln_b": ln_b}


def reference(x, mem, w_in, w_mem_kv, lam_re, lam_im, log_dt, b_re, b_im, c_re, c_im,
              d_skip, w_glu, b_glu, w_out, ln_g, ln_b):
    B, L, _ = x.shape
    pos = jnp.arange(L)
    mkv = jnp.einsum('bmd,de->bme', mem, w_mem_kv)
    m_k, m_v = jnp.split(mkv, 2, axis=-1)
    m_k = m_k.reshape(B, N_MEM, MEM_HEADS, HEAD_DIM)
    m_v = m_v.reshape(B, N_MEM, MEM_HEADS, HEAD_DIM)
    h = x
    for _ in range(DEPTH):
        z = jnp.einsum('bld,de->ble', h, w_in)
        (q, k, v, q_idx, k_idx, w_idx, g_att, u, g_ssm, q_mem, g_mem) = jnp.split(z, SPLIT_POINTS, axis=-1)

        q = rope_partial(q.reshape(B, L, ATT_HEADS, HEAD_DIM), pos)
        k = rope_partial(k.reshape(B, L, ATT_KV_HEADS, HEAD_DIM), pos)
        v = v.reshape(B, L, ATT_KV_HEADS, HEAD_DIM)
        q_idx = rope_partial(q_idx.reshape(B, L, IDX_HEADS, IDX_DIM), pos)
        k_idx = rope_partial(k_idx[:, :, None, :], pos)[:, :, 0, :]
        w_idx = w_idx * (IDX_HEADS ** -0.5 * IDX_DIM ** -0.5)
        o_att = dsa_attention(q, k, v, q_idx, k_idx, w_idx)

        y = s5_scan(u.reshape(B, L, SSM_GROUPS, SSM_GROUP), lam_re, lam_im, log_dt,
                    b_re, b_im, c_re, c_im, d_skip).reshape(B, L, SSM_WIDTH)
        y = jax.nn.gelu(y)
        o_ssm = y * jax.nn.sigmoid(jnp.einsum('blc,ce->ble', y, w_glu) + b_glu)

        qm = q_mem.reshape(B, L, MEM_HEADS, HEAD_DIM)
        sm = jnp.einsum('blhd,bmhd->bhlm', qm, m_k).astype(jnp.float32) * (HEAD_DIM ** -0.5)
        pm = jax.nn.softmax(sm, axis=-1).astype(m_v.dtype)
        o_mem = jnp.einsum('bhlm,bmhd->blhd', pm, m_v).reshape(B, L, MEM_WIDTH)

        cat = jnp.concatenate([o_att * jax.nn.silu(g_att),
                               o_ssm * jax.nn.silu(g_ssm),
                               o_mem * jax.nn.silu(g_mem)], axis=-1)
        sub = jnp.einsum('ble,ed->bld', cat, w_out)
        h = layer_norm(DN_ALPHA * h.astype(jnp.float32) + sub.astype(jnp.float32), ln_g, ln_b).astype(x.dtype)
    return h
```

```python
import contextlib
import math
import numpy as np
import concourse.bass as bass
import concourse.mybir as mybir
from concourse.bass_utils import run_bass_kernel_spmd

DT = mybir.dt
F32, BF16, I32 = DT.float32, DT.bfloat16, DT.int32
ALU = mybir.AluOpType
AF = mybir.ActivationFunctionType
ESIZE = {F32: 4, BF16: 2, I32: 4}

ENGS = ["pe", "act", "dve", "pool", "sp"]
SB_BYTES = 206 * 1024
PS_BYTES = 16 * 1024
SLOT = 128
NDMA = 60

L = 2048
NB = 16
BPC = 2
TOPK = 256
NBIS = 24
DN_ALPHA = 2.0 ** 0.25
LN_EPS = 1e-5
BIG = 1.0e30
MASKNEG = -30000.0
PI = math.pi


class Prog:
    def __init__(self, nc):
        self.nc = nc
        self.ops = {e: [] for e in ENGS}
        self.kinds = ENGS + ["d%d" % i for i in range(NDMA)]
        self.kidx = {k: i for i, k in enumerate(self.kinds)}
        nk = len(self.kinds)
        self.nslots = {"sb": SB_BYTES // SLOT, "ps": PS_BYTES // SLOT}
        self.wk = {s: np.full(n, -1, np.int64) for s, n in self.nslots.items()}
        self.wv = {s: np.zeros(n, np.int64) for s, n in self.nslots.items()}
        self.rv = {s: np.zeros((nk, n), np.int64) for s, n in self.nslots.items()}
        self.seen = {e: np.zeros(nk, np.int64) for e in ENGS}
        self.dma_cnt = [0] * NDMA
        self.arena = nc.alloc_sbuf_tensor("arena", [128, SB_BYTES // 4], F32)
        self.psum = nc.alloc_psum_tensor("psum_all", [128, PS_BYTES // 4], F32)
        self.sb_off = 0
        self.out_dma = []
        self.nsem = 0
        self._dummy = self.sb([8], F32)
        self._dummy_act = self.sb([8], F32)

    def newsem(self):
        self.nsem += 1
        assert self.nsem <= NDMA
        return self.nsem - 1

    def sb(self, shape, dtype):
        n = int(np.prod(shape)) * ESIZE[dtype]
        n = (n + SLOT - 1) // SLOT * SLOT
        off = self.sb_off
        self.sb_off += n
        assert self.sb_off <= SB_BYTES, "SBUF arena overflow %d" % self.sb_off
        return self.sb_at(off, shape, dtype)

    def sb_at(self, off, shape, dtype):
        assert off % 4 == 0
        nel = int(np.prod(shape))
        nb = nel * ESIZE[dtype]
        assert off + nb <= SB_BYTES, "SBUF arena overflow (at) %d" % (off + nb)
        v = self.arena[:, off // 4:(off + nb + 3) // 4]
        if dtype != F32:
            v = v.bitcast(dtype)[:, 0:nel]
        if len(shape) > 1:
            names = " ".join("a%d" % i for i in range(len(shape)))
            kw = {"a%d" % i: int(s) for i, s in enumerate(shape)}
            v = v.rearrange("p (%s) -> p %s" % (names, names), **kw)
        return v

    def ps(self, bank, shape=(512,), dtype=F32, off=0):
        nel = int(np.prod(shape))
        nb = nel * ESIZE[dtype]
        b0 = bank * 2048 + off
        assert off + nb <= 2048
        v = self.psum[:, b0 // 4:(b0 + nb + 3) // 4]
        if dtype != F32:
            v = v.bitcast(dtype)[:, 0:nel]
        if len(shape) > 1:
            names = " ".join("a%d" % i for i in range(len(shape)))
            kw = {"a%d" % i: int(s) for i, s in enumerate(shape)}
            v = v.rearrange("p (%s) -> p %s" % (names, names), **kw)
        return v

    def _range(self, ap):
        sp = str(ap.space).lower()
        if "sb" in sp or "state" in sp:
            space, pitch = "sb", SB_BYTES
        elif "psum" in sp:
            space, pitch = "ps", PS_BYTES
        else:
            return None
        es = ESIZE[ap.dtype]
        off = (ap.offset * es) % pitch
        ext = 1
        for st, cnt in list(ap.ap)[1:]:
            ext += (cnt - 1) * abs(st)
        lo = off // SLOT
        hi = (off + ext * es - 1) // SLOT + 1
        if space == "ps":
            per = 2048 // SLOT
            lo = lo // per * per
            hi = (hi + per - 1) // per * per
        return space, lo, hi

    def _deps(self, eng, reads, writes):
        need = {}
        myk = self.kidx[eng]
        myseq = len(self.ops[eng]) + 1

        def add(k, v):
            if k < 0 or v <= 0:
                return
            if k == myk:
                if eng == "pe" or v < myseq - 1:
                    return
            if need.get(k, 0) < v:
                need[k] = v
        for ap in reads:
            r = self._range(ap)
            if r is None:
                continue
            s, lo, hi = r
            wk, wv = self.wk[s][lo:hi], self.wv[s][lo:hi]
            for k in np.unique(wk):
                if k >= 0:
                    add(int(k), int(wv[wk == k].max()))
            if s == "ps":
                rm = self.rv[s][:, lo:hi].max(axis=1)
                for k in np.nonzero(rm)[0]:
                    if int(k) != myk:
                        add(int(k), int(rm[k]))
        for ap in writes:
            r = self._range(ap)
            if r is None:
                continue
            s, lo, hi = r
            wk, wv = self.wk[s][lo:hi], self.wv[s][lo:hi]
            for k in np.unique(wk):
                if k >= 0:
                    add(int(k), int(wv[wk == k].max()))
            rm = self.rv[s][:, lo:hi].max(axis=1)
            for k in np.nonzero(rm)[0]:
                add(int(k), int(rm[k]))
        waits = {}
        seen = self.seen[eng]
        for k, v in need.items():
            if k == myk or seen[k] < v:
                waits[k] = v
                if k != myk:
                    seen[k] = v
        return waits

    def _mark(self, kind, val, reads, writes):
        for ap in reads:
            r = self._range(ap)
            if r is None:
                continue
            s, lo, hi = r
            self.rv[s][kind, lo:hi] = val
        for ap in writes:
            r = self._range(ap)
            if r is None:
                continue
            s, lo, hi = r
            self.wk[s][lo:hi] = kind
            self.wv[s][lo:hi] = val
            self.rv[s][:, lo:hi] = 0

    def op(self, eng, fn, reads=(), writes=(), accum=False):
        waits = self._deps(eng, reads, writes)
        self.ops[eng].append((fn, waits, ("accum", eng) if accum else None))
        self._mark(self.kidx[eng], len(self.ops[eng]), reads, writes)

    def dma(self, out, in_, sem, eng="sp", is_output=False, after=(), **kw):
        waits = self._deps(eng, [in_], [out])
        for s_, v_ in after:
            kk_ = self.kidx["d%d" % s_]
            if waits.get(kk_, 0) < v_:
                waits[kk_] = v_
        self.dma_cnt[sem] += 1
        val = 16 * self.dma_cnt[sem]
        fn = lambda e, out=out, in_=in_, kw=kw: e.dma_start(out=out, in_=in_, **kw)
        self.ops[eng].append((fn, waits, sem))
        self._mark(self.kidx["d%d" % sem], val, [in_], [out])
        if is_output:
            self.out_dma.append(sem)

    def emit(self):
        nc = self.nc
        waited = {e: set() for e in ENGS}
        selfw = {e: set() for e in ENGS}
        for e in ENGS:
            for fn, waits, dsem in self.ops[e]:
                for k, v in waits.items():
                    if k < len(ENGS):
                        waited[ENGS[k]].add(v)
                        if ENGS[k] == e:
                            selfw[e].add(v)
        rank = {e: {s: i + 1 for i, s in enumerate(sorted(waited[e]))} for e in ENGS}
        print("ops", {e: len(self.ops[e]) for e in ENGS}, "sem max", {e: len(rank[e]) for e in ENGS}, "dma", max(self.dma_cnt) * 16)
        with contextlib.ExitStack() as st:
            sems = {e: st.enter_context(nc.semaphore("s_" + e)) for e in ENGS}
            dsems = [st.enter_context(nc.semaphore("sd%d" % i)) for i in range(max(self.nsem, 1))]
            block = st.enter_context(nc.Block())

            def run(engname):
                def body(engine):
                    seq = 0
                    for fn, waits, dsem in self.ops[engname]:
                        seq += 1
                        for k, v in sorted(waits.items()):
                            if k < len(ENGS):
                                engine.wait_ge(sems[ENGS[k]], rank[ENGS[k]][v])
                            else:
                                engine.wait_ge(dsems[k - len(ENGS)], v)
                        ins = fn(engine)
                        if isinstance(dsem, tuple):
                            if seq in rank[engname] and seq not in selfw[engname]:
                                ins.then_inc(sems[engname], 1)
                            elif seq in rank[engname]:
                                if engname == "dve":
                                    ins = engine.memset(self._dummy[:, 0:1], 0.0)
                                else:
                                    ins = engine.activation(out=self._dummy_act[:, 0:1], in_=self._dummy_act[:, 1:2], func=AF.Copy)
                                ins.then_inc(sems[engname], 1)
                        elif dsem is not None:
                            ins.then_inc(dsems[dsem], 16)
                        elif seq in rank[engname]:
                            ins.then_inc(sems[engname], 1)
                    if engname == "sp":
                        for s in sorted(set(self.out_dma)):
                            engine.wait_ge(dsems[s], 16 * self.dma_cnt[s])
                return body
            block.tensor(run("pe"))
            block.scalar(run("act"))
            block.vector(run("dve"))
            block.gpsimd(run("pool"))
            block.sync(run("sp"))


def _aps(*xs):
    return [x for x in xs if not isinstance(x, (int, float)) and x is not None]


class K:
    def __init__(self, P):
        self.P = P

    def tt(self, eng, out, a, b, op):
        self.P.op(eng, lambda e: e.tensor_tensor(out=out, in0=a, in1=b, op=op), [a, b], [out])

    def ts(self, eng, out, a, s1, op0, s2=None, op1=None, accum_out=None):
        def fn(e):
            if op1 is None:
                return e.tensor_scalar(out=out, in0=a, scalar1=s1, scalar2=None, op0=op0)
            if accum_out is not None:
                return e.tensor_scalar(out=out, in0=a, scalar1=s1, scalar2=s2, op0=op0, op1=op1, accum_out=accum_out)
            return e.tensor_scalar(out=out, in0=a, scalar1=s1, scalar2=s2, op0=op0, op1=op1)
        self.P.op(eng, fn, [a] + _aps(s1, s2), [out] + _aps(accum_out), accum=accum_out is not None)

    def stt(self, out, a, s, b, op0, op1):
        self.P.op("dve", lambda e: e.scalar_tensor_tensor(out=out, in0=a, scalar=s, in1=b, op0=op0, op1=op1),
                  [a, b] + _aps(s), [out])

    def act(self, out, a, func, bias=None, scale=1.0, accum_out=None):
        def fn(e):
            kw = {}
            if bias is not None:
                kw["bias"] = bias
            if accum_out is not None:
                kw["accum_out"] = accum_out
            return e.activation(out=out, in_=a, func=func, scale=scale, **kw)
        self.P.op("act", fn, [a] + _aps(bias, scale), [out] + _aps(accum_out), accum=accum_out is not None)

    def cp(self, eng, out, a):
        if eng == "act":
            self.P.op("act", lambda e: e.activation(out=out, in_=a, func=AF.Copy), [a], [out])
        else:
            self.P.op(eng, lambda e: e.tensor_copy(out=out, in_=a), [a], [out])

    def memset(self, eng, out, val):
        self.P.op(eng, lambda e: e.memset(out, val), [], [out])

    def mm(self, out, lhsT, rhs, start, stop, skip=False):
        self.P.op("pe", lambda e: e.matmul(out, lhsT=lhsT, rhs=rhs, start=start, stop=stop, skip_group_check=skip),
                  [lhsT, rhs], [out])

    def tr(self, out, a, ident):
        self.P.op("pe", lambda e: e.transpose(out, a, ident), [a, ident], [out])


SPLITS = [512, 128, 128, 256, 32, 8, 512, 256, 256, 256, 256]
NFM = 24


def _swap_perm(width, hd, half):
    perm = np.arange(width)
    for c in range(width):
        d = c % hd
        if d < half:
            perm[c] = c + half
        elif d < 2 * half:
            perm[c] = c - half
    return perm


def _rope_tables(hd, reps):
    r = hd // 4
    half = r // 2
    inv = (np.float32(500000.0) ** (-np.arange(0, half, dtype=np.float32) * np.float32(2.0) / np.float32(r))).astype(np.float32)
    pos = np.arange(L, dtype=np.float32)
    ang = (pos[:, None] * inv[None, :]).astype(np.float32)
    cos = np.cos(ang).astype(np.float32).T
    sin = np.sin(ang).astype(np.float32).T
    C = np.ones((hd, L), np.float32)
    S = np.zeros((hd, L), np.float32)
    C[0:half] = cos
    C[half:2 * half] = cos
    S[0:half] = -sin
    S[half:2 * half] = sin
    return np.tile(C, (reps, 1)), np.tile(S, (reps, 1))


def host_consts():
    c64, s64 = _rope_tables(64, 2)
    c32, s32 = _rope_tables(32, 4)
    ident = np.eye(128, dtype=np.float32)
    causal = np.where(np.arange(128)[None, :] <= np.arange(128)[:, None], 0.0, -BIG).astype(np.float32)
    chix = np.ascontiguousarray(np.repeat(np.arange(256, dtype=np.float32)[None, :], 128, axis=0))
    pm64 = np.zeros((128, 128), np.float32)
    pm64[_swap_perm(128, 64, 8), np.arange(128)] = 1.0
    pm32 = np.zeros((128, 128), np.float32)
    pm32[_swap_perm(128, 32, 4), np.arange(128)] = 1.0
    return dict(c64=c64, s64=s64, c32=c32, s32=s32, ident=ident, causal=causal, chix=chix, pm64=pm64, pm32=pm32)


def host_weights(w_in, w_mem_kv, lam_re, lam_im, log_dt, b_re, b_im, c_re, c_im, d_skip, w_glu, b_glu, w_out, ln_g, ln_b):
    sp = np.concatenate([[0], np.cumsum(SPLITS)])
    col = lambda i: w_in[:, sp[i]:sp[i + 1]]
    q_c, k_c, v_c, qi_c, ki_c, wi_c, gatt_c, u_c, gssm_c, qm_c, gmem_c = [col(i) for i in range(11)]
    p64_512 = _swap_perm(512, 64, 8)
    p64_128 = _swap_perm(128, 64, 8)
    p32_256 = _swap_perm(256, 32, 4)
    p32_32 = _swap_perm(32, 32, 4)
    tiles = []
    for t in range(2):
        tiles += [qi_c[:, 128 * t:128 * t + 128]]
    for v in range(4):
        a = np.zeros((1024, 128), np.float32)
        a[:, 32 * v:32 * v + 32] = ki_c
        tiles += [a]
    for t in range(4):
        tiles += [q_c[:, 128 * t:128 * t + 128]]
    for g in range(2):
        tiles += [np.concatenate([k_c[:, 64 * g:64 * g + 64]] * 2, axis=1)]
    tiles += [u_c[:, 0:128], u_c[:, 128:256], qm_c[:, 0:128], qm_c[:, 128:256]]
    gates = np.concatenate([gatt_c, gssm_c, gmem_c], axis=1)
    for t in range(8):
        tiles.append(gates[:, 128 * t:128 * t + 128])
    assert len(tiles) == NFM
    wfm = np.ascontiguousarray(np.stack(tiles, 0)).astype(np.float32)
    wtm = np.ascontiguousarray(np.concatenate([v_c, wi_c], axis=1)).astype(np.float32)
    bre = np.zeros((8, 128, 128), np.float32)
    bim = np.zeros((8, 128, 128), np.float32)
    cre = np.zeros((8, 128, 128), np.float32)
    cim = np.zeros((8, 128, 128), np.float32)
    for i in range(8):
        for gl in range(2):
            g = 2 * i + gl
            g8 = g % 8
            bre[i, 16 * g8:16 * g8 + 16, 64 * gl:64 * gl + 64] = b_re[g].T
            bim[i, 16 * g8:16 * g8 + 16, 64 * gl:64 * gl + 64] = b_im[g].T
            cre[i, 64 * gl:64 * gl + 64, 16 * g8:16 * g8 + 16] = c_re[g].T
            cim[i, 64 * gl:64 * gl + 64, 16 * g8:16 * g8 + 16] = c_im[g].T
    dblk = np.zeros((2, 128, 128), np.float32)
    dflat = d_skip.reshape(256)
    for ct in range(2):
        dblk[ct][np.arange(128), np.arange(128)] = dflat[128 * ct:128 * ct + 128]

    def st(a):
        return np.ascontiguousarray(a.reshape(8, 2, 64).transpose(1, 2, 0).reshape(128, 8)).astype(np.float32)
    lamre = st(lam_re)
    lamim = st(lam_im)
    logdt = st(np.repeat(log_dt[:, None], 64, axis=1))
    bglu = np.ascontiguousarray(b_glu.reshape(2, 128).T).astype(np.float32)
    lng = np.ascontiguousarray(np.repeat(ln_g[None, :], 128, axis=0)).astype(np.float32)
    lnb = np.ascontiguousarray(np.repeat(ln_b[None, :], 128, axis=0)).astype(np.float32)
    bre = np.ascontiguousarray(bre.transpose(0, 2, 1))
    bim = np.ascontiguousarray(bim.transpose(0, 2, 1))
    return dict(wfm=wfm, wtm=wtm, wmem=np.ascontiguousarray(w_mem_kv), wglu=np.ascontiguousarray(w_glu),
                wout=np.ascontiguousarray(w_out), bre=bre, bim=bim, cre=cre, cim=cim, dblk=dblk,
                lamre=lamre, lamim=lamim, logdt=logdt, bglu=bglu, lng=lng, lnb=lnb)


def build(stop=99, debug=False):
    nc = bass.Bass("TRN2", target_bir_lowering=False)

    def din(name, shape):
        return nc.dram_tensor(name, list(shape), F32, kind="ExternalInput").ap()
    xT = din("xT", [BPC, 1024, L])
    xtm = din("x", [BPC, L, 1024])
    memT = din("memT", [BPC, 1024, 256])
    wfm = din("wfm", [NFM, 1024, 128])
    wtm = din("wtm", [1024, 136])
    wmem = din("wmem", [1024, 512])
    wglu = din("wglu", [256, 256])
    wout = din("wout", [1024, 1024])
    d_c64, d_s64, d_c32, d_s32 = [din(n, [128, L]) for n in ("c64", "s64", "c32", "s32")]
    d_bre, d_bim, d_cre, d_cim = [din(n, [8, 128, 128]) for n in ("bre", "bim", "cre", "cim")]
    d_dblk = din("dblk", [2, 128, 128])
    d_lamre, d_lamim, d_logdt = [din(n, [128, 8]) for n in ("lamre", "lamim", "logdt")]
    d_bglu = din("bglu", [128, 2])
    d_lng = din("lng", [128, 1024])
    d_lnb = din("lnb", [128, 1024])
    d_ident = din("ident", [128, 128])
    d_causal = din("causal", [128, 128])
    d_chix = din("chix", [128, 256])
    d_pm64 = din("pm64", [128, 128])
    d_pm32 = din("pm32", [128, 128])
    out = nc.dram_tensor("out", [BPC, L, 1024], F32, kind="ExternalOutput").ap()
    gsc = nc.dram_tensor("gsc", [BPC, 8, 128, L], BF16).ap()
    ssc = nc.dram_tensor("ssc", [BPC, 2, 128, L], BF16).ap()

    P = Prog(nc)
    k = K(P)
    dbg = {}

    def dump(name, src, shape, dtype=F32):
        if not debug:
            return
        t = nc.dram_tensor("dbg_" + name, list(shape), dtype, kind="ExternalOutput").ap()
        P.dma(t, src, P.newsem(), is_output=True)
        dbg[name] = (shape, dtype)

    identf = P.sb([128], F32)
    identb = P.sb([128], BF16)
    ident4 = P.sb([4, 128], BF16)
    causal = P.sb([128], F32)
    pmb = P.sb([2, 128], BF16)
    wglub = P.sb([2, 256], BF16)
    bglu = P.sb([2], F32)
    wtmb = P.sb([8, 136], BF16)
    sc8 = {n: P.sb([8], F32) for n in ("lre", "lim", "ldt", "dt", "a", "th", "rho", "t1", "t2", "sin", "cos", "lbr", "lbi",
                                       "nre", "nim", "den", "kre", "kim", "nkim", "u1", "u2", "r8")}
    sc8i = P.sb([8], I32)
    mure = P.sb([11, 8], F32)
    muim = P.sb([11, 8], F32)
    nmuim = P.sb([11, 8], F32)
    uT = P.sb([2, L], BF16)
    PB0 = P.sb_off
    qT = P.sb([4, L], BF16)
    kT = P.sb([2, L], BF16)
    qiT = P.sb([2, L], BF16)
    kiT4 = P.sb([4, L], BF16)
    vaug = P.sb([NB, 2, 65], BF16)
    absw = P.sb([NB, 8], F32)
    sgnw = P.sb([NB, 8], F32)
    qmT = P.sb([2, L], BF16)
    mkT = P.sb([2, 256], BF16)
    mvaug = P.sb([2, 4, 65], BF16)
    woutb = P.sb([8, 1024], BF16)
    lng = P.sb([1024], F32)
    lnb = P.sb([1024], F32)
    PB1 = P.sb_off
    XC = P.sb([8, L], BF16)
    small = P.sb([64], F32)
    ARENA0 = P.sb_off
    ARENA_SZ = SB_BYTES - ARENA0
    print("resident bytes", ARENA0, "arena", ARENA_SZ)

    class Arena:
        def __init__(self):
            self.off = ARENA0

        def sb(self, shape, dtype):
            n = int(np.prod(shape)) * ESIZE[dtype]
            n = (n + SLOT - 1) // SLOT * SLOT
            o = self.off
            self.off += n
            assert self.off <= SB_BYTES, "phase arena overflow %d" % (self.off - ARENA0)
            return P.sb_at(o, shape, dtype)

    A = Arena()
    stg = A.sb([8, 1024], F32)
    P.dma(identf, d_ident, P.newsem())
    P.dma(causal, d_causal, P.newsem())
    P.dma(bglu, d_bglu, P.newsem())
    for n, d in (("lre", d_lamre), ("lim", d_lamim), ("ldt", d_logdt)):
        P.dma(sc8[n], d, P.newsem())
    k.cp("dve", identb, identf)
    for q_, dsrc in enumerate((d_pm32, d_pm64)):
        P.dma(stg[:, q_, 0:128], dsrc, P.newsem())
        k.cp("dve", pmb[:, q_, :], stg[:, q_, 0:128])
    for r in range(4):
        k.cp("dve", ident4[:, r, :], identf)
    s_stg = P.newsem()
    v = stg[:, 0, 0:512].rearrange("p (i c) -> p i c", i=2)
    P.dma(v, wglu.rearrange("(k p) c -> p k c", p=128), s_stg)
    k.cp("dve", wglub, v)
    v = stg[:, 0:2, :].rearrange("p a b -> p (a b)")[:, 0:8 * 136].rearrange("p (i c) -> p i c", i=8)
    P.dma(v, wtm.rearrange("(k p) c -> p k c", p=128), s_stg)
    k.cp("dve", wtmb, v)

    s = sc8
    k.act(s["dt"], s["ldt"], AF.Exp)
    k.tt("dve", s["a"], s["lre"], s["dt"], ALU.mult)
    k.tt("dve", s["th"], s["lim"], s["dt"], ALU.mult)
    k.act(s["rho"], s["a"], AF.Exp)

    def sin_of(dst, src, shift):
        k.ts("dve", s["t1"], src, shift, ALU.add, 1.0 / (2 * PI), ALU.mult)
        k.cp("dve", sc8i, s["t1"])
        k.cp("dve", s["t2"], sc8i)
        k.ts("dve", s["t1"], src, shift, ALU.add)
        k.stt(s["t1"], s["t2"], -2 * PI, s["t1"], ALU.mult, ALU.add)
        k.ts("dve", s["t1"], s["t1"], 3.1415925, ALU.min, -3.1415925, ALU.max)
        k.act(dst, s["t1"], AF.Sin)
    sin_of(s["sin"], s["th"], 0.0)
    sin_of(s["cos"], s["th"], PI / 2)
    k.tt("dve", s["lbr"], s["rho"], s["cos"], ALU.mult)
    k.tt("dve", s["lbi"], s["rho"], s["sin"], ALU.mult)
    k.ts("dve", s["t1"], s["lbr"], -1.0, ALU.add)
    k.tt("dve", s["nre"], s["t1"], s["lre"], ALU.mult)
    k.tt("dve", s["u1"], s["lbi"], s["lim"], ALU.mult)
    k.tt("dve", s["nre"], s["nre"], s["u1"], ALU.add)
    k.tt("dve", s["nim"], s["lbi"], s["lre"], ALU.mult)
    k.tt("dve", s["u1"], s["t1"], s["lim"], ALU.mult)
    k.tt("dve", s["nim"], s["nim"], s["u1"], ALU.subtract)
    k.tt("dve", s["den"], s["lre"], s["lre"], ALU.mult)
    k.tt("dve", s["u1"], s["lim"], s["lim"], ALU.mult)
    k.tt("dve", s["den"], s["den"], s["u1"], ALU.add)
    P.op("dve", lambda e: e.reciprocal(out=s["u2"], in_=s["den"]), [s["den"]], [s["u2"]])
    k.tt("dve", s["kre"], s["nre"], s["u2"], ALU.mult)
    k.tt("dve", s["kim"], s["nim"], s["u2"], ALU.mult)
    k.ts("dve", s["nkim"], s["kim"], -1.0, ALU.mult)
    k.cp("dve", mure[:, 0, :], s["lbr"])
    k.cp("dve", muim[:, 0, :], s["lbi"])
    for lv in range(1, 11):
        k.tt("dve", s["u1"], mure[:, lv - 1, :], mure[:, lv - 1, :], ALU.mult)
        k.tt("dve", s["u2"], muim[:, lv - 1, :], muim[:, lv - 1, :], ALU.mult)
        k.tt("dve", mure[:, lv, :], s["u1"], s["u2"], ALU.subtract)
        k.tt("dve", s["u1"], mure[:, lv - 1, :], muim[:, lv - 1, :], ALU.mult)
        k.ts("dve", muim[:, lv, :], s["u1"], 2.0, ALU.mult)
    k.ts("dve", nmuim, muim, -1.0, ALU.mult)
    if debug:
        dump("mure", mure, [128, 11, 8])
        dump("muim", muim, [128, 11, 8])
        dump("kre", s["kre"], [128, 8])
        dump("kim", s["kim"], [128, 8])

    s_x = P.newsem()
    s_x2 = P.newsem()
    s_tab = [P.newsem() for _ in range(2)]
    s_w = [P.newsem() for _ in range(2)]
    s_g = [P.newsem() for _ in range(2)]
    s_misc = P.newsem()
    s_misc2 = P.newsem()
    s_out = [P.newsem() for _ in range(2)]
    s_xt = [P.newsem() for _ in range(2)]
    s_gl = [P.newsem() for _ in range(2)]
    psrot = [0]

    def nextbank(lo=0, hi=8):
        b = lo + psrot[0] % (hi - lo)
        psrot[0] += 1
        return b


    T8 = 8
    NCH = L // T8
    s_ss = [P.newsem() for _ in range(2)]
    s_ssl = P.newsem()
    s_bt = [P.newsem() for _ in range(5)]
    s_xs = P.newsem()
    s_ws = [P.newsem() for _ in range(2)]
    W1 = P.sb_at(PB0, [8, T8, 2, 128], BF16)
    W2 = P.sb_at(PB0 + 32768, [8, T8, 2, 128], BF16)
    Kt = P.sb_at(PB0 + 65536, [2, T8, 128], BF16)
    assert PB0 + 65536 + 4096 <= PB1
    A = Arena()
    BTr = A.sb([8, 128], F32)
    BTi = A.sb([8, 128], F32)
    CTr = A.sb([8, 128], F32)
    CTi = A.sb([8, 128], F32)
    nCTi = A.sb([8, 128], F32)
    dbf = A.sb([2, 128], F32)
    bts = [[A.sb([128], F32) for _ in range(2)] for _ in range(2)]
    w2t = [A.sb([128], F32) for _ in range(2)]
    LKr, LKi, nLKr, nLKi = [A.sb([T8 + 1, 8], F32) for _ in range(4)]
    GKr, GKi, nGKi = [A.sb([T8, 8], F32) for _ in range(3)]
    for dsrc, dst, sm in ((d_bre, BTr, 0), (d_bim, BTi, 1), (d_cre, CTr, 2), (d_cim, CTi, 3)):
        P.dma(dst, dsrc.rearrange("i p c -> p i c"), s_bt[sm])
    P.dma(dbf, d_dblk.rearrange("i p c -> p i c"), s_bt[4])
    k.ts("pool", nCTi, CTi, -1.0, ALU.mult, 0.0, ALU.add)
    k.cp("dve", LKr[:, 1, :], s["lbr"])
    k.cp("dve", LKi[:, 1, :], s["lbi"])
    for kk in range(2, T8 + 1):
        k.tt("dve", s["u1"], LKr[:, kk - 1, :], s["lbr"], ALU.mult)
        k.tt("dve", s["u2"], LKi[:, kk - 1, :], s["lbi"], ALU.mult)
        k.tt("dve", LKr[:, kk, :], s["u1"], s["u2"], ALU.subtract)
        k.tt("dve", s["u1"], LKr[:, kk - 1, :], s["lbi"], ALU.mult)
        k.tt("dve", s["u2"], LKi[:, kk - 1, :], s["lbr"], ALU.mult)
        k.tt("dve", LKi[:, kk, :], s["u1"], s["u2"], ALU.add)
    k.ts("dve", nLKr[:, 1:, :], LKr[:, 1:, :], -1.0, ALU.mult)
    k.ts("dve", nLKi[:, 1:, :], LKi[:, 1:, :], -1.0, ALU.mult)
    k.cp("dve", GKr[:, 0, :], s["kre"])
    k.cp("dve", GKi[:, 0, :], s["kim"])
    for kk in range(1, T8):
        k.tt("dve", s["u1"], LKr[:, kk, :], s["kre"], ALU.mult)
        k.tt("dve", s["u2"], LKi[:, kk, :], s["kim"], ALU.mult)
        k.tt("dve", GKr[:, kk, :], s["u1"], s["u2"], ALU.subtract)
        k.tt("dve", s["u1"], LKr[:, kk, :], s["kim"], ALU.mult)
        k.tt("dve", s["u2"], LKi[:, kk, :], s["kre"], ALU.mult)
        k.tt("dve", GKi[:, kk, :], s["u1"], s["u2"], ALU.add)
    k.ts("dve", nGKi, GKi, -1.0, ALU.mult)
    COST = P.sb_at(PB0 + 65536 + 4096, [8, NCH], F32)
    SINT = P.sb_at(PB0 + 65536 + 4096 + 8192, [8, NCH], F32)
    assert PB0 + 65536 + 4096 + 16384 <= PB1
    r8 = s["r8"]
    ff = A.sb([8], F32)
    ffi = A.sb([8], I32)
    chix = A.sb([NCH], F32)
    Gt = A.sb([8, NCH], F32)
    Gi = A.sb([8, NCH], I32)
    Gf = A.sb([8, NCH], F32)
    P.dma(chix, d_chix, P.newsem())
    k.act(r8, s["a"], AF.Exp, scale=float(T8))
    k.ts("dve", ff, s["th"], float(T8) / (2 * PI), ALU.mult)
    k.cp("dve", ffi, ff)
    k.cp("dve", s["u1"], ffi)
    k.tt("dve", ff, ff, s["u1"], ALU.subtract)
    k.tt("dve", Gt, ff.unsqueeze(2).broadcast_to([128, 8, NCH]), chix.unsqueeze(1).broadcast_to([128, 8, NCH]), ALU.mult)
    for shift, dstT in ((0.0, SINT), (0.25, COST)):
        src = Gt
        if shift != 0.0:
            k.ts("dve", Gf, Gt, shift, ALU.add)
            src = Gf
        k.cp("dve", Gi, src)
        k.cp("pool", dstT, Gi)
        k.tt("dve", dstT, src, dstT, ALU.subtract)
        k.ts("dve", dstT, dstT, 2 * PI, ALU.mult, 3.1415925, ALU.min)
        k.ts("dve", dstT, dstT, -3.1415925, ALU.max)
        k.act(dstT, dstT, AF.Sin)
    nb_ = 0
    for kk in range(T8):
        for ct in range(2):
            kp = P.ps(ct, (128,))
            for i in range(4 * ct, 4 * ct + 4):
                br_, bi_ = bts[nb_ % 2]
                nb_ += 1
                gr = GKr[:, kk, i:i + 1]
                gi = GKi[:, kk, i:i + 1]
                ngi = nGKi[:, kk, i:i + 1]
                k.ts("dve", br_, BTr[:, i, :], gr, ALU.mult)
                k.stt(br_, BTi[:, i, :], ngi, br_, ALU.mult, ALU.add)
                k.ts("dve", bi_, BTr[:, i, :], gi, ALU.mult)
                k.stt(bi_, BTi[:, i, :], gr, bi_, ALU.mult, ALU.add)
                for x_, src in ((0, br_), (1, bi_)):
                    pt = P.ps(4 + (2 * nb_ + x_) % 4, (128,))
                    k.tr(pt, src, identf)
                    k.cp("act", W1[:, i, T8 - 1 - kk, x_, :], pt)
                k.mm(kp, br_, CTr[:, i, :], i == 4 * ct, False)
                k.mm(kp, bi_, nCTi[:, i, :], False, i == 4 * ct + 3)
            if kk == 0:
                k.tt("dve", Kt[:, ct, kk, :], kp, dbf[:, ct, :], ALU.add)
            else:
                k.cp("act", Kt[:, ct, kk, :], kp)
    for i in range(8):
        for j in range(T8):
            lr = LKr[:, j + 1, i:i + 1]
            nli = nLKi[:, j + 1, i:i + 1]
            nlr = nLKr[:, j + 1, i:i + 1]
            t_ = w2t[(i * T8 + j) % 2]
            k.ts("pool", t_, CTr[:, i, :], lr, ALU.mult, 0.0, ALU.add)
            k.stt(W2[:, i, j, 0, :], CTi[:, i, :], nli, t_, ALU.mult, ALU.add)
            t2_ = w2t[(i * T8 + j + 1) % 2]
            k.act(t2_, CTr[:, i, :], AF.Copy, scale=nli)
            k.stt(W2[:, i, j, 1, :], CTi[:, i, :], nlr, t2_, ALU.mult, ALU.add)

    import os
    PHS = int(os.environ.get("PHS_VARIANT", "99"))
    A = Arena()
    xstS = A.sb([L], F32)
    wstS = [A.sb([8, 128], F32) for _ in range(2)]
    wbfS = [A.sb([8, 128], BF16) for _ in range(2)]
    Lr = A.sb([8, NCH], F32)
    Li = A.sb([8, NCH], F32)
    Br = A.sb([8, NCH], F32)
    Bi = A.sb([8, NCH], F32)
    Spr = A.sb([8, NCH], BF16)
    Spi = A.sb([8, NCH], BF16)
    uTd = A.sb([2, T8, NCH], BF16)
    ygb = P.sb_at(ARENA0, [2, L], BF16)
    osb = [P.sb_at(ARENA0 + 8192 + 4096 * q_, [L], BF16) for q_ in range(2)]
    assert 8192 + 8192 <= 8192 + 2 * 4096 + 2 * 2048
    k.memset("pool", Spr[:, :, 0:1], 0.0)
    k.memset("pool", Spi[:, :, 0:1], 0.0)
    def xload_S(bb):
        for kk in range(8):
            P.dma(xstS, xT[bb, 128 * kk:128 * kk + 128, :], s_xs)
            k.cp("dve" if kk % 2 == 0 else "act", XC[:, kk, :], xstS)
    xload_S(0)
    for b in range(BPC if PHS > 10 else 0):
        for ct in range(2):
            P.dma(wstS[ct], wfm[12 + ct].rearrange("(k p) c -> p k c", p=128), s_ws[ct])
            k.cp("act", wbfS[ct], wstS[ct])
            for c in range(4):
                ps = P.ps(nextbank(0, 4))
                for kk in range(8):
                    k.mm(ps, wbfS[ct][:, kk, :], XC[:, kk, 512 * c:512 * c + 512], kk == 0, kk == 7)
                k.cp("act", uT[:, ct, 512 * c:512 * c + 512], ps)
        xload_S(b + 1 if b + 1 < BPC else 0)
        for ct in range(2):
            k.cp("pool", uTd[:, ct, :, :], uT[:, ct, :].rearrange("p (c j) -> p j c", j=T8))
        for i in range(8):
            ct = i // 4
            for x_, dstL in ((0, Lr), (1, Li)):
                q_ = 2 * i + x_
                psL = P.ps(4 + q_ % 4, (NCH,))
                for j in range(T8):
                    k.mm(psL, W1[:, i, j, x_, :], uTd[:, ct, j, :], j == 0, j == T8 - 1, skip=True)
                k.cp("act", dstL[:, i, :], psL)
        T1, T2 = Br, Bi
        k.tt("dve", T1, COST, Lr, ALU.mult)
        k.tt("dve", T2, SINT, Li, ALU.mult)
        k.tt("dve", T1, T1, T2, ALU.add)
        k.tt("dve", T2, SINT, Lr, ALU.mult)
        k.tt("dve", Li, COST, Li, ALU.mult)
        k.tt("dve", Li, Li, T2, ALU.subtract)
        for i in range(8):
            d0 = r8[:, i:i + 1].broadcast_to([128, NCH])
            P.op("dve", lambda e, o_=Lr[:, i, :], d0=d0, d1=T1[:, i, :]: e.tensor_tensor_scan(out=o_, data0=d0, data1=d1, initial=0.0, op0=ALU.mult, op1=ALU.add),
                 [T1[:, i, :], r8], [Lr[:, i, :]])
            P.op("dve", lambda e, o_=T2[:, i, :], d0=d0, d1=Li[:, i, :]: e.tensor_tensor_scan(out=o_, data0=d0, data1=d1, initial=0.0, op0=ALU.mult, op1=ALU.add),
                 [Li[:, i, :], r8], [T2[:, i, :]])
        k.tt("dve", T1, COST, Lr, ALU.mult)
        k.tt("dve", Li, SINT, T2, ALU.mult)
        k.tt("dve", T1, T1, Li, ALU.subtract)
        k.tt("dve", Li, SINT, Lr, ALU.mult)
        k.tt("dve", T2, COST, T2, ALU.mult)
        k.tt("dve", Li, Li, T2, ALU.add)
        for i in range(8):
            k.cp("act", Spr[:, i, 1:NCH], T1[:, i, 0:NCH - 1])
            k.cp("act", Spi[:, i, 1:NCH], Li[:, i, 0:NCH - 1])
        if PHS <= 20:
            continue
        y = P.sb_at(ARENA0 + 20480, [L], F32)
        t1 = P.sb_at(ARENA0 + 20480 + 8192, [L], F32)
        t2 = P.sb_at(ARENA0 + 20480 + 16384, [L], F32)
        sg = P.sb_at(ARENA0 + 20480 + 24576, [L], F32)
        for ct in range(2):
            for c4 in range(4):
                yp = P.ps(c4, (64, T8))
                uv = uT[:, ct, 512 * c4:512 * c4 + 512].rearrange("p (c j) -> p c j", j=T8)
                for tau in range(T8):
                    k.mm(yp[:, :, tau:T8], Kt[:, ct, tau, :], uv[:, :, 0:T8 - tau], tau == 0, False, skip=True)
                for i in range(4 * ct, 4 * ct + 4):
                    for j in range(T8):
                        k.mm(yp[:, :, j], W2[:, i, j, 0, :], Spr[:, i, 64 * c4:64 * c4 + 64], False, False, skip=True)
                        k.mm(yp[:, :, j], W2[:, i, j, 1, :], Spi[:, i, 64 * c4:64 * c4 + 64], False,
                             (i == 4 * ct + 3 and j == T8 - 1), skip=True)
            for c in range(4):
                sl = slice(512 * c, 512 * c + 512)
                k.cp("act", y[:, sl], P.ps(c))
            if debug and b == 0:
                dump("y%d" % ct, y, [128, L])
            k.tt("dve", t1, y, y, ALU.mult)
            k.ts("dve", t1, t1, 0.044715, ALU.mult, 1.0, ALU.add)
            k.tt("dve", t1, t1, y, ALU.mult)
            k.act(t2, t1, AF.Sigmoid, scale=2.0 * 0.7978845608028654)
            k.tt("dve", ygb[:, ct, :], y, t2, ALU.mult)
        for et in range(2):
            for c in range(4):
                sl = slice(512 * c, 512 * c + 512)
                ps = P.ps(nextbank(4, 8))
                for kc in range(2):
                    k.mm(ps, wglub[:, kc, 128 * et:128 * et + 128], ygb[:, kc, sl], kc == 0, kc == 1)
                k.act(sg[:, sl], ps, AF.Sigmoid, bias=bglu[:, et:et + 1])
                k.tt("dve", osb[et][:, sl], ygb[:, et, sl], sg[:, sl], ALU.mult)
            P.dma(ssc[b, et], osb[et], s_ss[et])
    A = Arena()
    stg = A.sb([8, 1024], F32)
    P.dma(lng, d_lng, P.newsem())
    P.dma(lnb, d_lnb, P.newsem())
    P.dma(stg, wout.rearrange("(k p) c -> p k c", p=128), s_stg)
    for kk in range(8):
        k.cp("dve" if kk % 2 == 0 else "act", woutb[:, kk, :], stg[:, kk, :])

    for b in range(BPC):
        A = Arena()
        xst = A.sb([L], F32)
        xst2 = A.sb([L], F32)
        wst = [A.sb([8, 128], F32) for _ in range(2)]
        wbf = [A.sb([8, 128], BF16) for _ in range(2)]
        tabC = A.sb([L], F32)
        tabS = A.sb([L], F32)
        tmpA = [A.sb([512], F32) for _ in range(3)]
        zbf = [A.sb([512], BF16) for _ in range(3)]
        tmp2 = [A.sb([512], F32) for _ in range(2)]
        gst = [A.sb([L], BF16) for _ in range(1)]
        for kk in range(8 if b > 0 else 0):
            xs_ = xst if kk % 2 == 0 else xst2
            P.dma(xs_, xT[b, 128 * kk:128 * kk + 128, :], s_x if kk % 2 == 0 else s_x2)
            k.cp("dve" if kk % 2 == 0 else "act", XC[:, kk, :], xs_)

        widx = {}

        def load_w(m):
            widx[m] = len(widx) % 2
            P.dma(wst[widx[m]], wfm[m].rearrange("(k p) c -> p k c", p=128), s_w[widx[m]])
            k.cp("act", wbf[widx[m]], wst[widx[m]])

        def proj(m, c):
            bank = nextbank()
            ps = P.ps(bank)
            for kk in range(8):
                k.mm(ps, wbf[widx[m]][:, kk, :], XC[:, kk, 512 * c:512 * c + 512], kk == 0, kk == 7)
            return ps
        dests = {}
        for t in range(2):
            dests[t] = ("rope", qiT[:, t, :], 0)
        for v_ in range(4):
            dests[2 + v_] = ("rope", kiT4[:, v_, :], 0)
        for t in range(4):
            dests[6 + t] = ("rope", qT[:, t, :], 1)
        for g in range(2):
            dests[10 + g] = ("rope", kT[:, g, :], 1)
        dests[14] = ("plain", qmT[:, 0, :], None)
        dests[15] = ("plain", qmT[:, 1, :], None)
        for t in range(8):
            dests[16 + t] = ("gate", t, None)
        order = [m for m in range(NFM) if m not in (12, 13)]
        nta = 0
        pend = [None]
        for oi, m in enumerate(order):
            if m == 0:
                P.dma(tabC, d_c32, s_tab[0])
                P.dma(tabS, d_s32, s_tab[1])
                load_w(0)
            if m == 6:
                if pend[0] is not None:
                    pend[0]()
                    pend[0] = None
                P.dma(tabC, d_c64, s_tab[0])
                P.dma(tabS, d_s64, s_tab[1])
            if oi + 1 < len(order):
                load_w(order[oi + 1])
            kind, dst, pq = dests[m]
            if kind == "rope":
                for c in range(4):
                    sl = slice(512 * c, 512 * c + 512)
                    ps = proj(m, c)
                    if pend[0] is not None:
                        pend[0]()
                        pend[0] = None
                    ta = tmpA[nta % 3][:, 0:512]
                    zb = zbf[nta % 3]
                    nta += 1
                    k.cp("act", zb, ps)
                    k.tt("dve", ta, ps, tabC[:, sl], ALU.mult)

                    def fin(zb=zb, ta=ta, sl=sl, dst=dst, pq=pq, c=c):
                        ps2 = P.ps(nextbank())
                        k.mm(ps2, pmb[:, pq, :], zb, True, True)
                        t2 = tmp2[c % 2]
                        k.tt("dve", t2, ps2, tabS[:, sl], ALU.mult)
                        k.tt("pool", dst[:, sl], t2, ta, ALU.add)
                    pend[0] = fin
            elif kind == "plain":
                for c in range(4):
                    ps = proj(m, c)
                    if pend[0] is not None:
                        pend[0]()
                        pend[0] = None
                    k.cp("act", dst[:, 512 * c:512 * c + 512], ps)
            else:
                gt = gst[0]
                for c in range(4):
                    ps = proj(m, c)
                    k.act(gt[:, 512 * c:512 * c + 512], ps, AF.Silu)
                P.dma(gsc[b, dst], gt, s_g[0], eng="act")
        for i in range(NB):
            bank = nextbank()
            ps = P.ps(bank, (136,))
            for kk in range(8):
                k.mm(ps, XC[:, kk, 128 * i:128 * i + 128], wtmb[:, kk, :], kk == 0, kk == 7)
            k.cp("act", vaug[:, i, :, 0:64], ps[:, 0:128].rearrange("p (g d) -> p g d", g=2))
            k.act(sgnw[:, i, :], ps[:, 128:136], AF.Sign)
            k.tt("dve", absw[:, i, :], ps[:, 128:136], sgnw[:, i, :], ALU.mult)
            k.ts("dve", absw[:, i, :], absw[:, i, :], 1.0 / 16.0, ALU.mult)
        if b == 0:
            k.memset("pool", vaug[:, :, :, 64:65], 1.0)
            k.memset("pool", mvaug[:, :, :, 64:65], 1.0)
        if debug and b == 0:
            dump("qT", qT, [128, 4, L], BF16)
            dump("kT", kT, [128, 2, L], BF16)
            dump("qiT", qiT, [128, 2, L], BF16)
            dump("kiT4", kiT4, [128, 4, L], BF16)
            dump("vaug", vaug, [128, NB, 2, 65], BF16)
            dump("absw", absw, [128, NB, 8])
            dump("sgnw", sgnw, [128, NB, 8])
            dump("qmT", qmT, [128, 2, L], BF16)
        if stop <= 1:
            continue

        A = Arena()
        mst = A.sb([8, 256], F32)
        mbf = A.sb([8, 256], BF16)
        wms = A.sb([8, 512], F32)
        wmb = A.sb([8, 512], BF16)
        P.dma(mst, memT[b].rearrange("(k p) c -> p k c", p=128), s_misc)
        P.dma(wms, wmem.rearrange("(k p) c -> p k c", p=128), s_misc2)
        k.cp("dve", mbf, mst)
        k.cp("pool", wmb, wms)
        for t in range(2):
            ps = P.ps(nextbank(), (256,))
            for kk in range(8):
                k.mm(ps, wmb[:, kk, 128 * t:128 * t + 128], mbf[:, kk, :], kk == 0, kk == 7)
            k.cp("act", mkT[:, t, :], ps)
        for mb_ in range(2):
            ps = P.ps(nextbank(), (256,))
            for kk in range(8):
                k.mm(ps, mbf[:, kk, 128 * mb_:128 * mb_ + 128], wmb[:, kk, 256:512], kk == 0, kk == 7)
            k.cp("act", mvaug[:, mb_, :, 0:64], ps.rearrange("p (h d) -> p h d", h=4))
        if debug and b == 0:
            dump("mkT", mkT, [128, 2, 256], BF16)
            dump("mvaug", mvaug, [128, 2, 4, 65], BF16)

        A = Arena()
        em = [A.sb([2, 512], BF16) for _ in range(2)]
        otm = [A.sb([128], BF16) for _ in range(2)]
        rd = A.sb([16], F32)
        nrd = 0
        for hp in range(2):
            for c in range(4):
                e_h = []
                for hh in range(2):
                    h = 2 * hp + hh
                    e = em[hh]
                    for mb_ in range(2):
                        ps = P.ps(nextbank(0, 4))
                        k.mm(ps, mkT[64 * hh:64 * hh + 64, hp, 128 * mb_:128 * mb_ + 128],
                             qmT[64 * hh:64 * hh + 64, hp, 512 * c:512 * c + 512], True, True)
                        k.act(e[:, mb_, :], ps, AF.Exp, scale=0.125)
                    e_h.append(e)
                for tb in range(4):
                    i = 4 * c + tb
                    o = otm[i % 2]
                    po = P.ps(4 + (i % 2), (2, 65))
                    for hh in range(2):
                        h = 2 * hp + hh
                        for mb_ in range(2):
                            k.mm(po[:, hh, :], e_h[hh][:, mb_, 128 * tb:128 * tb + 128], mvaug[:, mb_, h, :],
                                 (hh == 0 and mb_ == 0), mb_ == 1, skip=True)
                    r_ = rd[:, 2 * (nrd % 8):2 * (nrd % 8) + 2]
                    nrd += 1
                    P.op("dve", lambda e, r_=r_, po=po: e.reciprocal(out=r_, in_=po[:, :, 64]), [po], [r_])
                    k.tt("dve", o.rearrange("p (h d) -> p h d", h=2), po[:, :, 0:64],
                         r_.unsqueeze(2).broadcast_to([128, 2, 64]), ALU.mult)
                    pt = P.ps(6 + (i % 2), (128,), BF16)
                    k.tr(pt, o, identb)
                    k.cp("act", XC[:, 6 + hp, 128 * i:128 * i + 128], pt)
        if stop <= 4:
            continue

        A = Arena()
        NSC = 3
        sc = [A.sb([L], F32) for _ in range(NSC)]
        rt = [A.sb([512], F32) for _ in range(4)]
        junk = {"dve": A.sb([L], BF16), "act": A.sb([L], BF16)}
        mbk = [A.sb([L], BF16) for _ in range(NSC)]
        eg = [A.sb([4, 128], BF16) for _ in range(4)]
        oat = [A.sb([512], BF16) for _ in range(2)]
        scal = A.sb([NB, 32], F32)
        rda = A.sb([16], F32)
        nrt = [0]
        neg = [0]
        W0 = 32.0
        NIT = NBIS - 2

        def g_scores(i):
            S = 128 * (i + 1)
            s_ = sc[i % NSC]
            for c in range((S + 511) // 512):
                wc = min(512, S - 512 * c)
                sl = slice(512 * c, 512 * c + wc)
                for h in range(8):
                    ps = P.ps(nextbank(0, 4))[:, 0:wc]
                    k.mm(ps, qiT[:, h // 4, 128 * i:128 * i + 128], kiT4[:, h % 4, sl], True, True)
                    r_ = rt[nrt[0] % 4][:, 0:wc]
                    nrt[0] += 1
                    k.act(r_, ps, AF.Relu, scale=absw[:, i, h:h + 1])
                    if h == 0:
                        k.ts("dve", s_[:, sl], r_, sgnw[:, i, h:h + 1], ALU.mult)
                    else:
                        k.stt(s_[:, sl], r_, sgnw[:, i, h:h + 1], s_[:, sl], ALU.mult, ALU.add)
                    yield
            k.tt("pool", s_[:, 128 * i:128 * i + 128], s_[:, 128 * i:128 * i + 128], causal, ALU.add)
            yield

        def g_bisect(i):
            S = 128 * (i + 1)
            s_ = sc[i % NSC][:, 0:S]
            m = scal[:, i, 0:1]
            nm = scal[:, i, 1:2]
            c_ = scal[:, i, 2:3]
            a_ = scal[:, i, 3:4]
            eng = "dve" if i in (3, 5, 7, 9, 11, 13, 15) else "act"
            if i < 2:
                k.memset("dve", m, -64.0)
                yield
            elif eng == "dve":
                k.memset("dve", m, 0.0)
                w = W0
                for it in range(NIT):
                    k.ts("dve", junk["dve"][:, 0:S], s_, m, ALU.is_ge, 0.0, ALU.add, accum_out=c_)
                    k.ts("dve", a_, c_, TOPK - 0.5, ALU.is_ge, w / 2, ALU.mult)
                    k.stt(m, a_, -w / 4, m, ALU.add, ALU.add)
                    w = w / 2
                    yield
                k.ts("dve", m, m, -w / 2, ALU.add)
            else:
                k.memset("dve", nm, 0.0)
                w = W0
                for it in range(NIT):
                    k.act(junk["act"][:, 0:S], s_, AF.Sign, bias=nm, accum_out=c_)
                    k.ts("dve", a_, c_, float(2 * TOPK - 1 - S), ALU.is_ge, -w / 2, ALU.mult)
                    k.stt(nm, a_, w / 4, nm, ALU.add, ALU.add)
                    w = w / 2
                    yield
                k.ts("dve", m, nm, -1.0, ALU.mult, -w / 2, ALU.add)
            k.ts("pool", mbk[i % NSC][:, 0:S], s_, m, ALU.is_lt, MASKNEG, ALU.mult)
            yield

        def g_attend(i):
            mb_ = mbk[i % NSC]
            po = [P.ps(6, (4, 65)), P.ps(7, (4, 65))]
            pend = [None]

            def av_step(j, par, e):
                for q_ in range(4):
                    h = par + 2 * q_
                    g = h // 4
                    k.mm(po[g][:, h % 4, :], e[:, q_, :], vaug[:, j, g, :], (j == 0 and par == 0 and h in (0, 4)), j == i, skip=True)
            for j in range(i + 1):
                for par in range(2):
                    lg = P.ps(4 + par, (4, 128))
                    k.mm(lg.rearrange("p r t -> p (r t)"), mb_[:, 128 * j:128 * j + 128], ident4.rearrange("p r t -> p (r t)"),
                         True, False, skip=True)
                    ro = 64 * par
                    for q_ in range(4):
                        h = par + 2 * q_
                        g = h // 4
                        k.mm(lg[:, q_, :], kT[ro:ro + 64, g, 128 * j:128 * j + 128], qT[ro:ro + 64, h // 2, 128 * i:128 * i + 128],
                             False, q_ == 3, skip=True)
                    e = eg[neg[0] % 4]
                    neg[0] += 1
                    k.act(e, lg, AF.Exp, scale=0.125)
                    if pend[0] is not None:
                        av_step(*pend[0])
                    pend[0] = (j, par, e)
                    yield
            av_step(*pend[0])
            yield

        def normalize(i):
            po = [P.ps(6, (4, 65)), P.ps(7, (4, 65))]
            o = oat[i % 2]
            for g in range(2):
                r_ = rda[:, 4 * ((2 * i + g) % 4):4 * ((2 * i + g) % 4) + 4]
                P.op("dve", lambda e, r_=r_, pg=po[g]: e.reciprocal(out=r_, in_=pg[:, :, 64]), [po[g]], [r_])
                k.tt("dve", o[:, 256 * g:256 * g + 256].rearrange("p (h d) -> p h d", h=4), po[g][:, :, 0:64],
                     r_.unsqueeze(2).broadcast_to([128, 4, 64]), ALU.mult)
            for t in range(4):
                pt = P.ps(nextbank(0, 4), (128,), BF16)
                k.tr(pt, o[:, 128 * t:128 * t + 128], identb)
                k.cp("act", XC[:, t, 128 * i:128 * i + 128], pt)

        def nsteps_bisect(i):
            return 1 if i < 2 else NIT + 1

        def run_interleaved(tasks):
            order = []
            for ti, (g_, n) in enumerate(tasks):
                for q_ in range(n):
                    order.append(((q_ + 0.5) / n, ti))
            order.sort()
            for _, ti in order:
                try:
                    next(tasks[ti][0])
                except StopIteration:
                    pass

        def drain(g_):
            for _ in g_:
                pass

        def nsteps_scores(i):
            return sum(8 for _ in range((128 * (i + 1) + 511) // 512)) + 1
        bis = {}
        for i in range(3):
            drain(g_scores(i))
        for n in (0, 1):
            bis[n] = g_bisect(n)
        drain(bis[0])
        half = lambda n: (nsteps_bisect(n) + 1) // 2
        run_interleaved([(bis[1], half(1))])
        for i in range(NB):
            tasks = []
            if i + 3 < NB:
                tasks.append((g_scores(i + 3), nsteps_scores(i + 3)))
            if i + 2 < NB:
                bis[i + 2] = g_bisect(i + 2)
                tasks.append((bis[i + 2], half(i + 2)))
            if i + 1 < NB:
                tasks.append((bis[i + 1], nsteps_bisect(i + 1) - half(i + 1) + 1))
            tasks.append((g_attend(i), 2 * (i + 1) + 1))
            run_interleaved(tasks)
            if i + 1 < NB:
                drain(bis[i + 1])
            normalize(i)
        if debug and b == 0:
            dump("catT", XC, [128, 8, L], BF16)
        if stop <= 5:
            continue

        A = Arena()
        gl = [A.sb([L], BF16) for _ in range(2)]
        xt = [A.sb([1024], F32) for _ in range(2)]
        hb = [A.sb([1024], F32) for _ in range(2)]
        ob = [A.sb([1024], F32) for _ in range(2)]
        st6 = A.sb([2, 32], F32)
        mv2 = A.sb([2, 32], F32)
        rs = A.sb([2, 32], F32)
        for et in range(2):
            P.dma(XC[:, 4 + et, :], ssc[b, et], s_ssl, after=[(s_ss[0], 16 * P.dma_cnt[s_ss[0]]), (s_ss[1], 16 * P.dma_cnt[s_ss[1]])])
        for kk in range(8):
            P.dma(gl[kk % 2], gsc[b, kk], s_gl[kk % 2], after=[(s_g[0], 16 * P.dma_cnt[s_g[0]])])
            k.tt("pool" if kk % 2 else "dve", XC[:, kk, :], XC[:, kk, :], gl[kk % 2], ALU.mult)
        P.dma(xt[0], xtm[b, 0:128, :], s_xt[0])
        for i in range(NB):
            if i + 1 < NB:
                P.dma(xt[(i + 1) % 2], xtm[b, 128 * (i + 1):128 * (i + 1) + 128, :], s_xt[(i + 1) % 2])
            h_ = hb[i % 2]
            for hf in range(2):
                ps = P.ps(nextbank(0, 8))
                for kk in range(8):
                    k.mm(ps, XC[:, kk, 128 * i:128 * i + 128], woutb[:, kk, 512 * hf:512 * hf + 512], kk == 0, kk == 7)
                k.stt(h_[:, 512 * hf:512 * hf + 512], xt[i % 2][:, 512 * hf:512 * hf + 512], DN_ALPHA, ps, ALU.mult, ALU.add)
                P.op("dve", lambda e, o_=st6[:, i % 2, 6 * hf:6 * hf + 6], a_=h_[:, 512 * hf:512 * hf + 512]: e.bn_stats(out=o_, in_=a_),
                     [h_[:, 512 * hf:512 * hf + 512]], [st6[:, i % 2, 6 * hf:6 * hf + 6]])
            mv = mv2[:, i % 2, 0:2]
            P.op("dve", lambda e, o_=mv, a_=st6[:, i % 2, 0:12]: e.bn_aggr(out=o_, in_=a_),
                 [st6[:, i % 2, 0:12]], [mv])
            r4 = rs[:, i % 2, 0:4]
            k.ts("dve", r4[:, 0:1], mv[:, 1:2], LN_EPS, ALU.add)
            k.act(r4[:, 1:2], r4[:, 0:1], AF.Sqrt)
            P.op("dve", lambda e, o_=r4[:, 2:3], a_=r4[:, 1:2]: e.reciprocal(out=o_, in_=a_), [r4[:, 1:2]], [r4[:, 2:3]])
            k.ts("dve", r4[:, 3:4], mv[:, 0:1], -1.0, ALU.mult, r4[:, 2:3], ALU.mult)
            o_ = ob[i % 2]
            k.act(o_, h_, AF.Identity, bias=r4[:, 3:4], scale=r4[:, 2:3])
            k.tt("dve", o_, o_, lng, ALU.mult)
            k.tt("pool", o_, o_, lnb, ALU.add)
            P.dma(out[b, 128 * i:128 * i + 128, :], o_, s_out[i % 2], is_output=True, eng="pool")

    P.emit()
    return nc, dbg


_CACHE = {}


def kernel(x, mem, w_in, w_mem_kv, lam_re, lam_im, log_dt, b_re, b_im, c_re, c_im, d_skip,
           w_glu, b_glu, w_out, ln_g, ln_b):
    x = np.asarray(x, np.float32)
    mem = np.asarray(mem, np.float32)
    f = lambda a: np.asarray(a, np.float32)
    hw = host_weights(f(w_in), f(w_mem_kv), f(lam_re), f(lam_im), f(log_dt), f(b_re), f(b_im), f(c_re), f(c_im),
                      f(d_skip), f(w_glu), f(b_glu), f(w_out), f(ln_g), f(ln_b))
    hc = host_consts()
    if "nc" not in _CACHE:
        _CACHE["nc"] = build()[0]
    nc = _CACHE["nc"]
    in_maps = []
    for c in range(8):
        xb = x[BPC * c:BPC * c + BPC]
        m = dict(hw)
        m.update(hc)
        m["x"] = np.ascontiguousarray(xb)
        m["xT"] = np.ascontiguousarray(xb.transpose(0, 2, 1))
        m["memT"] = np.ascontiguousarray(mem[BPC * c:BPC * c + BPC].transpose(0, 2, 1))
        in_maps.append(m)
    res = run_bass_kernel_spmd(nc, in_maps, core_ids=list(range(8)))
    return np.concatenate([r["out"] for r in res.results], axis=0).astype(np.float32)
```

```python
import contextlib
import math
import numpy as np
import concourse.bass as bass
import concourse.mybir as mybir
from concourse.bass_utils import run_bass_kernel_spmd

DT = mybir.dt
F32, BF16, I32 = DT.float32, DT.bfloat16, DT.int32
ALU = mybir.AluOpType
AF = mybir.ActivationFunctionType
ESIZE = {F32: 4, BF16: 2, I32: 4}

ENGS = ["pe", "act", "dve", "pool", "sp"]
SB_BYTES = 206 * 1024
PS_BYTES = 16 * 1024
SLOT = 128
NDMA = 60

L = 2048
NB = 16
BPC = 2
TOPK = 256
NBIS = 21
DN_ALPHA = 2.0 ** 0.25
LN_EPS = 1e-5
BIG = 1.0e30
MASKNEG = -30000.0
PI = math.pi


class Prog:
    def __init__(self, nc):
        self.nc = nc
        self.ops = {e: [] for e in ENGS}
        self.kinds = ENGS + ["d%d" % i for i in range(NDMA)]
        self.kidx = {k: i for i, k in enumerate(self.kinds)}
        nk = len(self.kinds)
        self.nslots = {"sb": SB_BYTES // SLOT, "ps": PS_BYTES // SLOT}
        self.wk = {s: np.full(n, -1, np.int64) for s, n in self.nslots.items()}
        self.wv = {s: np.zeros(n, np.int64) for s, n in self.nslots.items()}
        self.rv = {s: np.zeros((nk, n), np.int64) for s, n in self.nslots.items()}
        self.seen = {e: np.zeros(nk, np.int64) for e in ENGS}
        self.dma_cnt = [0] * NDMA
        self.arena = nc.alloc_sbuf_tensor("arena", [128, SB_BYTES // 4], F32)
        self.psum = nc.alloc_psum_tensor("psum_all", [128, PS_BYTES // 4], F32)
        self.sb_off = 0
        self.out_dma = []
        self.nsem = 0
        self._dummy = self.sb([8], F32)
        self._dummy_act = self.sb([8], F32)

    def newsem(self):
        self.nsem += 1
        assert self.nsem <= NDMA
        return self.nsem - 1

    def sb(self, shape, dtype):
        n = int(np.prod(shape)) * ESIZE[dtype]
        n = (n + SLOT - 1) // SLOT * SLOT
        off = self.sb_off
        self.sb_off += n
        assert self.sb_off <= SB_BYTES, "SBUF arena overflow %d" % self.sb_off
        return self.sb_at(off, shape, dtype)

    def sb_at(self, off, shape, dtype):
        assert off % 4 == 0
        nel = int(np.prod(shape))
        nb = nel * ESIZE[dtype]
        assert off + nb <= SB_BYTES, "SBUF arena overflow (at) %d" % (off + nb)
        v = self.arena[:, off // 4:(off + nb + 3) // 4]
        if dtype != F32:
            v = v.bitcast(dtype)[:, 0:nel]
        if len(shape) > 1:
            names = " ".join("a%d" % i for i in range(len(shape)))
            kw = {"a%d" % i: int(s) for i, s in enumerate(shape)}
            v = v.rearrange("p (%s) -> p %s" % (names, names), **kw)
        return v

    def ps(self, bank, shape=(512,), dtype=F32, off=0):
        nel = int(np.prod(shape))
        nb = nel * ESIZE[dtype]
        b0 = bank * 2048 + off
        assert off + nb <= 2048
        v = self.psum[:, b0 // 4:(b0 + nb + 3) // 4]
        if dtype != F32:
            v = v.bitcast(dtype)[:, 0:nel]
        if len(shape) > 1:
            names = " ".join("a%d" % i for i in range(len(shape)))
            kw = {"a%d" % i: int(s) for i, s in enumerate(shape)}
            v = v.rearrange("p (%s) -> p %s" % (names, names), **kw)
        return v

    def _range(self, ap):
        sp = str(ap.space).lower()
        if "sb" in sp or "state" in sp:
            space, pitch = "sb", SB_BYTES
        elif "psum" in sp:
            space, pitch = "ps", PS_BYTES
        else:
            return None
        es = ESIZE[ap.dtype]
        off = (ap.offset * es) % pitch
        ext = 1
        for st, cnt in list(ap.ap)[1:]:
            ext += (cnt - 1) * abs(st)
        lo = off // SLOT
        hi = (off + ext * es - 1) // SLOT + 1
        if space == "ps":
            per = 2048 // SLOT
            lo = lo // per * per
            hi = (hi + per - 1) // per * per
        return space, lo, hi

    def _deps(self, eng, reads, writes):
        need = {}
        myk = self.kidx[eng]
        myseq = len(self.ops[eng]) + 1

        def add(k, v):
            if k < 0 or v <= 0:
                return
            if k == myk:
                if eng == "pe" or v < myseq - 1:
                    return
            if need.get(k, 0) < v:
                need[k] = v
        for ap in reads:
            r = self._range(ap)
            if r is None:
                continue
            s, lo, hi = r
            wk, wv = self.wk[s][lo:hi], self.wv[s][lo:hi]
            for k in np.unique(wk):
                if k >= 0:
                    add(int(k), int(wv[wk == k].max()))
            if s == "ps":
                rm = self.rv[s][:, lo:hi].max(axis=1)
                for k in np.nonzero(rm)[0]:
                    if int(k) != myk:
                        add(int(k), int(rm[k]))
        for ap in writes:
            r = self._range(ap)
            if r is None:
                continue
            s, lo, hi = r
            wk, wv = self.wk[s][lo:hi], self.wv[s][lo:hi]
            for k in np.unique(wk):
                if k >= 0:
                    add(int(k), int(wv[wk == k].max()))
            rm = self.rv[s][:, lo:hi].max(axis=1)
            for k in np.nonzero(rm)[0]:
                add(int(k), int(rm[k]))
        waits = {}
        seen = self.seen[eng]
        for k, v in need.items():
            if k == myk or seen[k] < v:
                waits[k] = v
                if k != myk:
                    seen[k] = v
        return waits

    def _mark(self, kind, val, reads, writes):
        for ap in reads:
            r = self._range(ap)
            if r is None:
                continue
            s, lo, hi = r
            self.rv[s][kind, lo:hi] = val
        for ap in writes:
            r = self._range(ap)
            if r is None:
                continue
            s, lo, hi = r
            self.wk[s][lo:hi] = kind
            self.wv[s][lo:hi] = val
            self.rv[s][:, lo:hi] = 0

    def op(self, eng, fn, reads=(), writes=(), accum=False):
        waits = self._deps(eng, reads, writes)
        self.ops[eng].append((fn, waits, ("accum", eng) if accum else None))
        self._mark(self.kidx[eng], len(self.ops[eng]), reads, writes)

    def dma(self, out, in_, sem, eng="sp", is_output=False, after=(), **kw):
        waits = self._deps(eng, [in_], [out])
        for s_, v_ in after:
            kk_ = self.kidx["d%d" % s_]
            if waits.get(kk_, 0) < v_:
                waits[kk_] = v_
        self.dma_cnt[sem] += 1
        val = 16 * self.dma_cnt[sem]
        fn = lambda e, out=out, in_=in_, kw=kw: e.dma_start(out=out, in_=in_, **kw)
        self.ops[eng].append((fn, waits, sem))
        self._mark(self.kidx["d%d" % sem], val, [in_], [out])
        if is_output:
            self.out_dma.append(sem)

    def emit(self):
        nc = self.nc
        waited = {e: set() for e in ENGS}
        selfw = {e: set() for e in ENGS}
        for e in ENGS:
            for fn, waits, dsem in self.ops[e]:
                for k, v in waits.items():
                    if k < len(ENGS):
                        waited[ENGS[k]].add(v)
                        if ENGS[k] == e:
                            selfw[e].add(v)
        rank = {e: {s: i + 1 for i, s in enumerate(sorted(waited[e]))} for e in ENGS}
        print("ops", {e: len(self.ops[e]) for e in ENGS}, "sem max", {e: len(rank[e]) for e in ENGS}, "dma", max(self.dma_cnt) * 16)
        with contextlib.ExitStack() as st:
            sems = {e: st.enter_context(nc.semaphore("s_" + e)) for e in ENGS}
            dsems = [st.enter_context(nc.semaphore("sd%d" % i)) for i in range(max(self.nsem, 1))]
            block = st.enter_context(nc.Block())

            def run(engname):
                def body(engine):
                    seq = 0
                    for fn, waits, dsem in self.ops[engname]:
                        seq += 1
                        for k, v in sorted(waits.items()):
                            if k < len(ENGS):
                                engine.wait_ge(sems[ENGS[k]], rank[ENGS[k]][v])
                            else:
                                engine.wait_ge(dsems[k - len(ENGS)], v)
                        ins = fn(engine)
                        if isinstance(dsem, tuple):
                            if seq in rank[engname] and seq not in selfw[engname]:
                                ins.then_inc(sems[engname], 1)
                            elif seq in rank[engname]:
                                if engname == "dve":
                                    ins = engine.memset(self._dummy[:, 0:1], 0.0)
                                else:
                                    ins = engine.activation(out=self._dummy_act[:, 0:1], in_=self._dummy_act[:, 1:2], func=AF.Copy)
                                ins.then_inc(sems[engname], 1)
                        elif dsem is not None:
                            ins.then_inc(dsems[dsem], 16)
                        elif seq in rank[engname]:
                            ins.then_inc(sems[engname], 1)
                    if engname == "sp":
                        for s in sorted(set(self.out_dma)):
                            engine.wait_ge(dsems[s], 16 * self.dma_cnt[s])
                return body
            block.tensor(run("pe"))
            block.scalar(run("act"))
            block.vector(run("dve"))
            block.gpsimd(run("pool"))
            block.sync(run("sp"))


def _aps(*xs):
    return [x for x in xs if not isinstance(x, (int, float)) and x is not None]


class K:
    def __init__(self, P):
        self.P = P

    def tt(self, eng, out, a, b, op):
        self.P.op(eng, lambda e: e.tensor_tensor(out=out, in0=a, in1=b, op=op), [a, b], [out])

    def ts(self, eng, out, a, s1, op0, s2=None, op1=None, accum_out=None):
        def fn(e):
            if op1 is None:
                return e.tensor_scalar(out=out, in0=a, scalar1=s1, scalar2=None, op0=op0)
            if accum_out is not None:
                return e.tensor_scalar(out=out, in0=a, scalar1=s1, scalar2=s2, op0=op0, op1=op1, accum_out=accum_out)
            return e.tensor_scalar(out=out, in0=a, scalar1=s1, scalar2=s2, op0=op0, op1=op1)
        self.P.op(eng, fn, [a] + _aps(s1, s2), [out] + _aps(accum_out), accum=accum_out is not None)

    def stt(self, out, a, s, b, op0, op1):
        self.P.op("dve", lambda e: e.scalar_tensor_tensor(out=out, in0=a, scalar=s, in1=b, op0=op0, op1=op1),
                  [a, b] + _aps(s), [out])

    def act(self, out, a, func, bias=None, scale=1.0, accum_out=None):
        def fn(e):
            kw = {}
            if bias is not None:
                kw["bias"] = bias
            if accum_out is not None:
                kw["accum_out"] = accum_out
            return e.activation(out=out, in_=a, func=func, scale=scale, **kw)
        self.P.op("act", fn, [a] + _aps(bias, scale), [out] + _aps(accum_out), accum=accum_out is not None)

    def cp(self, eng, out, a):
        if eng == "act":
            self.P.op("act", lambda e: e.activation(out=out, in_=a, func=AF.Copy), [a], [out])
        else:
            self.P.op(eng, lambda e: e.tensor_copy(out=out, in_=a), [a], [out])

    def memset(self, eng, out, val):
        self.P.op(eng, lambda e: e.memset(out, val), [], [out])

    def mm(self, out, lhsT, rhs, start, stop, skip=False):
        self.P.op("pe", lambda e: e.matmul(out, lhsT=lhsT, rhs=rhs, start=start, stop=stop, skip_group_check=skip),
                  [lhsT, rhs], [out])

    def tr(self, out, a, ident):
        self.P.op("pe", lambda e: e.transpose(out, a, ident), [a, ident], [out])


SPLITS = [512, 128, 128, 256, 32, 8, 512, 256, 256, 256, 256]
NFM = 24


def _swap_perm(width, hd, half):
    perm = np.arange(width)
    for c in range(width):
        d = c % hd
        if d < half:
            perm[c] = c + half
        elif d < 2 * half:
            perm[c] = c - half
    return perm


def _rope_tables(hd, reps):
    r = hd // 4
    half = r // 2
    inv = (np.float32(500000.0) ** (-np.arange(0, half, dtype=np.float32) * np.float32(2.0) / np.float32(r))).astype(np.float32)
    pos = np.arange(L, dtype=np.float32)
    ang = (pos[:, None] * inv[None, :]).astype(np.float32)
    cos = np.cos(ang).astype(np.float32).T
    sin = np.sin(ang).astype(np.float32).T
    C = np.ones((hd, L), np.float32)
    S = np.zeros((hd, L), np.float32)
    C[0:half] = cos
    C[half:2 * half] = cos
    S[0:half] = -sin
    S[half:2 * half] = sin
    return np.tile(C, (reps, 1)), np.tile(S, (reps, 1))


def host_consts():
    c64, s64 = _rope_tables(64, 2)
    c32, s32 = _rope_tables(32, 4)
    ident = np.eye(128, dtype=np.float32)
    causal = np.where(np.arange(128)[None, :] <= np.arange(128)[:, None], 0.0, -BIG).astype(np.float32)
    chix = np.ascontiguousarray(np.repeat(np.arange(256, dtype=np.float32)[None, :], 128, axis=0))
    pm64 = np.zeros((128, 128), np.float32)
    pm64[_swap_perm(128, 64, 8), np.arange(128)] = 1.0
    pm32 = np.zeros((128, 128), np.float32)
    pm32[_swap_perm(128, 32, 4), np.arange(128)] = 1.0
    return dict(c64=c64, s64=s64, c32=c32, s32=s32, ident=ident, causal=causal, chix=chix, pm64=pm64, pm32=pm32)


def host_weights(w_in, w_mem_kv, lam_re, lam_im, log_dt, b_re, b_im, c_re, c_im, d_skip, w_glu, b_glu, w_out, ln_g, ln_b):
    sp = np.concatenate([[0], np.cumsum(SPLITS)])
    col = lambda i: w_in[:, sp[i]:sp[i + 1]]
    q_c, k_c, v_c, qi_c, ki_c, wi_c, gatt_c, u_c, gssm_c, qm_c, gmem_c = [col(i) for i in range(11)]
    p64_512 = _swap_perm(512, 64, 8)
    p64_128 = _swap_perm(128, 64, 8)
    p32_256 = _swap_perm(256, 32, 4)
    p32_32 = _swap_perm(32, 32, 4)
    tiles = []
    for t in range(2):
        tiles += [qi_c[:, 128 * t:128 * t + 128]]
    for v in range(4):
        a = np.zeros((1024, 128), np.float32)
        a[:, 32 * v:32 * v + 32] = ki_c
        tiles += [a]
    for t in range(4):
        tiles += [q_c[:, 128 * t:128 * t + 128]]
    for g in range(2):
        tiles += [np.concatenate([k_c[:, 64 * g:64 * g + 64]] * 2, axis=1)]
    tiles += [u_c[:, 0:128], u_c[:, 128:256], qm_c[:, 0:128], qm_c[:, 128:256]]
    gates = np.concatenate([gatt_c, gssm_c, gmem_c], axis=1)
    for t in range(8):
        tiles.append(gates[:, 128 * t:128 * t + 128])
    assert len(tiles) == NFM
    wfm = np.ascontiguousarray(np.stack(tiles, 0)).astype(np.float32)
    wtm = np.ascontiguousarray(np.concatenate([v_c, wi_c], axis=1)).astype(np.float32)
    bre = np.zeros((8, 128, 128), np.float32)
    bim = np.zeros((8, 128, 128), np.float32)
    cre = np.zeros((8, 128, 128), np.float32)
    cim = np.zeros((8, 128, 128), np.float32)
    for i in range(8):
        for gl in range(2):
            g = 2 * i + gl
            g8 = g % 8
            bre[i, 16 * g8:16 * g8 + 16, 64 * gl:64 * gl + 64] = b_re[g].T
            bim[i, 16 * g8:16 * g8 + 16, 64 * gl:64 * gl + 64] = b_im[g].T
            cre[i, 64 * gl:64 * gl + 64, 16 * g8:16 * g8 + 16] = c_re[g].T
            cim[i, 64 * gl:64 * gl + 64, 16 * g8:16 * g8 + 16] = c_im[g].T
    dblk = np.zeros((2, 128, 128), np.float32)
    dflat = d_skip.reshape(256)
    for ct in range(2):
        dblk[ct][np.arange(128), np.arange(128)] = dflat[128 * ct:128 * ct + 128]

    def st(a):
        return np.ascontiguousarray(a.reshape(8, 2, 64).transpose(1, 2, 0).reshape(128, 8)).astype(np.float32)
    lamre = st(lam_re)
    lamim = st(lam_im)
    logdt = st(np.repeat(log_dt[:, None], 64, axis=1))
    bglu = np.ascontiguousarray(b_glu.reshape(2, 128).T).astype(np.float32)
    lng = np.ascontiguousarray(np.repeat(ln_g[None, :], 128, axis=0)).astype(np.float32)
    lnb = np.ascontiguousarray(np.repeat(ln_b[None, :], 128, axis=0)).astype(np.float32)
    bre = np.ascontiguousarray(bre.transpose(0, 2, 1))
    bim = np.ascontiguousarray(bim.transpose(0, 2, 1))
    return dict(wfm=wfm, wtm=wtm, wmem=np.ascontiguousarray(w_mem_kv), wglu=np.ascontiguousarray(w_glu),
                wout=np.ascontiguousarray(w_out), bre=bre, bim=bim, cre=cre, cim=cim, dblk=dblk,
                lamre=lamre, lamim=lamim, logdt=logdt, bglu=bglu, lng=lng, lnb=lnb)


def build(stop=99, debug=False):
    nc = bass.Bass("TRN2", target_bir_lowering=False)

    def din(name, shape):
        return nc.dram_tensor(name, list(shape), F32, kind="ExternalInput").ap()
    xT = din("xT", [BPC, 1024, L])
    xtm = din("x", [BPC, L, 1024])
    memT = din("memT", [BPC, 1024, 256])
    wfm = din("wfm", [NFM, 1024, 128])
    wtm = din("wtm", [1024, 136])
    wmem = din("wmem", [1024, 512])
    wglu = din("wglu", [256, 256])
    wout = din("wout", [1024, 1024])
    d_c64, d_s64, d_c32, d_s32 = [din(n, [128, L]) for n in ("c64", "s64", "c32", "s32")]
    d_bre, d_bim, d_cre, d_cim = [din(n, [8, 128, 128]) for n in ("bre", "bim", "cre", "cim")]
    d_dblk = din("dblk", [2, 128, 128])
    d_lamre, d_lamim, d_logdt = [din(n, [128, 8]) for n in ("lamre", "lamim", "logdt")]
    d_bglu = din("bglu", [128, 2])
    d_lng = din("lng", [128, 1024])
    d_lnb = din("lnb", [128, 1024])
    d_ident = din("ident", [128, 128])
    d_causal = din("causal", [128, 128])
    d_chix = din("chix", [128, 256])
    d_pm64 = din("pm64", [128, 128])
    d_pm32 = din("pm32", [128, 128])
    out = nc.dram_tensor("out", [BPC, L, 1024], F32, kind="ExternalOutput").ap()
    gsc = nc.dram_tensor("gsc", [BPC, 8, 128, L], BF16).ap()
    ssc = nc.dram_tensor("ssc", [BPC, 2, 128, L], BF16).ap()

    P = Prog(nc)
    k = K(P)
    dbg = {}

    def dump(name, src, shape, dtype=F32):
        if not debug:
            return
        t = nc.dram_tensor("dbg_" + name, list(shape), dtype, kind="ExternalOutput").ap()
        P.dma(t, src, P.newsem(), is_output=True)
        dbg[name] = (shape, dtype)

    identf = P.sb([128], F32)
    identb = P.sb([128], BF16)
    ident4 = P.sb([4, 128], BF16)
    causal = P.sb([128], F32)
    pmb = P.sb([2, 128], BF16)
    wglub = P.sb([2, 256], BF16)
    bglu = P.sb([2], F32)
    wtmb = P.sb([8, 136], BF16)
    sc8 = {n: P.sb([8], F32) for n in ("lre", "lim", "ldt", "dt", "a", "th", "rho", "t1", "t2", "sin", "cos", "lbr", "lbi",
                                       "nre", "nim", "den", "kre", "kim", "nkim", "u1", "u2", "r8")}
    sc8i = P.sb([8], I32)
    mure = P.sb([11, 8], F32)
    muim = P.sb([11, 8], F32)
    nmuim = P.sb([11, 8], F32)
    uT = P.sb([2, L], BF16)
    PB0 = P.sb_off
    qT = P.sb([4, L], BF16)
    kT = P.sb([2, L], BF16)
    qiT = P.sb([2, L], BF16)
    kiT4 = P.sb([4, L], BF16)
    vaug = P.sb([NB, 2, 65], BF16)
    absw = P.sb([NB, 8], F32)
    sgnw = P.sb([NB, 8], F32)
    qmT = P.sb([2, L], BF16)
    mkT = P.sb([2, 256], BF16)
    mvaug = P.sb([2, 4, 65], BF16)
    woutb = P.sb([8, 1024], BF16)
    lng = P.sb([1024], F32)
    lnb = P.sb([1024], F32)
    PB1 = P.sb_off
    XC = P.sb([8, L], BF16)
    small = P.sb([64], F32)
    ARENA0 = P.sb_off
    ARENA_SZ = SB_BYTES - ARENA0
    print("resident bytes", ARENA0, "arena", ARENA_SZ)

    class Arena:
        def __init__(self):
            self.off = ARENA0

        def sb(self, shape, dtype):
            n = int(np.prod(shape)) * ESIZE[dtype]
            n = (n + SLOT - 1) // SLOT * SLOT
            o = self.off
            self.off += n
            assert self.off <= SB_BYTES, "phase arena overflow %d" % (self.off - ARENA0)
            return P.sb_at(o, shape, dtype)

    A = Arena()
    stg = A.sb([8, 1024], F32)
    P.dma(identf, d_ident, P.newsem())
    P.dma(causal, d_causal, P.newsem())
    P.dma(bglu, d_bglu, P.newsem())
    for n, d in (("lre", d_lamre), ("lim", d_lamim), ("ldt", d_logdt)):
        P.dma(sc8[n], d, P.newsem())
    k.cp("dve", identb, identf)
    for q_, dsrc in enumerate((d_pm32, d_pm64)):
        P.dma(stg[:, q_, 0:128], dsrc, P.newsem())
        k.cp("dve", pmb[:, q_, :], stg[:, q_, 0:128])
    for r in range(4):
        k.cp("dve", ident4[:, r, :], identf)
    s_stg = P.newsem()
    v = stg[:, 0, 0:512].rearrange("p (i c) -> p i c", i=2)
    P.dma(v, wglu.rearrange("(k p) c -> p k c", p=128), s_stg)
    k.cp("dve", wglub, v)
    v = stg[:, 0:2, :].rearrange("p a b -> p (a b)")[:, 0:8 * 136].rearrange("p (i c) -> p i c", i=8)
    P.dma(v, wtm.rearrange("(k p) c -> p k c", p=128), s_stg)
    k.cp("dve", wtmb, v)

    s = sc8
    k.act(s["dt"], s["ldt"], AF.Exp)
    k.tt("dve", s["a"], s["lre"], s["dt"], ALU.mult)
    k.tt("dve", s["th"], s["lim"], s["dt"], ALU.mult)
    k.act(s["rho"], s["a"], AF.Exp)

    def sin_of(dst, src, shift):
        k.ts("dve", s["t1"], src, shift, ALU.add, 1.0 / (2 * PI), ALU.mult)
        k.cp("dve", sc8i, s["t1"])
        k.cp("dve", s["t2"], sc8i)
        k.ts("dve", s["t1"], src, shift, ALU.add)
        k.stt(s["t1"], s["t2"], -2 * PI, s["t1"], ALU.mult, ALU.add)
        k.ts("dve", s["t1"], s["t1"], 3.1415925, ALU.min, -3.1415925, ALU.max)
        k.act(dst, s["t1"], AF.Sin)
    sin_of(s["sin"], s["th"], 0.0)
    sin_of(s["cos"], s["th"], PI / 2)
    k.tt("dve", s["lbr"], s["rho"], s["cos"], ALU.mult)
    k.tt("dve", s["lbi"], s["rho"], s["sin"], ALU.mult)
    k.ts("dve", s["t1"], s["lbr"], -1.0, ALU.add)
    k.tt("dve", s["nre"], s["t1"], s["lre"], ALU.mult)
    k.tt("dve", s["u1"], s["lbi"], s["lim"], ALU.mult)
    k.tt("dve", s["nre"], s["nre"], s["u1"], ALU.add)
    k.tt("dve", s["nim"], s["lbi"], s["lre"], ALU.mult)
    k.tt("dve", s["u1"], s["t1"], s["lim"], ALU.mult)
    k.tt("dve", s["nim"], s["nim"], s["u1"], ALU.subtract)
    k.tt("dve", s["den"], s["lre"], s["lre"], ALU.mult)
    k.tt("dve", s["u1"], s["lim"], s["lim"], ALU.mult)
    k.tt("dve", s["den"], s["den"], s["u1"], ALU.add)
    P.op("dve", lambda e: e.reciprocal(out=s["u2"], in_=s["den"]), [s["den"]], [s["u2"]])
    k.tt("dve", s["kre"], s["nre"], s["u2"], ALU.mult)
    k.tt("dve", s["kim"], s["nim"], s["u2"], ALU.mult)
    k.ts("dve", s["nkim"], s["kim"], -1.0, ALU.mult)
    k.cp("dve", mure[:, 0, :], s["lbr"])
    k.cp("dve", muim[:, 0, :], s["lbi"])
    for lv in range(1, 11):
        k.tt("dve", s["u1"], mure[:, lv - 1, :], mure[:, lv - 1, :], ALU.mult)
        k.tt("dve", s["u2"], muim[:, lv - 1, :], muim[:, lv - 1, :], ALU.mult)
        k.tt("dve", mure[:, lv, :], s["u1"], s["u2"], ALU.subtract)
        k.tt("dve", s["u1"], mure[:, lv - 1, :], muim[:, lv - 1, :], ALU.mult)
        k.ts("dve", muim[:, lv, :], s["u1"], 2.0, ALU.mult)
    k.ts("dve", nmuim, muim, -1.0, ALU.mult)
    if debug:
        dump("mure", mure, [128, 11, 8])
        dump("muim", muim, [128, 11, 8])
        dump("kre", s["kre"], [128, 8])
        dump("kim", s["kim"], [128, 8])

    s_x = P.newsem()
    s_x2 = P.newsem()
    s_tab = [P.newsem() for _ in range(2)]
    s_w = [P.newsem() for _ in range(2)]
    s_g = [P.newsem() for _ in range(2)]
    s_misc = P.newsem()
    s_misc2 = P.newsem()
    s_out = [P.newsem() for _ in range(2)]
    s_xt = [P.newsem() for _ in range(2)]
    s_gl = [P.newsem() for _ in range(2)]
    psrot = [0]

    def nextbank(lo=0, hi=8):
        b = lo + psrot[0] % (hi - lo)
        psrot[0] += 1
        return b


    T8 = 8
    NCH = L // T8
    s_ss = [P.newsem() for _ in range(2)]
    s_ssl = P.newsem()
    s_bt = [P.newsem() for _ in range(5)]
    s_xs = P.newsem()
    s_ws = [P.newsem() for _ in range(2)]
    W1 = P.sb_at(PB0, [8, T8, 2, 128], BF16)
    W2 = P.sb_at(PB0 + 32768, [8, T8, 2, 128], BF16)
    Kt = P.sb_at(PB0 + 65536, [2, T8, 128], BF16)
    assert PB0 + 65536 + 4096 <= PB1
    A = Arena()
    BTr = A.sb([8, 128], F32)
    BTi = A.sb([8, 128], F32)
    CTr = A.sb([8, 128], F32)
    CTi = A.sb([8, 128], F32)
    nCTi = A.sb([8, 128], F32)
    dbf = A.sb([2, 128], F32)
    bts = [[A.sb([128], F32) for _ in range(2)] for _ in range(2)]
    w2t = [A.sb([128], F32) for _ in range(2)]
    LKr, LKi, nLKr, nLKi = [A.sb([T8 + 1, 8], F32) for _ in range(4)]
    GKr, GKi, nGKi = [A.sb([T8, 8], F32) for _ in range(3)]
    for dsrc, dst, sm in ((d_bre, BTr, 0), (d_bim, BTi, 1), (d_cre, CTr, 2), (d_cim, CTi, 3)):
        P.dma(dst, dsrc.rearrange("i p c -> p i c"), s_bt[sm])
    P.dma(dbf, d_dblk.rearrange("i p c -> p i c"), s_bt[4])
    k.ts("pool", nCTi, CTi, -1.0, ALU.mult, 0.0, ALU.add)
    k.cp("dve", LKr[:, 1, :], s["lbr"])
    k.cp("dve", LKi[:, 1, :], s["lbi"])
    for kk in range(2, T8 + 1):
        k.tt("dve", s["u1"], LKr[:, kk - 1, :], s["lbr"], ALU.mult)
        k.tt("dve", s["u2"], LKi[:, kk - 1, :], s["lbi"], ALU.mult)
        k.tt("dve", LKr[:, kk, :], s["u1"], s["u2"], ALU.subtract)
        k.tt("dve", s["u1"], LKr[:, kk - 1, :], s["lbi"], ALU.mult)
        k.tt("dve", s["u2"], LKi[:, kk - 1, :], s["lbr"], ALU.mult)
        k.tt("dve", LKi[:, kk, :], s["u1"], s["u2"], ALU.add)
    k.ts("dve", nLKr[:, 1:, :], LKr[:, 1:, :], -1.0, ALU.mult)
    k.ts("dve", nLKi[:, 1:, :], LKi[:, 1:, :], -1.0, ALU.mult)
    k.cp("dve", GKr[:, 0, :], s["kre"])
    k.cp("dve", GKi[:, 0, :], s["kim"])
    for kk in range(1, T8):
        k.tt("dve", s["u1"], LKr[:, kk, :], s["kre"], ALU.mult)
        k.tt("dve", s["u2"], LKi[:, kk, :], s["kim"], ALU.mult)
        k.tt("dve", GKr[:, kk, :], s["u1"], s["u2"], ALU.subtract)
        k.tt("dve", s["u1"], LKr[:, kk, :], s["kim"], ALU.mult)
        k.tt("dve", s["u2"], LKi[:, kk, :], s["kre"], ALU.mult)
        k.tt("dve", GKi[:, kk, :], s["u1"], s["u2"], ALU.add)
    k.ts("dve", nGKi, GKi, -1.0, ALU.mult)
    COST = P.sb_at(PB0 + 65536 + 4096, [8, NCH], F32)
    SINT = P.sb_at(PB0 + 65536 + 4096 + 8192, [8, NCH], F32)
    assert PB0 + 65536 + 4096 + 16384 <= PB1
    r8 = s["r8"]
    ff = A.sb([8], F32)
    ffi = A.sb([8], I32)
    chix = A.sb([NCH], F32)
    Gt = A.sb([8, NCH], F32)
    Gi = A.sb([8, NCH], I32)
    Gf = A.sb([8, NCH], F32)
    P.dma(chix, d_chix, P.newsem())
    k.act(r8, s["a"], AF.Exp, scale=float(T8))
    k.ts("dve", ff, s["th"], float(T8) / (2 * PI), ALU.mult)
    k.cp("dve", ffi, ff)
    k.cp("dve", s["u1"], ffi)
    k.tt("dve", ff, ff, s["u1"], ALU.subtract)
    k.tt("dve", Gt, ff.unsqueeze(2).broadcast_to([128, 8, NCH]), chix.unsqueeze(1).broadcast_to([128, 8, NCH]), ALU.mult)
    for shift, dstT in ((0.0, SINT), (0.25, COST)):
        src = Gt
        if shift != 0.0:
            k.ts("dve", Gf, Gt, shift, ALU.add)
            src = Gf
        k.cp("dve", Gi, src)
        k.cp("pool", dstT, Gi)
        k.tt("dve", dstT, src, dstT, ALU.subtract)
        k.ts("dve", dstT, dstT, 2 * PI, ALU.mult, 3.1415925, ALU.min)
        k.ts("dve", dstT, dstT, -3.1415925, ALU.max)
        k.act(dstT, dstT, AF.Sin)
    nb_ = 0
    for kk in range(T8):
        for ct in range(2):
            kp = P.ps(ct, (128,))
            for i in range(4 * ct, 4 * ct + 4):
                br_, bi_ = bts[nb_ % 2]
                nb_ += 1
                gr = GKr[:, kk, i:i + 1]
                gi = GKi[:, kk, i:i + 1]
                ngi = nGKi[:, kk, i:i + 1]
                k.ts("dve", br_, BTr[:, i, :], gr, ALU.mult)
                k.stt(br_, BTi[:, i, :], ngi, br_, ALU.mult, ALU.add)
                k.ts("dve", bi_, BTr[:, i, :], gi, ALU.mult)
                k.stt(bi_, BTi[:, i, :], gr, bi_, ALU.mult, ALU.add)
                for x_, src in ((0, br_), (1, bi_)):
                    pt = P.ps(4 + (2 * nb_ + x_) % 4, (128,))
                    k.tr(pt, src, identf)
                    k.cp("act", W1[:, i, T8 - 1 - kk, x_, :], pt)
                k.mm(kp, br_, CTr[:, i, :], i == 4 * ct, False)
                k.mm(kp, bi_, nCTi[:, i, :], False, i == 4 * ct + 3)
            if kk == 0:
                k.tt("dve", Kt[:, ct, kk, :], kp, dbf[:, ct, :], ALU.add)
            else:
                k.cp("act", Kt[:, ct, kk, :], kp)
    for i in range(8):
        for j in range(T8):
            lr = LKr[:, j + 1, i:i + 1]
            nli = nLKi[:, j + 1, i:i + 1]
            nlr = nLKr[:, j + 1, i:i + 1]
            t_ = w2t[(i * T8 + j) % 2]
            k.ts("pool", t_, CTr[:, i, :], lr, ALU.mult, 0.0, ALU.add)
            k.stt(W2[:, i, j, 0, :], CTi[:, i, :], nli, t_, ALU.mult, ALU.add)
            t2_ = w2t[(i * T8 + j + 1) % 2]
            k.act(t2_, CTr[:, i, :], AF.Copy, scale=nli)
            k.stt(W2[:, i, j, 1, :], CTi[:, i, :], nlr, t2_, ALU.mult, ALU.add)

    import os
    PHS = int(os.environ.get("PHS_VARIANT", "99"))
    A = Arena()
    xstS = A.sb([L], F32)
    wstS = [A.sb([8, 128], F32) for _ in range(2)]
    wbfS = [A.sb([8, 128], BF16) for _ in range(2)]
    Lr = A.sb([8, NCH], F32)
    Li = A.sb([8, NCH], F32)
    Br = A.sb([8, NCH], F32)
    Bi = A.sb([8, NCH], F32)
    Spr = A.sb([8, NCH], BF16)
    Spi = A.sb([8, NCH], BF16)
    uTd = A.sb([2, T8, NCH], BF16)
    ygb = P.sb_at(ARENA0, [2, L], BF16)
    osb = [P.sb_at(ARENA0 + 8192 + 4096 * q_, [L], BF16) for q_ in range(2)]
    assert 8192 + 8192 <= 8192 + 2 * 4096 + 2 * 2048
    k.memset("pool", Spr[:, :, 0:1], 0.0)
    k.memset("pool", Spi[:, :, 0:1], 0.0)
    def xload_S(bb):
        for kk in range(8):
            P.dma(xstS, xT[bb, 128 * kk:128 * kk + 128, :], s_xs)
            k.cp("dve" if kk % 2 == 0 else "act", XC[:, kk, :], xstS)
    xload_S(0)
    for b in range(BPC if PHS > 10 else 0):
        for ct in range(2):
            P.dma(wstS[ct], wfm[12 + ct].rearrange("(k p) c -> p k c", p=128), s_ws[ct])
            k.cp("act", wbfS[ct], wstS[ct])
            for c in range(4):
                ps = P.ps(nextbank(0, 4))
                for kk in range(8):
                    k.mm(ps, wbfS[ct][:, kk, :], XC[:, kk, 512 * c:512 * c + 512], kk == 0, kk == 7)
                k.cp("act", uT[:, ct, 512 * c:512 * c + 512], ps)
        xload_S(b + 1 if b + 1 < BPC else 0)
        for ct in range(2):
            k.cp("pool", uTd[:, ct, :, :], uT[:, ct, :].rearrange("p (c j) -> p j c", j=T8))
        for i in range(8):
            ct = i // 4
            for x_, dstL in ((0, Lr), (1, Li)):
                q_ = 2 * i + x_
                psL = P.ps(4 + q_ % 4, (NCH,))
                for j in range(T8):
                    k.mm(psL, W1[:, i, j, x_, :], uTd[:, ct, j, :], j == 0, j == T8 - 1, skip=True)
                k.cp("act", dstL[:, i, :], psL)
        T1, T2 = Br, Bi
        k.tt("dve", T1, COST, Lr, ALU.mult)
        k.tt("dve", T2, SINT, Li, ALU.mult)
        k.tt("dve", T1, T1, T2, ALU.add)
        k.tt("dve", T2, SINT, Lr, ALU.mult)
        k.tt("dve", Li, COST, Li, ALU.mult)
        k.tt("dve", Li, Li, T2, ALU.subtract)
        for i in range(8):
            d0 = r8[:, i:i + 1].broadcast_to([128, NCH])
            P.op("dve", lambda e, o_=Lr[:, i, :], d0=d0, d1=T1[:, i, :]: e.tensor_tensor_scan(out=o_, data0=d0, data1=d1, initial=0.0, op0=ALU.mult, op1=ALU.add),
                 [T1[:, i, :], r8], [Lr[:, i, :]])
            P.op("dve", lambda e, o_=T2[:, i, :], d0=d0, d1=Li[:, i, :]: e.tensor_tensor_scan(out=o_, data0=d0, data1=d1, initial=0.0, op0=ALU.mult, op1=ALU.add),
                 [Li[:, i, :], r8], [T2[:, i, :]])
        k.tt("dve", T1, COST, Lr, ALU.mult)
        k.tt("dve", Li, SINT, T2, ALU.mult)
        k.tt("dve", T1, T1, Li, ALU.subtract)
        k.tt("dve", Li, SINT, Lr, ALU.mult)
        k.tt("dve", T2, COST, T2, ALU.mult)
        k.tt("dve", Li, Li, T2, ALU.add)
        for i in range(8):
            k.cp("act", Spr[:, i, 1:NCH], T1[:, i, 0:NCH - 1])
            k.cp("act", Spi[:, i, 1:NCH], Li[:, i, 0:NCH - 1])
        if PHS <= 20:
            continue
        y = P.sb_at(ARENA0 + 20480, [L], F32)
        t1 = P.sb_at(ARENA0 + 20480 + 8192, [L], F32)
        t2 = P.sb_at(ARENA0 + 20480 + 16384, [L], F32)
        sg = P.sb_at(ARENA0 + 20480 + 24576, [L], F32)
        for ct in range(2):
            for c4 in range(4):
                yp = P.ps(c4, (64, T8))
                uv = uT[:, ct, 512 * c4:512 * c4 + 512].rearrange("p (c j) -> p c j", j=T8)
                for tau in range(T8):
                    k.mm(yp[:, :, tau:T8], Kt[:, ct, tau, :], uv[:, :, 0:T8 - tau], tau == 0, False, skip=True)
                for i in range(4 * ct, 4 * ct + 4):
                    for j in range(T8):
                        k.mm(yp[:, :, j], W2[:, i, j, 0, :], Spr[:, i, 64 * c4:64 * c4 + 64], False, False, skip=True)
                        k.mm(yp[:, :, j], W2[:, i, j, 1, :], Spi[:, i, 64 * c4:64 * c4 + 64], False,
                             (i == 4 * ct + 3 and j == T8 - 1), skip=True)
            for c in range(4):
                sl = slice(512 * c, 512 * c + 512)
                k.cp("act", y[:, sl], P.ps(c))
            if debug and b == 0:
                dump("y%d" % ct, y, [128, L])
            k.tt("dve", t1, y, y, ALU.mult)
            k.ts("dve", t1, t1, 0.044715, ALU.mult, 1.0, ALU.add)
            k.tt("dve", t1, t1, y, ALU.mult)
            k.act(t2, t1, AF.Sigmoid, scale=2.0 * 0.7978845608028654)
            k.tt("dve", ygb[:, ct, :], y, t2, ALU.mult)
        for et in range(2):
            for c in range(4):
                sl = slice(512 * c, 512 * c + 512)
                ps = P.ps(nextbank(4, 8))
                for kc in range(2):
                    k.mm(ps, wglub[:, kc, 128 * et:128 * et + 128], ygb[:, kc, sl], kc == 0, kc == 1)
                k.act(sg[:, sl], ps, AF.Sigmoid, bias=bglu[:, et:et + 1])
                k.tt("dve", osb[et][:, sl], ygb[:, et, sl], sg[:, sl], ALU.mult)
            P.dma(ssc[b, et], osb[et], s_ss[et])
    A = Arena()
    stg = A.sb([8, 1024], F32)
    P.dma(lng, d_lng, P.newsem())
    P.dma(lnb, d_lnb, P.newsem())
    P.dma(stg, wout.rearrange("(k p) c -> p k c", p=128), s_stg)
    for kk in range(8):
        k.cp("dve" if kk % 2 == 0 else "act", woutb[:, kk, :], stg[:, kk, :])

    for b in range(BPC):
        A = Arena()
        xst = A.sb([L], F32)
        xst2 = A.sb([L], F32)
        wst = [A.sb([8, 128], F32) for _ in range(2)]
        wbf = [A.sb([8, 128], BF16) for _ in range(2)]
        tabC = A.sb([L], F32)
        tabS = A.sb([L], F32)
        tmpA = [A.sb([512], F32) for _ in range(3)]
        zbf = [A.sb([512], BF16) for _ in range(3)]
        tmp2 = [A.sb([512], F32) for _ in range(2)]
        gst = [A.sb([L], BF16) for _ in range(1)]
        for kk in range(8 if b > 0 else 0):
            xs_ = xst if kk % 2 == 0 else xst2
            P.dma(xs_, xT[b, 128 * kk:128 * kk + 128, :], s_x if kk % 2 == 0 else s_x2)
            k.cp("dve" if kk % 2 == 0 else "act", XC[:, kk, :], xs_)

        widx = {}

        def load_w(m):
            widx[m] = len(widx) % 2
            P.dma(wst[widx[m]], wfm[m].rearrange("(k p) c -> p k c", p=128), s_w[widx[m]])
            k.cp("act", wbf[widx[m]], wst[widx[m]])

        def proj(m, c):
            bank = nextbank()
            ps = P.ps(bank)
            for kk in range(8):
                k.mm(ps, wbf[widx[m]][:, kk, :], XC[:, kk, 512 * c:512 * c + 512], kk == 0, kk == 7)
            return ps
        dests = {}
        for t in range(2):
            dests[t] = ("rope", qiT[:, t, :], 0)
        for v_ in range(4):
            dests[2 + v_] = ("rope", kiT4[:, v_, :], 0)
        for t in range(4):
            dests[6 + t] = ("rope", qT[:, t, :], 1)
        for g in range(2):
            dests[10 + g] = ("rope", kT[:, g, :], 1)
        dests[14] = ("plain", qmT[:, 0, :], None)
        dests[15] = ("plain", qmT[:, 1, :], None)
        for t in range(8):
            dests[16 + t] = ("gate", t, None)
        order = [m for m in range(NFM) if m not in (12, 13)]
        nta = 0
        pend = [None]
        for oi, m in enumerate(order):
            if m == 0:
                P.dma(tabC, d_c32, s_tab[0])
                P.dma(tabS, d_s32, s_tab[1])
                load_w(0)
            if m == 6:
                if pend[0] is not None:
                    pend[0]()
                    pend[0] = None
                P.dma(tabC, d_c64, s_tab[0])
                P.dma(tabS, d_s64, s_tab[1])
            if oi + 1 < len(order):
                load_w(order[oi + 1])
            kind, dst, pq = dests[m]
            if kind == "rope":
                for c in range(4):
                    sl = slice(512 * c, 512 * c + 512)
                    ps = proj(m, c)
                    if pend[0] is not None:
                        pend[0]()
                        pend[0] = None
                    ta = tmpA[nta % 3][:, 0:512]
                    zb = zbf[nta % 3]
                    nta += 1
                    k.cp("act", zb, ps)
                    k.tt("dve", ta, ps, tabC[:, sl], ALU.mult)

                    def fin(zb=zb, ta=ta, sl=sl, dst=dst, pq=pq, c=c):
                        ps2 = P.ps(nextbank())
                        k.mm(ps2, pmb[:, pq, :], zb, True, True)
                        t2 = tmp2[c % 2]
                        k.tt("dve", t2, ps2, tabS[:, sl], ALU.mult)
                        k.tt("pool", dst[:, sl], t2, ta, ALU.add)
                    pend[0] = fin
            elif kind == "plain":
                for c in range(4):
                    ps = proj(m, c)
                    if pend[0] is not None:
                        pend[0]()
                        pend[0] = None
                    k.cp("act", dst[:, 512 * c:512 * c + 512], ps)
            else:
                gt = gst[0]
                for c in range(4):
                    ps = proj(m, c)
                    k.act(gt[:, 512 * c:512 * c + 512], ps, AF.Silu)
                P.dma(gsc[b, dst], gt, s_g[0], eng="act")
        for i in range(NB):
            bank = nextbank()
            ps = P.ps(bank, (136,))
            for kk in range(8):
                k.mm(ps, XC[:, kk, 128 * i:128 * i + 128], wtmb[:, kk, :], kk == 0, kk == 7)
            k.cp("act", vaug[:, i, :, 0:64], ps[:, 0:128].rearrange("p (g d) -> p g d", g=2))
            k.act(sgnw[:, i, :], ps[:, 128:136], AF.Sign)
            k.tt("dve", absw[:, i, :], ps[:, 128:136], sgnw[:, i, :], ALU.mult)
            k.ts("dve", absw[:, i, :], absw[:, i, :], 1.0 / 16.0, ALU.mult)
        if b == 0:
            k.memset("pool", vaug[:, :, :, 64:65], 1.0)
            k.memset("pool", mvaug[:, :, :, 64:65], 1.0)
        if debug and b == 0:
            dump("qT", qT, [128, 4, L], BF16)
            dump("kT", kT, [128, 2, L], BF16)
            dump("qiT", qiT, [128, 2, L], BF16)
            dump("kiT4", kiT4, [128, 4, L], BF16)
            dump("vaug", vaug, [128, NB, 2, 65], BF16)
            dump("absw", absw, [128, NB, 8])
            dump("sgnw", sgnw, [128, NB, 8])
            dump("qmT", qmT, [128, 2, L], BF16)
        if stop <= 1:
            continue

        A = Arena()
        mst = A.sb([8, 256], F32)
        mbf = A.sb([8, 256], BF16)
        wms = A.sb([8, 512], F32)
        wmb = A.sb([8, 512], BF16)
        P.dma(mst, memT[b].rearrange("(k p) c -> p k c", p=128), s_misc)
        P.dma(wms, wmem.rearrange("(k p) c -> p k c", p=128), s_misc2)
        k.cp("dve", mbf, mst)
        k.cp("pool", wmb, wms)
        for t in range(2):
            ps = P.ps(nextbank(), (256,))
            for kk in range(8):
                k.mm(ps, wmb[:, kk, 128 * t:128 * t + 128], mbf[:, kk, :], kk == 0, kk == 7)
            k.cp("act", mkT[:, t, :], ps)
        for mb_ in range(2):
            ps = P.ps(nextbank(), (256,))
            for kk in range(8):
                k.mm(ps, mbf[:, kk, 128 * mb_:128 * mb_ + 128], wmb[:, kk, 256:512], kk == 0, kk == 7)
            k.cp("act", mvaug[:, mb_, :, 0:64], ps.rearrange("p (h d) -> p h d", h=4))
        if debug and b == 0:
            dump("mkT", mkT, [128, 2, 256], BF16)
            dump("mvaug", mvaug, [128, 2, 4, 65], BF16)

        A = Arena()
        em = [A.sb([2, 512], BF16) for _ in range(2)]
        otm = [A.sb([128], BF16) for _ in range(2)]
        rd = A.sb([16], F32)
        nrd = 0
        for hp in range(2):
            for c in range(4):
                e_h = []
                for hh in range(2):
                    h = 2 * hp + hh
                    e = em[hh]
                    for mb_ in range(2):
                        ps = P.ps(nextbank(0, 4))
                        k.mm(ps, mkT[64 * hh:64 * hh + 64, hp, 128 * mb_:128 * mb_ + 128],
                             qmT[64 * hh:64 * hh + 64, hp, 512 * c:512 * c + 512], True, True)
                        k.act(e[:, mb_, :], ps, AF.Exp, scale=0.125)
                    e_h.append(e)
                for tb in range(4):
                    i = 4 * c + tb
                    o = otm[i % 2]
                    po = P.ps(4 + (i % 2), (2, 65))
                    for hh in range(2):
                        h = 2 * hp + hh
                        for mb_ in range(2):
                            k.mm(po[:, hh, :], e_h[hh][:, mb_, 128 * tb:128 * tb + 128], mvaug[:, mb_, h, :],
                                 (hh == 0 and mb_ == 0), mb_ == 1, skip=True)
                    r_ = rd[:, 2 * (nrd % 8):2 * (nrd % 8) + 2]
                    nrd += 1
                    P.op("dve", lambda e, r_=r_, po=po: e.reciprocal(out=r_, in_=po[:, :, 64]), [po], [r_])
                    k.tt("dve", o.rearrange("p (h d) -> p h d", h=2), po[:, :, 0:64],
                         r_.unsqueeze(2).broadcast_to([128, 2, 64]), ALU.mult)
                    pt = P.ps(6 + (i % 2), (128,), BF16)
                    k.tr(pt, o, identb)
                    k.cp("act", XC[:, 6 + hp, 128 * i:128 * i + 128], pt)
        if stop <= 4:
            continue

        A = Arena()
        NSC = 3
        sc = [A.sb([L], F32) for _ in range(NSC)]
        rt = [A.sb([512], F32) for _ in range(4)]
        junk = {"dve": A.sb([L], BF16), "act": A.sb([L], BF16)}
        mbk = [A.sb([L], BF16) for _ in range(NSC)]
        eg = [A.sb([4, 128], BF16) for _ in range(4)]
        oat = [A.sb([512], BF16) for _ in range(2)]
        scal = A.sb([NB, 32], F32)
        rda = A.sb([16], F32)
        nrt = [0]
        neg = [0]
        W0 = 32.0
        NIT = NBIS - 2

        def g_scores(i):
            S = 128 * (i + 1)
            s_ = sc[i % NSC]
            for c in range((S + 511) // 512):
                wc = min(512, S - 512 * c)
                sl = slice(512 * c, 512 * c + wc)
                for h in range(8):
                    ps = P.ps(nextbank(0, 4))[:, 0:wc]
                    k.mm(ps, qiT[:, h // 4, 128 * i:128 * i + 128], kiT4[:, h % 4, sl], True, True)
                    r_ = rt[nrt[0] % 4][:, 0:wc]
                    nrt[0] += 1
                    k.act(r_, ps, AF.Relu, scale=absw[:, i, h:h + 1])
                    if h == 0:
                        k.ts("dve", s_[:, sl], r_, sgnw[:, i, h:h + 1], ALU.mult)
                    else:
                        k.stt(s_[:, sl], r_, sgnw[:, i, h:h + 1], s_[:, sl], ALU.mult, ALU.add)
                    yield
            k.tt("pool", s_[:, 128 * i:128 * i + 128], s_[:, 128 * i:128 * i + 128], causal, ALU.add)
            yield

        def g_bisect(i):
            S = 128 * (i + 1)
            s_ = sc[i % NSC][:, 0:S]
            m = scal[:, i, 0:1]
            nm = scal[:, i, 1:2]
            c_ = scal[:, i, 2:3]
            a_ = scal[:, i, 3:4]
            eng = "dve" if i in (3, 5, 8, 10, 13, 15) else "act"
            if i < 2:
                k.memset("dve", m, -64.0)
                yield
            elif eng == "dve":
                k.memset("dve", m, 0.0)
                w = W0
                for it in range(NIT):
                    k.ts("dve", junk["dve"][:, 0:S], s_, m, ALU.is_ge, 0.0, ALU.add, accum_out=c_)
                    k.ts("dve", a_, c_, TOPK - 0.5, ALU.is_ge, w / 2, ALU.mult)
                    k.stt(m, a_, -w / 4, m, ALU.add, ALU.add)
                    w = w / 2
                    yield
                k.ts("dve", m, m, -w / 2, ALU.add)
            else:
                k.memset("dve", nm, 0.0)
                w = W0
                for it in range(NIT):
                    k.act(junk["act"][:, 0:S], s_, AF.Sign, bias=nm, accum_out=c_)
                    k.ts("dve", a_, c_, float(2 * TOPK - 1 - S), ALU.is_ge, -w / 2, ALU.mult)
                    k.stt(nm, a_, w / 4, nm, ALU.add, ALU.add)
                    w = w / 2
                    yield
                k.ts("dve", m, nm, -1.0, ALU.mult, -w / 2, ALU.add)
            k.ts("dve", mbk[i % NSC][:, 0:S], s_, m, ALU.is_lt, MASKNEG, ALU.mult)
            yield

        def g_attend(i):
            mb_ = mbk[i % NSC]
            po = [P.ps(6, (4, 65)), P.ps(7, (4, 65))]
            pend = [None]

            def av_step(j, par, e):
                for q_ in range(4):
                    h = par + 2 * q_
                    g = h // 4
                    k.mm(po[g][:, h % 4, :], e[:, q_, :], vaug[:, j, g, :], (j == 0 and par == 0 and h in (0, 4)), j == i, skip=True)
            for j in range(i + 1):
                for par in range(2):
                    lg = P.ps(4 + par, (4, 128))
                    k.mm(lg.rearrange("p r t -> p (r t)"), mb_[:, 128 * j:128 * j + 128], ident4.rearrange("p r t -> p (r t)"),
                         True, False, skip=True)
                    ro = 64 * par
                    for q_ in range(4):
                        h = par + 2 * q_
                        g = h // 4
                        k.mm(lg[:, q_, :], kT[ro:ro + 64, g, 128 * j:128 * j + 128], qT[ro:ro + 64, h // 2, 128 * i:128 * i + 128],
                             False, q_ == 3, skip=True)
                    e = eg[neg[0] % 4]
                    neg[0] += 1
                    k.act(e, lg, AF.Exp, scale=0.125)
                    if pend[0] is not None:
                        av_step(*pend[0])
                    pend[0] = (j, par, e)
                    yield
            av_step(*pend[0])
            yield

        def normalize(i):
            po = [P.ps(6, (4, 65)), P.ps(7, (4, 65))]
            o = oat[i % 2]
            for g in range(2):
                r_ = rda[:, 4 * ((2 * i + g) % 4):4 * ((2 * i + g) % 4) + 4]
                P.op("dve", lambda e, r_=r_, pg=po[g]: e.reciprocal(out=r_, in_=pg[:, :, 64]), [po[g]], [r_])
                k.tt("dve", o[:, 256 * g:256 * g + 256].rearrange("p (h d) -> p h d", h=4), po[g][:, :, 0:64],
                     r_.unsqueeze(2).broadcast_to([128, 4, 64]), ALU.mult)
            for t in range(4):
                pt = P.ps(nextbank(0, 4), (128,), BF16)
                k.tr(pt, o[:, 128 * t:128 * t + 128], identb)
                k.cp("act", XC[:, t, 128 * i:128 * i + 128], pt)

        def nsteps_bisect(i):
            return 1 if i < 2 else NIT + 1

        def run_interleaved(tasks):
            order = []
            for ti, (g_, n) in enumerate(tasks):
                for q_ in range(n):
                    order.append(((q_ + 0.5) / n, ti))
            order.sort()
            for _, ti in order:
                try:
                    next(tasks[ti][0])
                except StopIteration:
                    pass

        def drain(g_):
            for _ in g_:
                pass

        def nsteps_scores(i):
            return sum(8 for _ in range((128 * (i + 1) + 511) // 512)) + 1
        bis = {}
        for i in range(3):
            drain(g_scores(i))
        for n in (0, 1):
            bis[n] = g_bisect(n)
        drain(bis[0])
        half = lambda n: (nsteps_bisect(n) + 1) // 2
        run_interleaved([(bis[1], half(1))])
        for i in range(NB):
            tasks = []
            if i + 3 < NB:
                tasks.append((g_scores(i + 3), nsteps_scores(i + 3)))
            if i + 2 < NB:
                bis[i + 2] = g_bisect(i + 2)
                tasks.append((bis[i + 2], half(i + 2)))
            if i + 1 < NB:
                tasks.append((bis[i + 1], nsteps_bisect(i + 1) - half(i + 1) + 1))
            tasks.append((g_attend(i), 2 * (i + 1) + 1))
            run_interleaved(tasks)
            if i + 1 < NB:
                drain(bis[i + 1])
            normalize(i)
        if debug and b == 0:
            dump("catT", XC, [128, 8, L], BF16)
        if stop <= 5:
            continue

        A = Arena()
        gl = [A.sb([L], BF16) for _ in range(2)]
        xt = [A.sb([1024], F32) for _ in range(2)]
        hb = [A.sb([1024], F32) for _ in range(2)]
        ob = [A.sb([1024], F32) for _ in range(2)]
        st6 = A.sb([2, 32], F32)
        mv2 = A.sb([2, 32], F32)
        rs = A.sb([2, 32], F32)
        for et in range(2):
            P.dma(XC[:, 4 + et, :], ssc[b, et], s_ssl, after=[(s_ss[0], 16 * P.dma_cnt[s_ss[0]]), (s_ss[1], 16 * P.dma_cnt[s_ss[1]])])
        for kk in range(8):
            P.dma(gl[kk % 2], gsc[b, kk], s_gl[kk % 2], after=[(s_g[0], 16 * P.dma_cnt[s_g[0]])])
            k.tt("pool" if kk % 2 else "dve", XC[:, kk, :], XC[:, kk, :], gl[kk % 2], ALU.mult)
        P.dma(xt[0], xtm[b, 0:128, :], s_xt[0])
        for i in range(NB):
            if i + 1 < NB:
                P.dma(xt[(i + 1) % 2], xtm[b, 128 * (i + 1):128 * (i + 1) + 128, :], s_xt[(i + 1) % 2])
            h_ = hb[i % 2]
            for hf in range(2):
                ps = P.ps(nextbank(0, 8))
                for kk in range(8):
                    k.mm(ps, XC[:, kk, 128 * i:128 * i + 128], woutb[:, kk, 512 * hf:512 * hf + 512], kk == 0, kk == 7)
                k.stt(h_[:, 512 * hf:512 * hf + 512], xt[i % 2][:, 512 * hf:512 * hf + 512], DN_ALPHA, ps, ALU.mult, ALU.add)
                P.op("dve", lambda e, o_=st6[:, i % 2, 6 * hf:6 * hf + 6], a_=h_[:, 512 * hf:512 * hf + 512]: e.bn_stats(out=o_, in_=a_),
                     [h_[:, 512 * hf:512 * hf + 512]], [st6[:, i % 2, 6 * hf:6 * hf + 6]])
            mv = mv2[:, i % 2, 0:2]
            P.op("dve", lambda e, o_=mv, a_=st6[:, i % 2, 0:12]: e.bn_aggr(out=o_, in_=a_),
                 [st6[:, i % 2, 0:12]], [mv])
            r4 = rs[:, i % 2, 0:4]
            k.ts("dve", r4[:, 0:1], mv[:, 1:2], LN_EPS, ALU.add)
            k.act(r4[:, 1:2], r4[:, 0:1], AF.Sqrt)
            P.op("dve", lambda e, o_=r4[:, 2:3], a_=r4[:, 1:2]: e.reciprocal(out=o_, in_=a_), [r4[:, 1:2]], [r4[:, 2:3]])
            k.ts("dve", r4[:, 3:4], mv[:, 0:1], -1.0, ALU.mult, r4[:, 2:3], ALU.mult)
            o_ = ob[i % 2]
            k.act(o_, h_, AF.Identity, bias=r4[:, 3:4], scale=r4[:, 2:3])
            k.tt("dve", o_, o_, lng, ALU.mult)
            k.tt("pool", o_, o_, lnb, ALU.add)
            P.dma(out[b, 128 * i:128 * i + 128, :], o_, s_out[i % 2], is_output=True, eng="pool")

    P.emit()
    return nc, dbg


_CACHE = {}


def kernel(x, mem, w_in, w_mem_kv, lam_re, lam_im, log_dt, b_re, b_im, c_re, c_im, d_skip,
           w_glu, b_glu, w_out, ln_g, ln_b):
    x = np.asarray(x, np.float32)
    mem = np.asarray(mem, np.float32)
    f = lambda a: np.asarray(a, np.float32)
    hw = host_weights(f(w_in), f(w_mem_kv), f(lam_re), f(lam_im), f(log_dt), f(b_re), f(b_im), f(c_re), f(c_im),
                      f(d_skip), f(w_glu), f(b_glu), f(w_out), f(ln_g), f(ln_b))
    hc = host_consts()
    if "nc" not in _CACHE:
        _CACHE["nc"] = build()[0]
    nc = _CACHE["nc"]
    in_maps = []
    for c in range(8):
        xb = x[BPC * c:BPC * c + BPC]
        m = dict(hw)
        m.update(hc)
        m["x"] = np.ascontiguousarray(xb)
        m["xT"] = np.ascontiguousarray(xb.transpose(0, 2, 1))
        m["memT"] = np.ascontiguousarray(mem[BPC * c:BPC * c + BPC].transpose(0, 2, 1))
        in_maps.append(m)
    res = run_bass_kernel_spmd(nc, in_maps, core_ids=list(range(8)))
    return np.concatenate([r["out"] for r in res.results], axis=0).astype(np.float32)
```

```python
import contextlib
import math
import numpy as np
import concourse.bass as bass
import concourse.mybir as mybir
from concourse.bass_utils import run_bass_kernel_spmd

DT = mybir.dt
F32, BF16, I32 = DT.float32, DT.bfloat16, DT.int32
ALU = mybir.AluOpType
AF = mybir.ActivationFunctionType
ESIZE = {F32: 4, BF16: 2, I32: 4}

ENGS = ["pe", "act", "dve", "pool", "sp"]
SB_BYTES = 206 * 1024
PS_BYTES = 16 * 1024
SLOT = 128
NDMA = 60

L = 2048
NB = 16
BPC = 2
TOPK = 256
NBIS = 21
DN_ALPHA = 2.0 ** 0.25
LN_EPS = 1e-5
BIG = 1.0e30
MASKNEG = -30000.0
PI = math.pi


class Prog:
    def __init__(self, nc):
        self.nc = nc
        self.ops = {e: [] for e in ENGS}
        self.kinds = ENGS + ["d%d" % i for i in range(NDMA)]
        self.kidx = {k: i for i, k in enumerate(self.kinds)}
        nk = len(self.kinds)
        self.nslots = {"sb": SB_BYTES // SLOT, "ps": PS_BYTES // SLOT}
        self.wk = {s: np.full(n, -1, np.int64) for s, n in self.nslots.items()}
        self.wv = {s: np.zeros(n, np.int64) for s, n in self.nslots.items()}
        self.rv = {s: np.zeros((nk, n), np.int64) for s, n in self.nslots.items()}
        self.seen = {e: np.zeros(nk, np.int64) for e in ENGS}
        self.dma_cnt = [0] * NDMA
        self.arena = nc.alloc_sbuf_tensor("arena", [128, SB_BYTES // 4], F32)
        self.psum = nc.alloc_psum_tensor("psum_all", [128, PS_BYTES // 4], F32)
        self.sb_off = 0
        self.out_dma = []
        self.nsem = 0
        self._dummy = self.sb([8], F32)
        self._dummy_act = self.sb([8], F32)

    def newsem(self):
        self.nsem += 1
        assert self.nsem <= NDMA
        return self.nsem - 1

    def sb(self, shape, dtype):
        n = int(np.prod(shape)) * ESIZE[dtype]
        n = (n + SLOT - 1) // SLOT * SLOT
        off = self.sb_off
        self.sb_off += n
        assert self.sb_off <= SB_BYTES, "SBUF arena overflow %d" % self.sb_off
        return self.sb_at(off, shape, dtype)

    def sb_at(self, off, shape, dtype):
        assert off % 4 == 0
        nel = int(np.prod(shape))
        nb = nel * ESIZE[dtype]
        assert off + nb <= SB_BYTES, "SBUF arena overflow (at) %d" % (off + nb)
        v = self.arena[:, off // 4:(off + nb + 3) // 4]
        if dtype != F32:
            v = v.bitcast(dtype)[:, 0:nel]
        if len(shape) > 1:
            names = " ".join("a%d" % i for i in range(len(shape)))
            kw = {"a%d" % i: int(s) for i, s in enumerate(shape)}
            v = v.rearrange("p (%s) -> p %s" % (names, names), **kw)
        return v

    def ps(self, bank, shape=(512,), dtype=F32, off=0):
        nel = int(np.prod(shape))
        nb = nel * ESIZE[dtype]
        b0 = bank * 2048 + off
        assert off + nb <= 2048
        v = self.psum[:, b0 // 4:(b0 + nb + 3) // 4]
        if dtype != F32:
            v = v.bitcast(dtype)[:, 0:nel]
        if len(shape) > 1:
            names = " ".join("a%d" % i for i in range(len(shape)))
            kw = {"a%d" % i: int(s) for i, s in enumerate(shape)}
            v = v.rearrange("p (%s) -> p %s" % (names, names), **kw)
        return v

    def _range(self, ap):
        sp = str(ap.space).lower()
        if "sb" in sp or "state" in sp:
            space, pitch = "sb", SB_BYTES
        elif "psum" in sp:
            space, pitch = "ps", PS_BYTES
        else:
            return None
        es = ESIZE[ap.dtype]
        off = (ap.offset * es) % pitch
        ext = 1
        for st, cnt in list(ap.ap)[1:]:
            ext += (cnt - 1) * abs(st)
        lo = off // SLOT
        hi = (off + ext * es - 1) // SLOT + 1
        if space == "ps":
            per = 2048 // SLOT
            lo = lo // per * per
            hi = (hi + per - 1) // per * per
        return space, lo, hi

    def _deps(self, eng, reads, writes):
        need = {}
        myk = self.kidx[eng]
        myseq = len(self.ops[eng]) + 1

        def add(k, v):
            if k < 0 or v <= 0:
                return
            if k == myk:
                if eng == "pe" or v < myseq - 1:
                    return
            if need.get(k, 0) < v:
                need[k] = v
        for ap in reads:
            r = self._range(ap)
            if r is None:
                continue
            s, lo, hi = r
            wk, wv = self.wk[s][lo:hi], self.wv[s][lo:hi]
            for k in np.unique(wk):
                if k >= 0:
                    add(int(k), int(wv[wk == k].max()))
            if s == "ps":
                rm = self.rv[s][:, lo:hi].max(axis=1)
                for k in np.nonzero(rm)[0]:
                    if int(k) != myk:
                        add(int(k), int(rm[k]))
        for ap in writes:
            r = self._range(ap)
            if r is None:
                continue
            s, lo, hi = r
            wk, wv = self.wk[s][lo:hi], self.wv[s][lo:hi]
            for k in np.unique(wk):
                if k >= 0:
                    add(int(k), int(wv[wk == k].max()))
            rm = self.rv[s][:, lo:hi].max(axis=1)
            for k in np.nonzero(rm)[0]:
                add(int(k), int(rm[k]))
        waits = {}
        seen = self.seen[eng]
        for k, v in need.items():
            if k == myk or seen[k] < v:
                waits[k] = v
                if k != myk:
                    seen[k] = v
        return waits

    def _mark(self, kind, val, reads, writes):
        for ap in reads:
            r = self._range(ap)
            if r is None:
                continue
            s, lo, hi = r
            self.rv[s][kind, lo:hi] = val
        for ap in writes:
            r = self._range(ap)
            if r is None:
                continue
            s, lo, hi = r
            self.wk[s][lo:hi] = kind
            self.wv[s][lo:hi] = val
            self.rv[s][:, lo:hi] = 0

    def op(self, eng, fn, reads=(), writes=(), accum=False):
        waits = self._deps(eng, reads, writes)
        self.ops[eng].append((fn, waits, ("accum", eng) if accum else None))
        self._mark(self.kidx[eng], len(self.ops[eng]), reads, writes)

    def dma(self, out, in_, sem, eng="sp", is_output=False, after=(), **kw):
        waits = self._deps(eng, [in_], [out])
        for s_, v_ in after:
            kk_ = self.kidx["d%d" % s_]
            if waits.get(kk_, 0) < v_:
                waits[kk_] = v_
        self.dma_cnt[sem] += 1
        val = 16 * self.dma_cnt[sem]
        fn = lambda e, out=out, in_=in_, kw=kw: e.dma_start(out=out, in_=in_, **kw)
        self.ops[eng].append((fn, waits, sem))
        self._mark(self.kidx["d%d" % sem], val, [in_], [out])
        if is_output:
            self.out_dma.append(sem)

    def emit(self):
        nc = self.nc
        waited = {e: set() for e in ENGS}
        selfw = {e: set() for e in ENGS}
        for e in ENGS:
            for fn, waits, dsem in self.ops[e]:
                for k, v in waits.items():
                    if k < len(ENGS):
                        waited[ENGS[k]].add(v)
                        if ENGS[k] == e:
                            selfw[e].add(v)
        rank = {e: {s: i + 1 for i, s in enumerate(sorted(waited[e]))} for e in ENGS}
        print("ops", {e: len(self.ops[e]) for e in ENGS}, "sem max", {e: len(rank[e]) for e in ENGS}, "dma", max(self.dma_cnt) * 16)
        with contextlib.ExitStack() as st:
            sems = {e: st.enter_context(nc.semaphore("s_" + e)) for e in ENGS}
            dsems = [st.enter_context(nc.semaphore("sd%d" % i)) for i in range(max(self.nsem, 1))]
            block = st.enter_context(nc.Block())

            def run(engname):
                def body(engine):
                    seq = 0
                    for fn, waits, dsem in self.ops[engname]:
                        seq += 1
                        for k, v in sorted(waits.items()):
                            if k < len(ENGS):
                                engine.wait_ge(sems[ENGS[k]], rank[ENGS[k]][v])
                            else:
                                engine.wait_ge(dsems[k - len(ENGS)], v)
                        ins = fn(engine)
                        if isinstance(dsem, tuple):
                            if seq in rank[engname] and seq not in selfw[engname]:
                                ins.then_inc(sems[engname], 1)
                            elif seq in rank[engname]:
                                if engname == "dve":
                                    ins = engine.memset(self._dummy[:, 0:1], 0.0)
                                else:
                                    ins = engine.activation(out=self._dummy_act[:, 0:1], in_=self._dummy_act[:, 1:2], func=AF.Copy)
                                ins.then_inc(sems[engname], 1)
                        elif dsem is not None:
                            ins.then_inc(dsems[dsem], 16)
                        elif seq in rank[engname]:
                            ins.then_inc(sems[engname], 1)
                    if engname == "sp":
                        for s in sorted(set(self.out_dma)):
                            engine.wait_ge(dsems[s], 16 * self.dma_cnt[s])
                return body
            block.tensor(run("pe"))
            block.scalar(run("act"))
            block.vector(run("dve"))
            block.gpsimd(run("pool"))
            block.sync(run("sp"))


def _aps(*xs):
    return [x for x in xs if not isinstance(x, (int, float)) and x is not None]


class K:
    def __init__(self, P):
        self.P = P

    def tt(self, eng, out, a, b, op):
        self.P.op(eng, lambda e: e.tensor_tensor(out=out, in0=a, in1=b, op=op), [a, b], [out])

    def ts(self, eng, out, a, s1, op0, s2=None, op1=None, accum_out=None):
        def fn(e):
            if op1 is None:
                return e.tensor_scalar(out=out, in0=a, scalar1=s1, scalar2=None, op0=op0)
            if accum_out is not None:
                return e.tensor_scalar(out=out, in0=a, scalar1=s1, scalar2=s2, op0=op0, op1=op1, accum_out=accum_out)
            return e.tensor_scalar(out=out, in0=a, scalar1=s1, scalar2=s2, op0=op0, op1=op1)
        self.P.op(eng, fn, [a] + _aps(s1, s2), [out] + _aps(accum_out), accum=accum_out is not None)

    def stt(self, out, a, s, b, op0, op1):
        self.P.op("dve", lambda e: e.scalar_tensor_tensor(out=out, in0=a, scalar=s, in1=b, op0=op0, op1=op1),
                  [a, b] + _aps(s), [out])

    def act(self, out, a, func, bias=None, scale=1.0, accum_out=None):
        def fn(e):
            kw = {}
            if bias is not None:
                kw["bias"] = bias
            if accum_out is not None:
                kw["accum_out"] = accum_out
            return e.activation(out=out, in_=a, func=func, scale=scale, **kw)
        self.P.op("act", fn, [a] + _aps(bias, scale), [out] + _aps(accum_out), accum=accum_out is not None)

    def cp(self, eng, out, a):
        if eng == "act":
            self.P.op("act", lambda e: e.activation(out=out, in_=a, func=AF.Copy), [a], [out])
        else:
            self.P.op(eng, lambda e: e.tensor_copy(out=out, in_=a), [a], [out])

    def memset(self, eng, out, val):
        self.P.op(eng, lambda e: e.memset(out, val), [], [out])

    def mm(self, out, lhsT, rhs, start, stop, skip=False):
        self.P.op("pe", lambda e: e.matmul(out, lhsT=lhsT, rhs=rhs, start=start, stop=stop, skip_group_check=skip),
                  [lhsT, rhs], [out])

    def tr(self, out, a, ident):
        self.P.op("pe", lambda e: e.transpose(out, a, ident), [a, ident], [out])


SPLITS = [512, 128, 128, 256, 32, 8, 512, 256, 256, 256, 256]
NFM = 21


def _swap_perm(width, hd, half):
    perm = np.arange(width)
    for c in range(width):
        d = c % hd
        if d < half:
            perm[c] = c + half
        elif d < 2 * half:
            perm[c] = c - half
    return perm


def _rope_tables(hd, reps):
    r = hd // 4
    half = r // 2
    inv = (np.float32(500000.0) ** (-np.arange(0, half, dtype=np.float32) * np.float32(2.0) / np.float32(r))).astype(np.float32)
    pos = np.arange(L, dtype=np.float32)
    ang = (pos[:, None] * inv[None, :]).astype(np.float32)
    cos = np.cos(ang).astype(np.float32).T
    sin = np.sin(ang).astype(np.float32).T
    C = np.ones((hd, L), np.float32)
    S = np.zeros((hd, L), np.float32)
    C[0:half] = cos
    C[half:2 * half] = cos
    S[0:half] = -sin
    S[half:2 * half] = sin
    return np.tile(C, (reps, 1)), np.tile(S, (reps, 1))


def host_consts():
    c64, s64 = _rope_tables(64, 2)
    c32, s32 = _rope_tables(32, 4)
    ident = np.eye(128, dtype=np.float32)
    causal = np.where(np.arange(128)[None, :] <= np.arange(128)[:, None], 0.0, -BIG).astype(np.float32)
    chix = np.ascontiguousarray(np.repeat(np.arange(256, dtype=np.float32)[None, :], 128, axis=0))
    pm64 = np.zeros((128, 128), np.float32)
    pm64[_swap_perm(128, 64, 8), np.arange(128)] = 1.0
    pm32 = np.zeros((128, 128), np.float32)
    pm32[_swap_perm(128, 32, 4), np.arange(128)] = 1.0
    vmask = (np.arange(128)[:, None] // 32 == np.arange(4)[None, :]).astype(np.float32)
    return dict(c64=c64, s64=s64, c32=c32, s32=s32, ident=ident, causal=causal, chix=chix, pm64=pm64, pm32=pm32, vmask=vmask)


def host_weights(w_in, w_mem_kv, lam_re, lam_im, log_dt, b_re, b_im, c_re, c_im, d_skip, w_glu, b_glu, w_out, ln_g, ln_b):
    sp = np.concatenate([[0], np.cumsum(SPLITS)])
    col = lambda i: w_in[:, sp[i]:sp[i + 1]]
    q_c, k_c, v_c, qi_c, ki_c, wi_c, gatt_c, u_c, gssm_c, qm_c, gmem_c = [col(i) for i in range(11)]
    p64_512 = _swap_perm(512, 64, 8)
    p64_128 = _swap_perm(128, 64, 8)
    p32_256 = _swap_perm(256, 32, 4)
    p32_32 = _swap_perm(32, 32, 4)
    tiles = []
    for t in range(2):
        tiles += [qi_c[:, 128 * t:128 * t + 128]]
    tiles += [np.concatenate([ki_c] * 4, axis=1)]
    for t in range(4):
        tiles += [q_c[:, 128 * t:128 * t + 128]]
    for g in range(2):
        tiles += [np.concatenate([k_c[:, 64 * g:64 * g + 64]] * 2, axis=1)]
    tiles += [u_c[:, 0:128], u_c[:, 128:256], qm_c[:, 0:128], qm_c[:, 128:256]]
    gates = np.concatenate([gatt_c, gssm_c, gmem_c], axis=1)
    for t in range(8):
        tiles.append(gates[:, 128 * t:128 * t + 128])
    assert len(tiles) == NFM
    wfm = np.ascontiguousarray(np.stack(tiles, 0)).astype(np.float32)
    wtm = np.ascontiguousarray(np.concatenate([v_c, wi_c], axis=1)).astype(np.float32)
    bre = np.zeros((8, 128, 128), np.float32)
    bim = np.zeros((8, 128, 128), np.float32)
    cre = np.zeros((8, 128, 128), np.float32)
    cim = np.zeros((8, 128, 128), np.float32)
    for i in range(8):
        for gl in range(2):
            g = 2 * i + gl
            g8 = g % 8
            bre[i, 16 * g8:16 * g8 + 16, 64 * gl:64 * gl + 64] = b_re[g].T
            bim[i, 16 * g8:16 * g8 + 16, 64 * gl:64 * gl + 64] = b_im[g].T
            cre[i, 64 * gl:64 * gl + 64, 16 * g8:16 * g8 + 16] = c_re[g].T
            cim[i, 64 * gl:64 * gl + 64, 16 * g8:16 * g8 + 16] = c_im[g].T
    dblk = np.zeros((2, 128, 128), np.float32)
    dflat = d_skip.reshape(256)
    for ct in range(2):
        dblk[ct][np.arange(128), np.arange(128)] = dflat[128 * ct:128 * ct + 128]

    def st(a):
        return np.ascontiguousarray(a.reshape(8, 2, 64).transpose(1, 2, 0).reshape(128, 8)).astype(np.float32)
    lamre = st(lam_re)
    lamim = st(lam_im)
    logdt = st(np.repeat(log_dt[:, None], 64, axis=1))
    bglu = np.ascontiguousarray(b_glu.reshape(2, 128).T).astype(np.float32)
    lng = np.ascontiguousarray(np.repeat(ln_g[None, :], 128, axis=0)).astype(np.float32)
    lnb = np.ascontiguousarray(np.repeat(ln_b[None, :], 128, axis=0)).astype(np.float32)
    bre = np.ascontiguousarray(bre.transpose(0, 2, 1))
    bim = np.ascontiguousarray(bim.transpose(0, 2, 1))
    return dict(wfm=wfm, wtm=wtm, wmem=np.ascontiguousarray(w_mem_kv), wglu=np.ascontiguousarray(w_glu),
                wout=np.ascontiguousarray(w_out), bre=bre, bim=bim, cre=cre, cim=cim, dblk=dblk,
                lamre=lamre, lamim=lamim, logdt=logdt, bglu=bglu, lng=lng, lnb=lnb)


def build(stop=99, debug=False):
    nc = bass.Bass("TRN2", target_bir_lowering=False)

    def din(name, shape):
        return nc.dram_tensor(name, list(shape), F32, kind="ExternalInput").ap()
    xT = din("xT", [BPC, 1024, L])
    xtm = din("x", [BPC, L, 1024])
    memT = din("memT", [BPC, 1024, 256])
    wfm = din("wfm", [NFM, 1024, 128])
    wtm = din("wtm", [1024, 136])
    wmem = din("wmem", [1024, 512])
    wglu = din("wglu", [256, 256])
    wout = din("wout", [1024, 1024])
    d_c64, d_s64, d_c32, d_s32 = [din(n, [128, L]) for n in ("c64", "s64", "c32", "s32")]
    d_bre, d_bim, d_cre, d_cim = [din(n, [8, 128, 128]) for n in ("bre", "bim", "cre", "cim")]
    d_dblk = din("dblk", [2, 128, 128])
    d_lamre, d_lamim, d_logdt = [din(n, [128, 8]) for n in ("lamre", "lamim", "logdt")]
    d_bglu = din("bglu", [128, 2])
    d_lng = din("lng", [128, 1024])
    d_lnb = din("lnb", [128, 1024])
    d_ident = din("ident", [128, 128])
    d_causal = din("causal", [128, 128])
    d_chix = din("chix", [128, 256])
    d_pm64 = din("pm64", [128, 128])
    d_pm32 = din("pm32", [128, 128])
    d_vmask = din("vmask", [128, 4])
    out = nc.dram_tensor("out", [BPC, L, 1024], F32, kind="ExternalOutput").ap()
    gsc = nc.dram_tensor("gsc", [BPC, 8, 128, L], BF16).ap()
    ssc = nc.dram_tensor("ssc", [BPC, 2, 128, L], BF16).ap()

    P = Prog(nc)
    k = K(P)
    dbg = {}

    def dump(name, src, shape, dtype=F32):
        if not debug:
            return
        t = nc.dram_tensor("dbg_" + name, list(shape), dtype, kind="ExternalOutput").ap()
        P.dma(t, src, P.newsem(), is_output=True)
        dbg[name] = (shape, dtype)

    identf = P.sb([128], F32)
    identb = P.sb([128], BF16)
    ident4 = P.sb([4, 128], BF16)
    causal = P.sb([128], F32)
    pmb = P.sb([2, 128], BF16)
    vmask = P.sb([4], F32)
    wglub = P.sb([2, 256], BF16)
    bglu = P.sb([2], F32)
    wtmb = P.sb([8, 136], BF16)
    sc8 = {n: P.sb([8], F32) for n in ("lre", "lim", "ldt", "dt", "a", "th", "rho", "t1", "t2", "sin", "cos", "lbr", "lbi",
                                       "nre", "nim", "den", "kre", "kim", "nkim", "u1", "u2", "r8")}
    sc8i = P.sb([8], I32)
    mure = P.sb([11, 8], F32)
    muim = P.sb([11, 8], F32)
    nmuim = P.sb([11, 8], F32)
    uT = P.sb([2, L], BF16)
    PB0 = P.sb_off
    qT = P.sb([4, L], BF16)
    kT = P.sb([2, L], BF16)
    qiT = P.sb([2, L], BF16)
    kiT4 = P.sb([4, L], BF16)
    vaug = P.sb([NB, 2, 65], BF16)
    absw = P.sb([NB, 8], F32)
    sgnw = P.sb([NB, 8], F32)
    qmT = P.sb([2, L], BF16)
    mkT = P.sb([2, 256], BF16)
    mvaug = P.sb([2, 4, 65], BF16)
    woutb = P.sb([8, 1024], BF16)
    lng = P.sb([1024], F32)
    lnb = P.sb([1024], F32)
    PB1 = P.sb_off
    XC = P.sb([8, L], BF16)
    small = P.sb([64], F32)
    ARENA0 = P.sb_off
    ARENA_SZ = SB_BYTES - ARENA0
    print("resident bytes", ARENA0, "arena", ARENA_SZ)

    class Arena:
        def __init__(self):
            self.off = ARENA0

        def sb(self, shape, dtype):
            n = int(np.prod(shape)) * ESIZE[dtype]
            n = (n + SLOT - 1) // SLOT * SLOT
            o = self.off
            self.off += n
            assert self.off <= SB_BYTES, "phase arena overflow %d" % (self.off - ARENA0)
            return P.sb_at(o, shape, dtype)

    A = Arena()
    stg = A.sb([8, 1024], F32)
    P.dma(identf, d_ident, P.newsem())
    P.dma(causal, d_causal, P.newsem())
    P.dma(bglu, d_bglu, P.newsem())
    for n, d in (("lre", d_lamre), ("lim", d_lamim), ("ldt", d_logdt)):
        P.dma(sc8[n], d, P.newsem())
    k.cp("dve", identb, identf)
    P.dma(vmask, d_vmask, P.newsem())
    for q_, dsrc in enumerate((d_pm32, d_pm64)):
        P.dma(stg[:, q_, 0:128], dsrc, P.newsem())
        k.cp("dve", pmb[:, q_, :], stg[:, q_, 0:128])
    for r in range(4):
        k.cp("dve", ident4[:, r, :], identf)
    s_stg = P.newsem()
    v = stg[:, 0, 0:512].rearrange("p (i c) -> p i c", i=2)
    P.dma(v, wglu.rearrange("(k p) c -> p k c", p=128), s_stg)
    k.cp("dve", wglub, v)
    v = stg[:, 0:2, :].rearrange("p a b -> p (a b)")[:, 0:8 * 136].rearrange("p (i c) -> p i c", i=8)
    P.dma(v, wtm.rearrange("(k p) c -> p k c", p=128), s_stg)
    k.cp("dve", wtmb, v)

    s = sc8
    k.act(s["dt"], s["ldt"], AF.Exp)
    k.tt("dve", s["a"], s["lre"], s["dt"], ALU.mult)
    k.tt("dve", s["th"], s["lim"], s["dt"], ALU.mult)
    k.act(s["rho"], s["a"], AF.Exp)

    def sin_of(dst, src, shift):
        k.ts("dve", s["t1"], src, shift, ALU.add, 1.0 / (2 * PI), ALU.mult)
        k.cp("dve", sc8i, s["t1"])
        k.cp("dve", s["t2"], sc8i)
        k.ts("dve", s["t1"], src, shift, ALU.add)
        k.stt(s["t1"], s["t2"], -2 * PI, s["t1"], ALU.mult, ALU.add)
        k.ts("dve", s["t1"], s["t1"], 3.1415925, ALU.min, -3.1415925, ALU.max)
        k.act(dst, s["t1"], AF.Sin)
    sin_of(s["sin"], s["th"], 0.0)
    sin_of(s["cos"], s["th"], PI / 2)
    k.tt("dve", s["lbr"], s["rho"], s["cos"], ALU.mult)
    k.tt("dve", s["lbi"], s["rho"], s["sin"], ALU.mult)
    k.ts("dve", s["t1"], s["lbr"], -1.0, ALU.add)
    k.tt("dve", s["nre"], s["t1"], s["lre"], ALU.mult)
    k.tt("dve", s["u1"], s["lbi"], s["lim"], ALU.mult)
    k.tt("dve", s["nre"], s["nre"], s["u1"], ALU.add)
    k.tt("dve", s["nim"], s["lbi"], s["lre"], ALU.mult)
    k.tt("dve", s["u1"], s["t1"], s["lim"], ALU.mult)
    k.tt("dve", s["nim"], s["nim"], s["u1"], ALU.subtract)
    k.tt("dve", s["den"], s["lre"], s["lre"], ALU.mult)
    k.tt("dve", s["u1"], s["lim"], s["lim"], ALU.mult)
    k.tt("dve", s["den"], s["den"], s["u1"], ALU.add)
    P.op("dve", lambda e: e.reciprocal(out=s["u2"], in_=s["den"]), [s["den"]], [s["u2"]])
    k.tt("dve", s["kre"], s["nre"], s["u2"], ALU.mult)
    k.tt("dve", s["kim"], s["nim"], s["u2"], ALU.mult)
    k.ts("dve", s["nkim"], s["kim"], -1.0, ALU.mult)
    k.cp("dve", mure[:, 0, :], s["lbr"])
    k.cp("dve", muim[:, 0, :], s["lbi"])
    for lv in range(1, 11):
        k.tt("dve", s["u1"], mure[:, lv - 1, :], mure[:, lv - 1, :], ALU.mult)
        k.tt("dve", s["u2"], muim[:, lv - 1, :], muim[:, lv - 1, :], ALU.mult)
        k.tt("dve", mure[:, lv, :], s["u1"], s["u2"], ALU.subtract)
        k.tt("dve", s["u1"], mure[:, lv - 1, :], muim[:, lv - 1, :], ALU.mult)
        k.ts("dve", muim[:, lv, :], s["u1"], 2.0, ALU.mult)
    k.ts("dve", nmuim, muim, -1.0, ALU.mult)
    if debug:
        dump("mure", mure, [128, 11, 8])
        dump("muim", muim, [128, 11, 8])
        dump("kre", s["kre"], [128, 8])
        dump("kim", s["kim"], [128, 8])

    s_x = P.newsem()
    s_x2 = P.newsem()
    s_tab = [P.newsem() for _ in range(2)]
    s_w = [P.newsem() for _ in range(2)]
    s_g = [P.newsem() for _ in range(2)]
    s_misc = P.newsem()
    s_misc2 = P.newsem()
    s_out = [P.newsem() for _ in range(2)]
    s_xt = [P.newsem() for _ in range(2)]
    s_gl = [P.newsem() for _ in range(2)]
    psrot = [0]

    def nextbank(lo=0, hi=8):
        b = lo + psrot[0] % (hi - lo)
        psrot[0] += 1
        return b


    T8 = 8
    NCH = L // T8
    s_ss = [P.newsem() for _ in range(2)]
    s_ssl = P.newsem()
    s_bt = [P.newsem() for _ in range(5)]
    s_xs = P.newsem()
    s_ws = [P.newsem() for _ in range(2)]
    W1 = P.sb_at(PB0, [8, T8, 2, 128], BF16)
    W2 = P.sb_at(PB0 + 32768, [8, T8, 2, 128], BF16)
    Kt = P.sb_at(PB0 + 65536, [2, T8, 128], BF16)
    assert PB0 + 65536 + 4096 <= PB1
    A = Arena()
    BTr = A.sb([8, 128], F32)
    BTi = A.sb([8, 128], F32)
    CTr = A.sb([8, 128], F32)
    CTi = A.sb([8, 128], F32)
    nCTi = A.sb([8, 128], F32)
    dbf = A.sb([2, 128], F32)
    bts = [[A.sb([128], F32) for _ in range(2)] for _ in range(2)]
    w2t = [A.sb([128], F32) for _ in range(2)]
    LKr, LKi, nLKr, nLKi = [A.sb([T8 + 1, 8], F32) for _ in range(4)]
    GKr, GKi, nGKi = [A.sb([T8, 8], F32) for _ in range(3)]
    for dsrc, dst, sm in ((d_bre, BTr, 0), (d_bim, BTi, 1), (d_cre, CTr, 2), (d_cim, CTi, 3)):
        P.dma(dst, dsrc.rearrange("i p c -> p i c"), s_bt[sm])
    P.dma(dbf, d_dblk.rearrange("i p c -> p i c"), s_bt[4])
    k.ts("pool", nCTi, CTi, -1.0, ALU.mult, 0.0, ALU.add)
    k.cp("dve", LKr[:, 1, :], s["lbr"])
    k.cp("dve", LKi[:, 1, :], s["lbi"])
    for kk in range(2, T8 + 1):
        k.tt("dve", s["u1"], LKr[:, kk - 1, :], s["lbr"], ALU.mult)
        k.tt("dve", s["u2"], LKi[:, kk - 1, :], s["lbi"], ALU.mult)
        k.tt("dve", LKr[:, kk, :], s["u1"], s["u2"], ALU.subtract)
        k.tt("dve", s["u1"], LKr[:, kk - 1, :], s["lbi"], ALU.mult)
        k.tt("dve", s["u2"], LKi[:, kk - 1, :], s["lbr"], ALU.mult)
        k.tt("dve", LKi[:, kk, :], s["u1"], s["u2"], ALU.add)
    k.ts("dve", nLKr[:, 1:, :], LKr[:, 1:, :], -1.0, ALU.mult)
    k.ts("dve", nLKi[:, 1:, :], LKi[:, 1:, :], -1.0, ALU.mult)
    k.cp("dve", GKr[:, 0, :], s["kre"])
    k.cp("dve", GKi[:, 0, :], s["kim"])
    for kk in range(1, T8):
        k.tt("dve", s["u1"], LKr[:, kk, :], s["kre"], ALU.mult)
        k.tt("dve", s["u2"], LKi[:, kk, :], s["kim"], ALU.mult)
        k.tt("dve", GKr[:, kk, :], s["u1"], s["u2"], ALU.subtract)
        k.tt("dve", s["u1"], LKr[:, kk, :], s["kim"], ALU.mult)
        k.tt("dve", s["u2"], LKi[:, kk, :], s["kre"], ALU.mult)
        k.tt("dve", GKi[:, kk, :], s["u1"], s["u2"], ALU.add)
    k.ts("dve", nGKi, GKi, -1.0, ALU.mult)
    COST = P.sb_at(PB0 + 65536 + 4096, [8, NCH], F32)
    SINT = P.sb_at(PB0 + 65536 + 4096 + 8192, [8, NCH], F32)
    assert PB0 + 65536 + 4096 + 16384 <= PB1
    r8 = s["r8"]
    ff = A.sb([8], F32)
    ffi = A.sb([8], I32)
    chix = A.sb([NCH], F32)
    Gt = A.sb([8, NCH], F32)
    Gi = A.sb([8, NCH], I32)
    Gf = A.sb([8, NCH], F32)
    P.dma(chix, d_chix, P.newsem())
    k.act(r8, s["a"], AF.Exp, scale=float(T8))
    k.ts("dve", ff, s["th"], float(T8) / (2 * PI), ALU.mult)
    k.cp("dve", ffi, ff)
    k.cp("dve", s["u1"], ffi)
    k.tt("dve", ff, ff, s["u1"], ALU.subtract)
    k.tt("dve", Gt, ff.unsqueeze(2).broadcast_to([128, 8, NCH]), chix.unsqueeze(1).broadcast_to([128, 8, NCH]), ALU.mult)
    for shift, dstT in ((0.0, SINT), (0.25, COST)):
        src = Gt
        if shift != 0.0:
            k.ts("dve", Gf, Gt, shift, ALU.add)
            src = Gf
        k.cp("dve", Gi, src)
        k.cp("pool", dstT, Gi)
        k.tt("dve", dstT, src, dstT, ALU.subtract)
        k.ts("dve", dstT, dstT, 2 * PI, ALU.mult, 3.1415925, ALU.min)
        k.ts("dve", dstT, dstT, -3.1415925, ALU.max)
        k.act(dstT, dstT, AF.Sin)
    nb_ = 0
    for kk in range(T8):
        for ct in range(2):
            kp = P.ps(ct, (128,))
            for i in range(4 * ct, 4 * ct + 4):
                br_, bi_ = bts[nb_ % 2]
                nb_ += 1
                gr = GKr[:, kk, i:i + 1]
                gi = GKi[:, kk, i:i + 1]
                ngi = nGKi[:, kk, i:i + 1]
                k.ts("dve", br_, BTr[:, i, :], gr, ALU.mult)
                k.stt(br_, BTi[:, i, :], ngi, br_, ALU.mult, ALU.add)
                k.ts("dve", bi_, BTr[:, i, :], gi, ALU.mult)
                k.stt(bi_, BTi[:, i, :], gr, bi_, ALU.mult, ALU.add)
                for x_, src in ((0, br_), (1, bi_)):
                    pt = P.ps(4 + (2 * nb_ + x_) % 4, (128,))
                    k.tr(pt, src, identf)
                    k.cp("act", W1[:, i, T8 - 1 - kk, x_, :], pt)
                k.mm(kp, br_, CTr[:, i, :], i == 4 * ct, False)
                k.mm(kp, bi_, nCTi[:, i, :], False, i == 4 * ct + 3)
            if kk == 0:
                k.tt("dve", Kt[:, ct, kk, :], kp, dbf[:, ct, :], ALU.add)
            else:
                k.cp("act", Kt[:, ct, kk, :], kp)
    for i in range(8):
        for j in range(T8):
            lr = LKr[:, j + 1, i:i + 1]
            nli = nLKi[:, j + 1, i:i + 1]
            nlr = nLKr[:, j + 1, i:i + 1]
            t_ = w2t[(i * T8 + j) % 2]
            k.ts("pool", t_, CTr[:, i, :], lr, ALU.mult, 0.0, ALU.add)
            k.stt(W2[:, i, j, 0, :], CTi[:, i, :], nli, t_, ALU.mult, ALU.add)
            t2_ = w2t[(i * T8 + j + 1) % 2]
            k.act(t2_, CTr[:, i, :], AF.Copy, scale=nli)
            k.stt(W2[:, i, j, 1, :], CTi[:, i, :], nlr, t2_, ALU.mult, ALU.add)

    import os
    PHS = int(os.environ.get("PHS_VARIANT", "99"))
    A = Arena()
    xstS = A.sb([L], F32)
    wstS = [A.sb([8, 128], F32) for _ in range(2)]
    wbfS = [A.sb([8, 128], BF16) for _ in range(2)]
    Lr = A.sb([8, NCH], F32)
    Li = A.sb([8, NCH], F32)
    Br = A.sb([8, NCH], F32)
    Bi = A.sb([8, NCH], F32)
    Spr = A.sb([8, NCH], BF16)
    Spi = A.sb([8, NCH], BF16)
    uTd = A.sb([2, T8, NCH], BF16)
    ygb = P.sb_at(ARENA0, [2, L], BF16)
    osb = [P.sb_at(ARENA0 + 8192 + 4096 * q_, [L], BF16) for q_ in range(2)]
    assert 8192 + 8192 <= 8192 + 2 * 4096 + 2 * 2048
    k.memset("pool", Spr[:, :, 0:1], 0.0)
    k.memset("pool", Spi[:, :, 0:1], 0.0)
    def xload_S(bb):
        for kk in range(8):
            P.dma(xstS, xT[bb, 128 * kk:128 * kk + 128, :], s_xs)
            k.cp("dve" if kk % 2 == 0 else "act", XC[:, kk, :], xstS)
    xload_S(0)
    for b in range(BPC if PHS > 10 else 0):
        for ct in range(2):
            P.dma(wstS[ct], wfm[9 + ct].rearrange("(k p) c -> p k c", p=128), s_ws[ct])
            k.cp("act", wbfS[ct], wstS[ct])
            for c in range(4):
                ps = P.ps(nextbank(0, 4))
                for kk in range(8):
                    k.mm(ps, wbfS[ct][:, kk, :], XC[:, kk, 512 * c:512 * c + 512], kk == 0, kk == 7)
                k.cp("act", uT[:, ct, 512 * c:512 * c + 512], ps)
        xload_S(b + 1 if b + 1 < BPC else 0)
        for ct in range(2):
            k.cp("pool", uTd[:, ct, :, :], uT[:, ct, :].rearrange("p (c j) -> p j c", j=T8))
        for i in range(8):
            ct = i // 4
            for x_, dstL in ((0, Lr), (1, Li)):
                q_ = 2 * i + x_
                psL = P.ps(4 + q_ % 4, (NCH,))
                for j in range(T8):
                    k.mm(psL, W1[:, i, j, x_, :], uTd[:, ct, j, :], j == 0, j == T8 - 1, skip=True)
                k.cp("act", dstL[:, i, :], psL)
        T1, T2 = Br, Bi
        k.tt("dve", T1, COST, Lr, ALU.mult)
        k.tt("dve", T2, SINT, Li, ALU.mult)
        k.tt("dve", T1, T1, T2, ALU.add)
        k.tt("dve", T2, SINT, Lr, ALU.mult)
        k.tt("dve", Li, COST, Li, ALU.mult)
        k.tt("dve", Li, Li, T2, ALU.subtract)
        for i in range(8):
            d0 = r8[:, i:i + 1].broadcast_to([128, NCH])
            P.op("dve", lambda e, o_=Lr[:, i, :], d0=d0, d1=T1[:, i, :]: e.tensor_tensor_scan(out=o_, data0=d0, data1=d1, initial=0.0, op0=ALU.mult, op1=ALU.add),
                 [T1[:, i, :], r8], [Lr[:, i, :]])
            P.op("dve", lambda e, o_=T2[:, i, :], d0=d0, d1=Li[:, i, :]: e.tensor_tensor_scan(out=o_, data0=d0, data1=d1, initial=0.0, op0=ALU.mult, op1=ALU.add),
                 [Li[:, i, :], r8], [T2[:, i, :]])
        k.tt("dve", T1, COST, Lr, ALU.mult)
        k.tt("dve", Li, SINT, T2, ALU.mult)
        k.tt("dve", T1, T1, Li, ALU.subtract)
        k.tt("dve", Li, SINT, Lr, ALU.mult)
        k.tt("dve", T2, COST, T2, ALU.mult)
        k.tt("dve", Li, Li, T2, ALU.add)
        for i in range(8):
            k.cp("act", Spr[:, i, 1:NCH], T1[:, i, 0:NCH - 1])
            k.cp("act", Spi[:, i, 1:NCH], Li[:, i, 0:NCH - 1])
        if PHS <= 20:
            continue
        y = P.sb_at(ARENA0 + 20480, [L], F32)
        t1 = P.sb_at(ARENA0 + 20480 + 8192, [L], F32)
        t2 = P.sb_at(ARENA0 + 20480 + 16384, [L], F32)
        sg = P.sb_at(ARENA0 + 20480 + 24576, [L], F32)
        for ct in range(2):
            for c4 in range(4):
                yp = P.ps(c4, (64, T8))
                uv = uT[:, ct, 512 * c4:512 * c4 + 512].rearrange("p (c j) -> p c j", j=T8)
                for tau in range(T8):
                    k.mm(yp[:, :, tau:T8], Kt[:, ct, tau, :], uv[:, :, 0:T8 - tau], tau == 0, False, skip=True)
                for i in range(4 * ct, 4 * ct + 4):
                    for j in range(T8):
                        k.mm(yp[:, :, j], W2[:, i, j, 0, :], Spr[:, i, 64 * c4:64 * c4 + 64], False, False, skip=True)
                        k.mm(yp[:, :, j], W2[:, i, j, 1, :], Spi[:, i, 64 * c4:64 * c4 + 64], False,
                             (i == 4 * ct + 3 and j == T8 - 1), skip=True)
            for c in range(4):
                sl = slice(512 * c, 512 * c + 512)
                k.cp("act", y[:, sl], P.ps(c))
            if debug and b == 0:
                dump("y%d" % ct, y, [128, L])
            k.tt("dve", t1, y, y, ALU.mult)
            k.ts("dve", t1, t1, 0.044715, ALU.mult, 1.0, ALU.add)
            k.tt("dve", t1, t1, y, ALU.mult)
            k.act(t2, t1, AF.Sigmoid, scale=2.0 * 0.7978845608028654)
            k.tt("dve", ygb[:, ct, :], y, t2, ALU.mult)
        for et in range(2):
            for c in range(4):
                sl = slice(512 * c, 512 * c + 512)
                ps = P.ps(nextbank(4, 8))
                for kc in range(2):
                    k.mm(ps, wglub[:, kc, 128 * et:128 * et + 128], ygb[:, kc, sl], kc == 0, kc == 1)
                k.act(sg[:, sl], ps, AF.Sigmoid, bias=bglu[:, et:et + 1])
                k.tt("dve", osb[et][:, sl], ygb[:, et, sl], sg[:, sl], ALU.mult)
            P.dma(ssc[b, et], osb[et], s_ss[et])
    A = Arena()
    stg = A.sb([8, 1024], F32)
    P.dma(lng, d_lng, P.newsem())
    P.dma(lnb, d_lnb, P.newsem())
    P.dma(stg, wout.rearrange("(k p) c -> p k c", p=128), s_stg)
    for kk in range(8):
        k.cp("dve" if kk % 2 == 0 else "act", woutb[:, kk, :], stg[:, kk, :])

    for b in range(BPC):
        A = Arena()
        xst = A.sb([L], F32)
        xst2 = A.sb([L], F32)
        wst = [A.sb([8, 128], F32) for _ in range(2)]
        wbf = [A.sb([8, 128], BF16) for _ in range(2)]
        tabC = A.sb([L], F32)
        tabS = A.sb([L], F32)
        tmpA = [A.sb([512], F32) for _ in range(3)]
        zbf = [A.sb([512], BF16) for _ in range(3)]
        kirT = A.sb([L], BF16)
        tmp2 = [A.sb([512], F32) for _ in range(2)]
        gst = [A.sb([L], BF16) for _ in range(1)]
        for kk in range(8 if b > 0 else 0):
            xs_ = xst if kk % 2 == 0 else xst2
            P.dma(xs_, xT[b, 128 * kk:128 * kk + 128, :], s_x if kk % 2 == 0 else s_x2)
            k.cp("dve" if kk % 2 == 0 else "act", XC[:, kk, :], xs_)

        widx = {}

        def load_w(m):
            widx[m] = len(widx) % 2
            P.dma(wst[widx[m]], wfm[m].rearrange("(k p) c -> p k c", p=128), s_w[widx[m]])
            k.cp("act", wbf[widx[m]], wst[widx[m]])

        def proj(m, c):
            bank = nextbank()
            ps = P.ps(bank)
            for kk in range(8):
                k.mm(ps, wbf[widx[m]][:, kk, :], XC[:, kk, 512 * c:512 * c + 512], kk == 0, kk == 7)
            return ps
        dests = {}
        for t in range(2):
            dests[t] = ("rope", qiT[:, t, :], 0)
        dests[2] = ("rope", kirT, 0)
        for t in range(4):
            dests[3 + t] = ("rope", qT[:, t, :], 1)
        for g in range(2):
            dests[7 + g] = ("rope", kT[:, g, :], 1)
        dests[11] = ("plain", qmT[:, 0, :], None)
        dests[12] = ("plain", qmT[:, 1, :], None)
        for t in range(8):
            dests[13 + t] = ("gate", t, None)
        order = [m for m in range(NFM) if m not in (9, 10)]
        nta = 0
        pend = [None]
        for oi, m in enumerate(order):
            if m == 0:
                P.dma(tabC, d_c32, s_tab[0])
                P.dma(tabS, d_s32, s_tab[1])
                load_w(0)
            if m == 3:
                if pend[0] is not None:
                    pend[0]()
                    pend[0] = None
                for v_ in range(4):
                    k.ts("dve", kiT4[:, v_, :], kirT, vmask[:, v_:v_ + 1], ALU.mult)
                P.dma(tabC, d_c64, s_tab[0])
                P.dma(tabS, d_s64, s_tab[1])
            if oi + 1 < len(order):
                load_w(order[oi + 1])
            kind, dst, pq = dests[m]
            if kind == "rope":
                for c in range(4):
                    sl = slice(512 * c, 512 * c + 512)
                    ps = proj(m, c)
                    if pend[0] is not None:
                        pend[0]()
                        pend[0] = None
                    ta = tmpA[nta % 3][:, 0:512]
                    zb = zbf[nta % 3]
                    nta += 1
                    k.cp("act", zb, ps)
                    k.tt("dve", ta, ps, tabC[:, sl], ALU.mult)

                    def fin(zb=zb, ta=ta, sl=sl, dst=dst, pq=pq, c=c):
                        ps2 = P.ps(nextbank())
                        k.mm(ps2, pmb[:, pq, :], zb, True, True)
                        t2 = tmp2[c % 2]
                        k.tt("dve", t2, ps2, tabS[:, sl], ALU.mult)
                        k.tt("pool", dst[:, sl], t2, ta, ALU.add)
                    pend[0] = fin
            elif kind == "plain":
                for c in range(4):
                    ps = proj(m, c)
                    if pend[0] is not None:
                        pend[0]()
                        pend[0] = None
                    k.cp("act", dst[:, 512 * c:512 * c + 512], ps)
            else:
                gt = gst[0]
                for c in range(4):
                    ps = proj(m, c)
                    k.act(gt[:, 512 * c:512 * c + 512], ps, AF.Silu)
                P.dma(gsc[b, dst], gt, s_g[0], eng="act")
        for i in range(NB):
            bank = nextbank()
            ps = P.ps(bank, (136,))
            for kk in range(8):
                k.mm(ps, XC[:, kk, 128 * i:128 * i + 128], wtmb[:, kk, :], kk == 0, kk == 7)
            k.cp("act", vaug[:, i, :, 0:64], ps[:, 0:128].rearrange("p (g d) -> p g d", g=2))
            k.act(sgnw[:, i, :], ps[:, 128:136], AF.Sign)
            k.tt("dve", absw[:, i, :], ps[:, 128:136], sgnw[:, i, :], ALU.mult)
            k.ts("dve", absw[:, i, :], absw[:, i, :], 1.0 / 16.0, ALU.mult)
        if b == 0:
            k.memset("pool", vaug[:, :, :, 64:65], 1.0)
            k.memset("pool", mvaug[:, :, :, 64:65], 1.0)
        if debug and b == 0:
            dump("qT", qT, [128, 4, L], BF16)
            dump("kT", kT, [128, 2, L], BF16)
            dump("qiT", qiT, [128, 2, L], BF16)
            dump("kiT4", kiT4, [128, 4, L], BF16)
            dump("vaug", vaug, [128, NB, 2, 65], BF16)
            dump("absw", absw, [128, NB, 8])
            dump("sgnw", sgnw, [128, NB, 8])
            dump("qmT", qmT, [128, 2, L], BF16)
        if stop <= 1:
            continue

        A = Arena()
        mst = A.sb([8, 256], F32)
        mbf = A.sb([8, 256], BF16)
        wms = A.sb([8, 512], F32)
        wmb = A.sb([8, 512], BF16)
        P.dma(mst, memT[b].rearrange("(k p) c -> p k c", p=128), s_misc)
        P.dma(wms, wmem.rearrange("(k p) c -> p k c", p=128), s_misc2)
        k.cp("dve", mbf, mst)
        k.cp("pool", wmb, wms)
        for t in range(2):
            ps = P.ps(nextbank(), (256,))
            for kk in range(8):
                k.mm(ps, wmb[:, kk, 128 * t:128 * t + 128], mbf[:, kk, :], kk == 0, kk == 7)
            k.cp("act", mkT[:, t, :], ps)
        for mb_ in range(2):
            ps = P.ps(nextbank(), (256,))
            for kk in range(8):
                k.mm(ps, mbf[:, kk, 128 * mb_:128 * mb_ + 128], wmb[:, kk, 256:512], kk == 0, kk == 7)
            k.cp("act", mvaug[:, mb_, :, 0:64], ps.rearrange("p (h d) -> p h d", h=4))
        if debug and b == 0:
            dump("mkT", mkT, [128, 2, 256], BF16)
            dump("mvaug", mvaug, [128, 2, 4, 65], BF16)

        A = Arena()
        em = [A.sb([2, 512], BF16) for _ in range(2)]
        otm = [A.sb([128], BF16) for _ in range(2)]
        rd = A.sb([16], F32)
        nrd = 0
        for hp in range(2):
            for c in range(4):
                e_h = []
                for hh in range(2):
                    h = 2 * hp + hh
                    e = em[hh]
                    for mb_ in range(2):
                        ps = P.ps(nextbank(0, 4))
                        k.mm(ps, mkT[64 * hh:64 * hh + 64, hp, 128 * mb_:128 * mb_ + 128],
                             qmT[64 * hh:64 * hh + 64, hp, 512 * c:512 * c + 512], True, True)
                        k.act(e[:, mb_, :], ps, AF.Exp, scale=0.125)
                    e_h.append(e)
                for tb in range(4):
                    i = 4 * c + tb
                    o = otm[i % 2]
                    po = P.ps(4 + (i % 2), (2, 65))
                    for hh in range(2):
                        h = 2 * hp + hh
                        for mb_ in range(2):
                            k.mm(po[:, hh, :], e_h[hh][:, mb_, 128 * tb:128 * tb + 128], mvaug[:, mb_, h, :],
                                 (hh == 0 and mb_ == 0), mb_ == 1, skip=True)
                    r_ = rd[:, 2 * (nrd % 8):2 * (nrd % 8) + 2]
                    nrd += 1
                    P.op("dve", lambda e, r_=r_, po=po: e.reciprocal(out=r_, in_=po[:, :, 64]), [po], [r_])
                    k.tt("dve", o.rearrange("p (h d) -> p h d", h=2), po[:, :, 0:64],
                         r_.unsqueeze(2).broadcast_to([128, 2, 64]), ALU.mult)
                    pt = P.ps(6 + (i % 2), (128,), BF16)
                    k.tr(pt, o, identb)
                    k.cp("act", XC[:, 6 + hp, 128 * i:128 * i + 128], pt)
        if stop <= 4:
            continue

        A = Arena()
        NSC = 3
        sc = [A.sb([L], F32) for _ in range(NSC)]
        rt = [A.sb([512], F32) for _ in range(4)]
        junk = {"dve": A.sb([L], BF16), "act": A.sb([L], BF16)}
        mbk = [A.sb([L], BF16) for _ in range(NSC)]
        eg = [A.sb([4, 128], BF16) for _ in range(4)]
        oat = [A.sb([512], BF16) for _ in range(2)]
        scal = A.sb([NB, 32], F32)
        rda = A.sb([16], F32)
        nrt = [0]
        neg = [0]
        W0 = 32.0
        NIT = NBIS - 2

        def g_scores(i):
            S = 128 * (i + 1)
            s_ = sc[i % NSC]
            for c in range((S + 511) // 512):
                wc = min(512, S - 512 * c)
                sl = slice(512 * c, 512 * c + wc)
                for h in range(8):
                    ps = P.ps(nextbank(0, 4))[:, 0:wc]
                    k.mm(ps, qiT[:, h // 4, 128 * i:128 * i + 128], kiT4[:, h % 4, sl], True, True)
                    r_ = rt[nrt[0] % 4][:, 0:wc]
                    nrt[0] += 1
                    k.act(r_, ps, AF.Relu, scale=absw[:, i, h:h + 1])
                    if h == 0:
                        k.ts("dve", s_[:, sl], r_, sgnw[:, i, h:h + 1], ALU.mult)
                    else:
                        k.stt(s_[:, sl], r_, sgnw[:, i, h:h + 1], s_[:, sl], ALU.mult, ALU.add)
                    yield
            k.tt("pool", s_[:, 128 * i:128 * i + 128], s_[:, 128 * i:128 * i + 128], causal, ALU.add)
            yield

        def g_bisect(i):
            S = 128 * (i + 1)
            s_ = sc[i % NSC][:, 0:S]
            m = scal[:, i, 0:1]
            nm = scal[:, i, 1:2]
            c_ = scal[:, i, 2:3]
            a_ = scal[:, i, 3:4]
            eng = "dve" if i in (3, 5, 8, 10, 13, 15) else "act"
            if i < 2:
                k.memset("dve", m, -64.0)
                yield
            elif eng == "dve":
                k.memset("dve", m, 0.0)
                w = W0
                for it in range(NIT):
                    k.ts("dve", junk["dve"][:, 0:S], s_, m, ALU.is_ge, 0.0, ALU.add, accum_out=c_)
                    k.ts("dve", a_, c_, TOPK - 0.5, ALU.is_ge, w / 2, ALU.mult)
                    k.stt(m, a_, -w / 4, m, ALU.add, ALU.add)
                    w = w / 2
                    yield
                k.ts("dve", m, m, -w / 2, ALU.add)
            else:
                k.memset("dve", nm, 0.0)
                w = W0
                for it in range(NIT):
                    k.act(junk["act"][:, 0:S], s_, AF.Sign, bias=nm, accum_out=c_)
                    k.ts("dve", a_, c_, float(2 * TOPK - 1 - S), ALU.is_ge, -w / 2, ALU.mult)
                    k.stt(nm, a_, w / 4, nm, ALU.add, ALU.add)
                    w = w / 2
                    yield
                k.ts("dve", m, nm, -1.0, ALU.mult, -w / 2, ALU.add)
            k.ts("dve", mbk[i % NSC][:, 0:S], s_, m, ALU.is_lt, MASKNEG, ALU.mult)
            yield

        def g_attend(i):
            mb_ = mbk[i % NSC]
            po = [P.ps(6, (4, 65)), P.ps(7, (4, 65))]
            pend = [None]

            def av_step(j, par, e):
                for q_ in range(4):
                    h = par + 2 * q_
                    g = h // 4
                    k.mm(po[g][:, h % 4, :], e[:, q_, :], vaug[:, j, g, :], (j == 0 and par == 0 and h in (0, 4)), j == i, skip=True)
            for j in range(i + 1):
                for par in range(2):
                    lg = P.ps(4 + par, (4, 128))
                    k.mm(lg.rearrange("p r t -> p (r t)"), mb_[:, 128 * j:128 * j + 128], ident4.rearrange("p r t -> p (r t)"),
                         True, False, skip=True)
                    ro = 64 * par
                    for q_ in range(4):
                        h = par + 2 * q_
                        g = h // 4
                        k.mm(lg[:, q_, :], kT[ro:ro + 64, g, 128 * j:128 * j + 128], qT[ro:ro + 64, h // 2, 128 * i:128 * i + 128],
                             False, q_ == 3, skip=True)
                    e = eg[neg[0] % 4]
                    neg[0] += 1
                    k.act(e, lg, AF.Exp, scale=0.125)
                    if pend[0] is not None:
                        av_step(*pend[0])
                    pend[0] = (j, par, e)
                    yield
            av_step(*pend[0])
            yield

        def normalize(i):
            po = [P.ps(6, (4, 65)), P.ps(7, (4, 65))]
            o = oat[i % 2]
            for g in range(2):
                r_ = rda[:, 4 * ((2 * i + g) % 4):4 * ((2 * i + g) % 4) + 4]
                P.op("dve", lambda e, r_=r_, pg=po[g]: e.reciprocal(out=r_, in_=pg[:, :, 64]), [po[g]], [r_])
                k.tt("dve", o[:, 256 * g:256 * g + 256].rearrange("p (h d) -> p h d", h=4), po[g][:, :, 0:64],
                     r_.unsqueeze(2).broadcast_to([128, 4, 64]), ALU.mult)
            for t in range(4):
                pt = P.ps(nextbank(0, 4), (128,), BF16)
                k.tr(pt, o[:, 128 * t:128 * t + 128], identb)
                k.cp("act", XC[:, t, 128 * i:128 * i + 128], pt)

        def nsteps_bisect(i):
            return 1 if i < 2 else NIT + 1

        def run_interleaved(tasks):
            order = []
            for ti, (g_, n) in enumerate(tasks):
                for q_ in range(n):
                    order.append(((q_ + 0.5) / n, ti))
            order.sort()
            for _, ti in order:
                try:
                    next(tasks[ti][0])
                except StopIteration:
                    pass

        def drain(g_):
            for _ in g_:
                pass

        def nsteps_scores(i):
            return sum(8 for _ in range((128 * (i + 1) + 511) // 512)) + 1
        bis = {}
        for i in range(3):
            drain(g_scores(i))
        for n in (0, 1):
            bis[n] = g_bisect(n)
        drain(bis[0])
        half = lambda n: (nsteps_bisect(n) + 1) // 2
        run_interleaved([(bis[1], half(1))])
        for i in range(NB):
            tasks = []
            if i + 3 < NB:
                tasks.append((g_scores(i + 3), nsteps_scores(i + 3)))
            if i + 2 < NB:
                bis[i + 2] = g_bisect(i + 2)
                tasks.append((bis[i + 2], half(i + 2)))
            if i + 1 < NB:
                tasks.append((bis[i + 1], nsteps_bisect(i + 1) - half(i + 1) + 1))
            tasks.append((g_attend(i), 2 * (i + 1) + 1))
            run_interleaved(tasks)
            if i + 1 < NB:
                drain(bis[i + 1])
            normalize(i)
        if debug and b == 0:
            dump("catT", XC, [128, 8, L], BF16)
        if stop <= 5:
            continue

        A = Arena()
        gl = [A.sb([L], BF16) for _ in range(2)]
        xt = [A.sb([1024], F32) for _ in range(2)]
        hb = [A.sb([1024], F32) for _ in range(2)]
        ob = [A.sb([1024], F32) for _ in range(2)]
        st6 = A.sb([2, 32], F32)
        mv2 = A.sb([2, 32], F32)
        rs = A.sb([2, 32], F32)
        for et in range(2):
            P.dma(XC[:, 4 + et, :], ssc[b, et], s_ssl, after=[(s_ss[0], 16 * P.dma_cnt[s_ss[0]]), (s_ss[1], 16 * P.dma_cnt[s_ss[1]])])
        for kk in range(8):
            P.dma(gl[kk % 2], gsc[b, kk], s_gl[kk % 2], after=[(s_g[0], 16 * P.dma_cnt[s_g[0]])])
            k.tt("pool" if kk % 2 else "dve", XC[:, kk, :], XC[:, kk, :], gl[kk % 2], ALU.mult)
        P.dma(xt[0], xtm[b, 0:128, :], s_xt[0])
        for i in range(NB):
            if i + 1 < NB:
                P.dma(xt[(i + 1) % 2], xtm[b, 128 * (i + 1):128 * (i + 1) + 128, :], s_xt[(i + 1) % 2])
            h_ = hb[i % 2]
            for hf in range(2):
                ps = P.ps(nextbank(0, 8))
                for kk in range(8):
                    k.mm(ps, XC[:, kk, 128 * i:128 * i + 128], woutb[:, kk, 512 * hf:512 * hf + 512], kk == 0, kk == 7)
                k.stt(h_[:, 512 * hf:512 * hf + 512], xt[i % 2][:, 512 * hf:512 * hf + 512], DN_ALPHA, ps, ALU.mult, ALU.add)
                P.op("dve", lambda e, o_=st6[:, i % 2, 6 * hf:6 * hf + 6], a_=h_[:, 512 * hf:512 * hf + 512]: e.bn_stats(out=o_, in_=a_),
                     [h_[:, 512 * hf:512 * hf + 512]], [st6[:, i % 2, 6 * hf:6 * hf + 6]])
            mv = mv2[:, i % 2, 0:2]
            P.op("dve", lambda e, o_=mv, a_=st6[:, i % 2, 0:12]: e.bn_aggr(out=o_, in_=a_),
                 [st6[:, i % 2, 0:12]], [mv])
            r4 = rs[:, i % 2, 0:4]
            k.ts("dve", r4[:, 0:1], mv[:, 1:2], LN_EPS, ALU.add)
            k.act(r4[:, 1:2], r4[:, 0:1], AF.Sqrt)
            P.op("dve", lambda e, o_=r4[:, 2:3], a_=r4[:, 1:2]: e.reciprocal(out=o_, in_=a_), [r4[:, 1:2]], [r4[:, 2:3]])
            k.ts("dve", r4[:, 3:4], mv[:, 0:1], -1.0, ALU.mult, r4[:, 2:3], ALU.mult)
            o_ = ob[i % 2]
            k.act(o_, h_, AF.Identity, bias=r4[:, 3:4], scale=r4[:, 2:3])
            k.tt("dve", o_, o_, lng, ALU.mult)
            k.tt("pool", o_, o_, lnb, ALU.add)
            P.dma(out[b, 128 * i:128 * i + 128, :], o_, s_out[i % 2], is_output=True, eng="pool")

    P.emit()
    return nc, dbg


_CACHE = {}


def kernel(x, mem, w_in, w_mem_kv, lam_re, lam_im, log_dt, b_re, b_im, c_re, c_im, d_skip,
           w_glu, b_glu, w_out, ln_g, ln_b):
    x = np.asarray(x, np.float32)
    mem = np.asarray(mem, np.float32)
    f = lambda a: np.asarray(a, np.float32)
    hw = host_weights(f(w_in), f(w_mem_kv), f(lam_re), f(lam_im), f(log_dt), f(b_re), f(b_im), f(c_re), f(c_im),
                      f(d_skip), f(w_glu), f(b_glu), f(w_out), f(ln_g), f(ln_b))
    hc = host_consts()
    if "nc" not in _CACHE:
        _CACHE["nc"] = build()[0]
    nc = _CACHE["nc"]
    in_maps = []
    for c in range(8):
        xb = x[BPC * c:BPC * c + BPC]
        m = dict(hw)
        m.update(hc)
        m["x"] = np.ascontiguousarray(xb)
        m["xT"] = np.ascontiguousarray(xb.transpose(0, 2, 1))
        m["memT"] = np.ascontiguousarray(mem[BPC * c:BPC * c + BPC].transpose(0, 2, 1))
        in_maps.append(m)
    res = run_bass_kernel_spmd(nc, in_maps, core_ids=list(range(8)))
    return np.concatenate([r["out"] for r in res.results], axis=0).astype(np.float32)
```

```python
import contextlib
import math
import numpy as np
import concourse.bass as bass
import concourse.mybir as mybir
from concourse.bass_utils import run_bass_kernel_spmd

DT = mybir.dt
F32, BF16, I32 = DT.float32, DT.bfloat16, DT.int32
ALU = mybir.AluOpType
AF = mybir.ActivationFunctionType
ESIZE = {F32: 4, BF16: 2, I32: 4}

ENGS = ["pe", "act", "dve", "pool", "sp"]
SB_BYTES = 206 * 1024
PS_BYTES = 16 * 1024
SLOT = 128
NDMA = 60

L = 2048
NB = 16
BPC = 2
TOPK = 256
NBIS = 21
DN_ALPHA = 2.0 ** 0.25
LN_EPS = 1e-5
BIG = 1.0e30
MASKNEG = -30000.0
PI = math.pi


class Prog:
    def __init__(self, nc):
        self.nc = nc
        self.ops = {e: [] for e in ENGS}
        self.kinds = ENGS + ["d%d" % i for i in range(NDMA)]
        self.kidx = {k: i for i, k in enumerate(self.kinds)}
        nk = len(self.kinds)
        self.nslots = {"sb": SB_BYTES // SLOT, "ps": PS_BYTES // SLOT}
        self.wk = {s: np.full(n, -1, np.int64) for s, n in self.nslots.items()}
        self.wv = {s: np.zeros(n, np.int64) for s, n in self.nslots.items()}
        self.rv = {s: np.zeros((nk, n), np.int64) for s, n in self.nslots.items()}
        self.seen = {e: np.zeros(nk, np.int64) for e in ENGS}
        self.dma_cnt = [0] * NDMA
        self.arena = nc.alloc_sbuf_tensor("arena", [128, SB_BYTES // 4], F32)
        self.psum = nc.alloc_psum_tensor("psum_all", [128, PS_BYTES // 4], F32)
        self.sb_off = 0
        self.out_dma = []
        self.nsem = 0
        self._dummy = self.sb([8], F32)
        self._dummy_act = self.sb([8], F32)

    def newsem(self):
        self.nsem += 1
        assert self.nsem <= NDMA
        return self.nsem - 1

    def sb(self, shape, dtype):
        n = int(np.prod(shape)) * ESIZE[dtype]
        n = (n + SLOT - 1) // SLOT * SLOT
        off = self.sb_off
        self.sb_off += n
        assert self.sb_off <= SB_BYTES, "SBUF arena overflow %d" % self.sb_off
        return self.sb_at(off, shape, dtype)

    def sb_at(self, off, shape, dtype):
        assert off % 4 == 0
        nel = int(np.prod(shape))
        nb = nel * ESIZE[dtype]
        assert off + nb <= SB_BYTES, "SBUF arena overflow (at) %d" % (off + nb)
        v = self.arena[:, off // 4:(off + nb + 3) // 4]
        if dtype != F32:
            v = v.bitcast(dtype)[:, 0:nel]
        if len(shape) > 1:
            names = " ".join("a%d" % i for i in range(len(shape)))
            kw = {"a%d" % i: int(s) for i, s in enumerate(shape)}
            v = v.rearrange("p (%s) -> p %s" % (names, names), **kw)
        return v

    def ps(self, bank, shape=(512,), dtype=F32, off=0):
        nel = int(np.prod(shape))
        nb = nel * ESIZE[dtype]
        b0 = bank * 2048 + off
        assert off + nb <= 2048
        v = self.psum[:, b0 // 4:(b0 + nb + 3) // 4]
        if dtype != F32:
            v = v.bitcast(dtype)[:, 0:nel]
        if len(shape) > 1:
            names = " ".join("a%d" % i for i in range(len(shape)))
            kw = {"a%d" % i: int(s) for i, s in enumerate(shape)}
            v = v.rearrange("p (%s) -> p %s" % (names, names), **kw)
        return v

    def _range(self, ap):
        sp = str(ap.space).lower()
        if "sb" in sp or "state" in sp:
            space, pitch = "sb", SB_BYTES
        elif "psum" in sp:
            space, pitch = "ps", PS_BYTES
        else:
            return None
        es = ESIZE[ap.dtype]
        off = (ap.offset * es) % pitch
        ext = 1
        for st, cnt in list(ap.ap)[1:]:
            ext += (cnt - 1) * abs(st)
        lo = off // SLOT
        hi = (off + ext * es - 1) // SLOT + 1
        if space == "ps":
            per = 2048 // SLOT
            lo = lo // per * per
            hi = (hi + per - 1) // per * per
        return space, lo, hi

    def _deps(self, eng, reads, writes):
        need = {}
        myk = self.kidx[eng]
        myseq = len(self.ops[eng]) + 1

        def add(k, v):
            if k < 0 or v <= 0:
                return
            if k == myk:
                if eng == "pe" or v < myseq - 1:
                    return
            if need.get(k, 0) < v:
                need[k] = v
        for ap in reads:
            r = self._range(ap)
            if r is None:
                continue
            s, lo, hi = r
            wk, wv = self.wk[s][lo:hi], self.wv[s][lo:hi]
            for k in np.unique(wk):
                if k >= 0:
                    add(int(k), int(wv[wk == k].max()))
            if s == "ps":
                rm = self.rv[s][:, lo:hi].max(axis=1)
                for k in np.nonzero(rm)[0]:
                    if int(k) != myk:
                        add(int(k), int(rm[k]))
        for ap in writes:
            r = self._range(ap)
            if r is None:
                continue
            s, lo, hi = r
            wk, wv = self.wk[s][lo:hi], self.wv[s][lo:hi]
            for k in np.unique(wk):
                if k >= 0:
                    add(int(k), int(wv[wk == k].max()))
            rm = self.rv[s][:, lo:hi].max(axis=1)
            for k in np.nonzero(rm)[0]:
                add(int(k), int(rm[k]))
        waits = {}
        seen = self.seen[eng]
        for k, v in need.items():
            if k == myk or seen[k] < v:
                waits[k] = v
                if k != myk:
                    seen[k] = v
        return waits

    def _mark(self, kind, val, reads, writes):
        for ap in reads:
            r = self._range(ap)
            if r is None:
                continue
            s, lo, hi = r
            self.rv[s][kind, lo:hi] = val
        for ap in writes:
            r = self._range(ap)
            if r is None:
                continue
            s, lo, hi = r
            self.wk[s][lo:hi] = kind
            self.wv[s][lo:hi] = val
            self.rv[s][:, lo:hi] = 0

    def op(self, eng, fn, reads=(), writes=(), accum=False):
        waits = self._deps(eng, reads, writes)
        self.ops[eng].append((fn, waits, ("accum", eng) if accum else None))
        self._mark(self.kidx[eng], len(self.ops[eng]), reads, writes)

    def dma(self, out, in_, sem, eng="sp", is_output=False, after=(), **kw):
        waits = self._deps(eng, [in_], [out])
        for s_, v_ in after:
            kk_ = self.kidx["d%d" % s_]
            if waits.get(kk_, 0) < v_:
                waits[kk_] = v_
        self.dma_cnt[sem] += 1
        val = 16 * self.dma_cnt[sem]
        fn = lambda e, out=out, in_=in_, kw=kw: e.dma_start(out=out, in_=in_, **kw)
        self.ops[eng].append((fn, waits, sem))
        self._mark(self.kidx["d%d" % sem], val, [in_], [out])
        if is_output:
            self.out_dma.append(sem)

    def emit(self):
        nc = self.nc
        waited = {e: set() for e in ENGS}
        selfw = {e: set() for e in ENGS}
        for e in ENGS:
            for fn, waits, dsem in self.ops[e]:
                for k, v in waits.items():
                    if k < len(ENGS):
                        waited[ENGS[k]].add(v)
                        if ENGS[k] == e:
                            selfw[e].add(v)
        rank = {e: {s: i + 1 for i, s in enumerate(sorted(waited[e]))} for e in ENGS}
        print("ops", {e: len(self.ops[e]) for e in ENGS}, "sem max", {e: len(rank[e]) for e in ENGS}, "dma", max(self.dma_cnt) * 16)
        with contextlib.ExitStack() as st:
            sems = {e: st.enter_context(nc.semaphore("s_" + e)) for e in ENGS}
            dsems = [st.enter_context(nc.semaphore("sd%d" % i)) for i in range(max(self.nsem, 1))]
            block = st.enter_context(nc.Block())

            def run(engname):
                def body(engine):
                    seq = 0
                    for fn, waits, dsem in self.ops[engname]:
                        seq += 1
                        for k, v in sorted(waits.items()):
                            if k < len(ENGS):
                                engine.wait_ge(sems[ENGS[k]], rank[ENGS[k]][v])
                            else:
                                engine.wait_ge(dsems[k - len(ENGS)], v)
                        ins = fn(engine)
                        if isinstance(dsem, tuple):
                            if seq in rank[engname] and seq not in selfw[engname]:
                                ins.then_inc(sems[engname], 1)
                            elif seq in rank[engname]:
                                if engname == "dve":
                                    ins = engine.memset(self._dummy[:, 0:1], 0.0)
                                else:
                                    ins = engine.activation(out=self._dummy_act[:, 0:1], in_=self._dummy_act[:, 1:2], func=AF.Copy)
                                ins.then_inc(sems[engname], 1)
                        elif dsem is not None:
                            ins.then_inc(dsems[dsem], 16)
                        elif seq in rank[engname]:
                            ins.then_inc(sems[engname], 1)
                    if engname == "sp":
                        for s in sorted(set(self.out_dma)):
                            engine.wait_ge(dsems[s], 16 * self.dma_cnt[s])
                return body
            block.tensor(run("pe"))
            block.scalar(run("act"))
            block.vector(run("dve"))
            block.gpsimd(run("pool"))
            block.sync(run("sp"))


def _aps(*xs):
    return [x for x in xs if not isinstance(x, (int, float)) and x is not None]


class K:
    def __init__(self, P):
        self.P = P

    def tt(self, eng, out, a, b, op):
        self.P.op(eng, lambda e: e.tensor_tensor(out=out, in0=a, in1=b, op=op), [a, b], [out])

    def ts(self, eng, out, a, s1, op0, s2=None, op1=None, accum_out=None):
        def fn(e):
            if op1 is None:
                return e.tensor_scalar(out=out, in0=a, scalar1=s1, scalar2=None, op0=op0)
            if accum_out is not None:
                return e.tensor_scalar(out=out, in0=a, scalar1=s1, scalar2=s2, op0=op0, op1=op1, accum_out=accum_out)
            return e.tensor_scalar(out=out, in0=a, scalar1=s1, scalar2=s2, op0=op0, op1=op1)
        self.P.op(eng, fn, [a] + _aps(s1, s2), [out] + _aps(accum_out), accum=accum_out is not None)

    def stt(self, out, a, s, b, op0, op1):
        self.P.op("dve", lambda e: e.scalar_tensor_tensor(out=out, in0=a, scalar=s, in1=b, op0=op0, op1=op1),
                  [a, b] + _aps(s), [out])

    def act(self, out, a, func, bias=None, scale=1.0, accum_out=None):
        def fn(e):
            kw = {}
            if bias is not None:
                kw["bias"] = bias
            if accum_out is not None:
                kw["accum_out"] = accum_out
            return e.activation(out=out, in_=a, func=func, scale=scale, **kw)
        self.P.op("act", fn, [a] + _aps(bias, scale), [out] + _aps(accum_out), accum=accum_out is not None)

    def cp(self, eng, out, a):
        if eng == "act":
            self.P.op("act", lambda e: e.activation(out=out, in_=a, func=AF.Copy), [a], [out])
        else:
            self.P.op(eng, lambda e: e.tensor_copy(out=out, in_=a), [a], [out])

    def memset(self, eng, out, val):
        self.P.op(eng, lambda e: e.memset(out, val), [], [out])

    def mm(self, out, lhsT, rhs, start, stop, skip=False):
        self.P.op("pe", lambda e: e.matmul(out, lhsT=lhsT, rhs=rhs, start=start, stop=stop, skip_group_check=skip),
                  [lhsT, rhs], [out])

    def tr(self, out, a, ident):
        self.P.op("pe", lambda e: e.transpose(out, a, ident), [a, ident], [out])


SPLITS = [512, 128, 128, 256, 32, 8, 512, 256, 256, 256, 256]
NFM = 21


def _swap_perm(width, hd, half):
    perm = np.arange(width)
    for c in range(width):
        d = c % hd
        if d < half:
            perm[c] = c + half
        elif d < 2 * half:
            perm[c] = c - half
    return perm


def _rope_tables(hd, reps):
    r = hd // 4
    half = r // 2
    inv = (np.float32(500000.0) ** (-np.arange(0, half, dtype=np.float32) * np.float32(2.0) / np.float32(r))).astype(np.float32)
    pos = np.arange(L, dtype=np.float32)
    ang = (pos[:, None] * inv[None, :]).astype(np.float32)
    cos = np.cos(ang).astype(np.float32).T
    sin = np.sin(ang).astype(np.float32).T
    C = np.ones((hd, L), np.float32)
    S = np.zeros((hd, L), np.float32)
    C[0:half] = cos
    C[half:2 * half] = cos
    S[0:half] = -sin
    S[half:2 * half] = sin
    return np.tile(C, (reps, 1)), np.tile(S, (reps, 1))


def host_consts():
    c64, s64 = _rope_tables(64, 2)
    c32, s32 = _rope_tables(32, 4)
    ident = np.eye(128, dtype=np.float32)
    causal = np.where(np.arange(128)[None, :] <= np.arange(128)[:, None], 0.0, -BIG).astype(np.float32)
    chix = np.ascontiguousarray(np.repeat(np.arange(256, dtype=np.float32)[None, :], 128, axis=0))
    pm64 = np.zeros((128, 128), np.float32)
    pm64[_swap_perm(128, 64, 8), np.arange(128)] = 1.0
    pm32 = np.zeros((128, 128), np.float32)
    pm32[_swap_perm(128, 32, 4), np.arange(128)] = 1.0
    vmask = (np.arange(128)[:, None] // 32 == np.arange(4)[None, :]).astype(np.float32)
    return dict(c64=c64, s64=s64, c32=c32, s32=s32, ident=ident, causal=causal, chix=chix, pm64=pm64, pm32=pm32, vmask=vmask)


def host_weights(w_in, w_mem_kv, lam_re, lam_im, log_dt, b_re, b_im, c_re, c_im, d_skip, w_glu, b_glu, w_out, ln_g, ln_b):
    sp = np.concatenate([[0], np.cumsum(SPLITS)])
    col = lambda i: w_in[:, sp[i]:sp[i + 1]]
    q_c, k_c, v_c, qi_c, ki_c, wi_c, gatt_c, u_c, gssm_c, qm_c, gmem_c = [col(i) for i in range(11)]
    p64_512 = _swap_perm(512, 64, 8)
    p64_128 = _swap_perm(128, 64, 8)
    p32_256 = _swap_perm(256, 32, 4)
    p32_32 = _swap_perm(32, 32, 4)
    tiles = []
    for t in range(2):
        tiles += [qi_c[:, 128 * t:128 * t + 128]]
    tiles += [np.concatenate([ki_c] * 4, axis=1)]
    for t in range(4):
        tiles += [q_c[:, 128 * t:128 * t + 128]]
    for g in range(2):
        tiles += [np.concatenate([k_c[:, 64 * g:64 * g + 64]] * 2, axis=1)]
    tiles += [u_c[:, 0:128], u_c[:, 128:256], qm_c[:, 0:128], qm_c[:, 128:256]]
    gates = np.concatenate([gatt_c, gssm_c, gmem_c], axis=1)
    for t in range(8):
        tiles.append(gates[:, 128 * t:128 * t + 128])
    assert len(tiles) == NFM
    wfm = np.ascontiguousarray(np.stack(tiles, 0)).astype(np.float32)
    wtm = np.ascontiguousarray(np.concatenate([v_c, wi_c], axis=1)).astype(np.float32)
    bre = np.zeros((8, 128, 128), np.float32)
    bim = np.zeros((8, 128, 128), np.float32)
    cre = np.zeros((8, 128, 128), np.float32)
    cim = np.zeros((8, 128, 128), np.float32)
    for i in range(8):
        for gl in range(2):
            g = 2 * i + gl
            g8 = g % 8
            bre[i, 16 * g8:16 * g8 + 16, 64 * gl:64 * gl + 64] = b_re[g].T
            bim[i, 16 * g8:16 * g8 + 16, 64 * gl:64 * gl + 64] = b_im[g].T
            cre[i, 64 * gl:64 * gl + 64, 16 * g8:16 * g8 + 16] = c_re[g].T
            cim[i, 64 * gl:64 * gl + 64, 16 * g8:16 * g8 + 16] = c_im[g].T
    dblk = np.zeros((2, 128, 128), np.float32)
    dflat = d_skip.reshape(256)
    for ct in range(2):
        dblk[ct][np.arange(128), np.arange(128)] = dflat[128 * ct:128 * ct + 128]

    def st(a):
        return np.ascontiguousarray(a.reshape(8, 2, 64).transpose(1, 2, 0).reshape(128, 8)).astype(np.float32)
    lamre = st(lam_re)
    lamim = st(lam_im)
    logdt = st(np.repeat(log_dt[:, None], 64, axis=1))
    bglu = np.ascontiguousarray(b_glu.reshape(2, 128).T).astype(np.float32)
    lng = np.ascontiguousarray(np.repeat(ln_g[None, :], 128, axis=0)).astype(np.float32)
    lnb = np.ascontiguousarray(np.repeat(ln_b[None, :], 128, axis=0)).astype(np.float32)
    bre = np.ascontiguousarray(bre.transpose(0, 2, 1))
    bim = np.ascontiguousarray(bim.transpose(0, 2, 1))
    return dict(wfm=wfm, wtm=wtm, wmem=np.ascontiguousarray(w_mem_kv), wglu=np.ascontiguousarray(w_glu),
                wout=np.ascontiguousarray(w_out), bre=bre, bim=bim, cre=cre, cim=cim, dblk=dblk,
                lamre=lamre, lamim=lamim, logdt=logdt, bglu=bglu, lng=lng, lnb=lnb)


def build(stop=99, debug=False):
    nc = bass.Bass("TRN2", target_bir_lowering=False)

    def din(name, shape):
        return nc.dram_tensor(name, list(shape), F32, kind="ExternalInput").ap()
    xT = din("xT", [BPC, 1024, L])
    xtm = din("x", [BPC, L, 1024])
    memT = din("memT", [BPC, 1024, 256])
    wfm = din("wfm", [NFM, 1024, 128])
    wtm = din("wtm", [1024, 136])
    wmem = din("wmem", [1024, 512])
    wglu = din("wglu", [256, 256])
    wout = din("wout", [1024, 1024])
    d_c64, d_s64, d_c32, d_s32 = [din(n, [128, L]) for n in ("c64", "s64", "c32", "s32")]
    d_bre, d_bim, d_cre, d_cim = [din(n, [8, 128, 128]) for n in ("bre", "bim", "cre", "cim")]
    d_dblk = din("dblk", [2, 128, 128])
    d_lamre, d_lamim, d_logdt = [din(n, [128, 8]) for n in ("lamre", "lamim", "logdt")]
    d_bglu = din("bglu", [128, 2])
    d_lng = din("lng", [128, 1024])
    d_lnb = din("lnb", [128, 1024])
    d_ident = din("ident", [128, 128])
    d_causal = din("causal", [128, 128])
    d_chix = din("chix", [128, 256])
    d_pm64 = din("pm64", [128, 128])
    d_pm32 = din("pm32", [128, 128])
    d_vmask = din("vmask", [128, 4])
    out = nc.dram_tensor("out", [BPC, L, 1024], F32, kind="ExternalOutput").ap()
    gsc = nc.dram_tensor("gsc", [BPC, 8, 128, L], BF16).ap()
    ssc = nc.dram_tensor("ssc", [BPC, 2, 128, L], BF16).ap()

    P = Prog(nc)
    k = K(P)
    dbg = {}

    def dump(name, src, shape, dtype=F32):
        if not debug:
            return
        t = nc.dram_tensor("dbg_" + name, list(shape), dtype, kind="ExternalOutput").ap()
        P.dma(t, src, P.newsem(), is_output=True)
        dbg[name] = (shape, dtype)

    identf = P.sb([128], F32)
    identb = P.sb([128], BF16)
    ident4 = P.sb([4, 128], BF16)
    causal = P.sb([128], F32)
    pmb = P.sb([2, 128], BF16)
    vmask = P.sb([4], F32)
    wglub = P.sb([2, 256], BF16)
    bglu = P.sb([2], F32)
    wtmb = P.sb([8, 136], BF16)
    sc8 = {n: P.sb([8], F32) for n in ("lre", "lim", "ldt", "dt", "a", "th", "rho", "t1", "t2", "sin", "cos", "lbr", "lbi",
                                       "nre", "nim", "den", "kre", "kim", "nkim", "u1", "u2", "r8")}
    sc8i = P.sb([8], I32)
    mure = P.sb([11, 8], F32)
    muim = P.sb([11, 8], F32)
    nmuim = P.sb([11, 8], F32)
    uT = P.sb([2, L], BF16)
    PB0 = P.sb_off
    qT = P.sb([4, L], BF16)
    kT = P.sb([2, L], BF16)
    qiT = P.sb([2, L], BF16)
    kiT4 = P.sb([4, L], BF16)
    vaug = P.sb([NB, 2, 65], BF16)
    absw = P.sb([NB, 8], F32)
    sgnw = P.sb([NB, 8], F32)
    qmT = P.sb([2, L], BF16)
    mkT = P.sb([2, 256], BF16)
    mvaug = P.sb([2, 4, 65], BF16)
    woutb = P.sb([8, 1024], BF16)
    lng = P.sb([1024], F32)
    lnb = P.sb([1024], F32)
    PB1 = P.sb_off
    XC = P.sb([8, L], BF16)
    small = P.sb([64], F32)
    ARENA0 = P.sb_off
    ARENA_SZ = SB_BYTES - ARENA0
    print("resident bytes", ARENA0, "arena", ARENA_SZ)

    class Arena:
        def __init__(self):
            self.off = ARENA0

        def sb(self, shape, dtype):
            n = int(np.prod(shape)) * ESIZE[dtype]
            n = (n + SLOT - 1) // SLOT * SLOT
            o = self.off
            self.off += n
            assert self.off <= SB_BYTES, "phase arena overflow %d" % (self.off - ARENA0)
            return P.sb_at(o, shape, dtype)

    A = Arena()
    stg = A.sb([8, 1024], F32)
    P.dma(identf, d_ident, P.newsem())
    P.dma(causal, d_causal, P.newsem())
    P.dma(bglu, d_bglu, P.newsem())
    for n, d in (("lre", d_lamre), ("lim", d_lamim), ("ldt", d_logdt)):
        P.dma(sc8[n], d, P.newsem())
    k.cp("dve", identb, identf)
    P.dma(vmask, d_vmask, P.newsem())
    for q_, dsrc in enumerate((d_pm32, d_pm64)):
        P.dma(stg[:, q_, 0:128], dsrc, P.newsem())
        k.cp("dve", pmb[:, q_, :], stg[:, q_, 0:128])
    for r in range(4):
        k.cp("dve", ident4[:, r, :], identf)
    s_stg = P.newsem()
    v = stg[:, 0, 0:512].rearrange("p (i c) -> p i c", i=2)
    P.dma(v, wglu.rearrange("(k p) c -> p k c", p=128), s_stg)
    k.cp("dve", wglub, v)
    v = stg[:, 0:2, :].rearrange("p a b -> p (a b)")[:, 0:8 * 136].rearrange("p (i c) -> p i c", i=8)
    P.dma(v, wtm.rearrange("(k p) c -> p k c", p=128), s_stg)
    k.cp("dve", wtmb, v)

    s = sc8
    k.act(s["dt"], s["ldt"], AF.Exp)
    k.tt("dve", s["a"], s["lre"], s["dt"], ALU.mult)
    k.tt("dve", s["th"], s["lim"], s["dt"], ALU.mult)
    k.act(s["rho"], s["a"], AF.Exp)

    def sin_of(dst, src, shift):
        k.ts("dve", s["t1"], src, shift, ALU.add, 1.0 / (2 * PI), ALU.mult)
        k.cp("dve", sc8i, s["t1"])
        k.cp("dve", s["t2"], sc8i)
        k.ts("dve", s["t1"], src, shift, ALU.add)
        k.stt(s["t1"], s["t2"], -2 * PI, s["t1"], ALU.mult, ALU.add)
        k.ts("dve", s["t1"], s["t1"], 3.1415925, ALU.min, -3.1415925, ALU.max)
        k.act(dst, s["t1"], AF.Sin)
    sin_of(s["sin"], s["th"], 0.0)
    sin_of(s["cos"], s["th"], PI / 2)
    k.tt("dve", s["lbr"], s["rho"], s["cos"], ALU.mult)
    k.tt("dve", s["lbi"], s["rho"], s["sin"], ALU.mult)
    k.ts("dve", s["t1"], s["lbr"], -1.0, ALU.add)
    k.tt("dve", s["nre"], s["t1"], s["lre"], ALU.mult)
    k.tt("dve", s["u1"], s["lbi"], s["lim"], ALU.mult)
    k.tt("dve", s["nre"], s["nre"], s["u1"], ALU.add)
    k.tt("dve", s["nim"], s["lbi"], s["lre"], ALU.mult)
    k.tt("dve", s["u1"], s["t1"], s["lim"], ALU.mult)
    k.tt("dve", s["nim"], s["nim"], s["u1"], ALU.subtract)
    k.tt("dve", s["den"], s["lre"], s["lre"], ALU.mult)
    k.tt("dve", s["u1"], s["lim"], s["lim"], ALU.mult)
    k.tt("dve", s["den"], s["den"], s["u1"], ALU.add)
    P.op("dve", lambda e: e.reciprocal(out=s["u2"], in_=s["den"]), [s["den"]], [s["u2"]])
    k.tt("dve", s["kre"], s["nre"], s["u2"], ALU.mult)
    k.tt("dve", s["kim"], s["nim"], s["u2"], ALU.mult)
    k.ts("dve", s["nkim"], s["kim"], -1.0, ALU.mult)
    k.cp("dve", mure[:, 0, :], s["lbr"])
    k.cp("dve", muim[:, 0, :], s["lbi"])
    for lv in range(1, 11):
        k.tt("dve", s["u1"], mure[:, lv - 1, :], mure[:, lv - 1, :], ALU.mult)
        k.tt("dve", s["u2"], muim[:, lv - 1, :], muim[:, lv - 1, :], ALU.mult)
        k.tt("dve", mure[:, lv, :], s["u1"], s["u2"], ALU.subtract)
        k.tt("dve", s["u1"], mure[:, lv - 1, :], muim[:, lv - 1, :], ALU.mult)
        k.ts("dve", muim[:, lv, :], s["u1"], 2.0, ALU.mult)
    k.ts("dve", nmuim, muim, -1.0, ALU.mult)
    if debug:
        dump("mure", mure, [128, 11, 8])
        dump("muim", muim, [128, 11, 8])
        dump("kre", s["kre"], [128, 8])
        dump("kim", s["kim"], [128, 8])

    s_x = P.newsem()
    s_x2 = P.newsem()
    s_tab = [P.newsem() for _ in range(2)]
    s_w = [P.newsem() for _ in range(2)]
    s_g = [P.newsem() for _ in range(2)]
    s_misc = P.newsem()
    s_misc2 = P.newsem()
    s_out = [P.newsem() for _ in range(2)]
    s_xt = [P.newsem() for _ in range(2)]
    s_gl = [P.newsem() for _ in range(2)]
    psrot = [0]

    def nextbank(lo=0, hi=8):
        b = lo + psrot[0] % (hi - lo)
        psrot[0] += 1
        return b


    T8 = 8
    NCH = L // T8
    s_ss = [P.newsem() for _ in range(2)]
    s_ssl = P.newsem()
    s_bt = [P.newsem() for _ in range(5)]
    s_xs = P.newsem()
    s_ws = [P.newsem() for _ in range(2)]
    W1 = P.sb_at(PB0, [8, T8, 2, 128], BF16)
    W2 = P.sb_at(PB0 + 32768, [8, T8, 2, 128], BF16)
    Kt = P.sb_at(PB0 + 65536, [2, T8, 128], BF16)
    assert PB0 + 65536 + 4096 <= PB1
    A = Arena()
    BTr = A.sb([8, 128], F32)
    BTi = A.sb([8, 128], F32)
    CTr = A.sb([8, 128], F32)
    CTi = A.sb([8, 128], F32)
    nCTi = A.sb([8, 128], F32)
    dbf = A.sb([2, 128], F32)
    bts = [[A.sb([128], F32) for _ in range(2)] for _ in range(2)]
    w2t = [A.sb([128], F32) for _ in range(2)]
    LKr, LKi, nLKr, nLKi = [A.sb([T8 + 1, 8], F32) for _ in range(4)]
    GKr, GKi, nGKi = [A.sb([T8, 8], F32) for _ in range(3)]
    for dsrc, dst, sm in ((d_bre, BTr, 0), (d_bim, BTi, 1), (d_cre, CTr, 2), (d_cim, CTi, 3)):
        P.dma(dst, dsrc.rearrange("i p c -> p i c"), s_bt[sm])
    P.dma(dbf, d_dblk.rearrange("i p c -> p i c"), s_bt[4])
    k.ts("pool", nCTi, CTi, -1.0, ALU.mult, 0.0, ALU.add)
    k.cp("dve", LKr[:, 1, :], s["lbr"])
    k.cp("dve", LKi[:, 1, :], s["lbi"])
    for kk in range(2, T8 + 1):
        k.tt("dve", s["u1"], LKr[:, kk - 1, :], s["lbr"], ALU.mult)
        k.tt("dve", s["u2"], LKi[:, kk - 1, :], s["lbi"], ALU.mult)
        k.tt("dve", LKr[:, kk, :], s["u1"], s["u2"], ALU.subtract)
        k.tt("dve", s["u1"], LKr[:, kk - 1, :], s["lbi"], ALU.mult)
        k.tt("dve", s["u2"], LKi[:, kk - 1, :], s["lbr"], ALU.mult)
        k.tt("dve", LKi[:, kk, :], s["u1"], s["u2"], ALU.add)
    k.ts("dve", nLKr[:, 1:, :], LKr[:, 1:, :], -1.0, ALU.mult)
    k.ts("dve", nLKi[:, 1:, :], LKi[:, 1:, :], -1.0, ALU.mult)
    k.cp("dve", GKr[:, 0, :], s["kre"])
    k.cp("dve", GKi[:, 0, :], s["kim"])
    for kk in range(1, T8):
        k.tt("dve", s["u1"], LKr[:, kk, :], s["kre"], ALU.mult)
        k.tt("dve", s["u2"], LKi[:, kk, :], s["kim"], ALU.mult)
        k.tt("dve", GKr[:, kk, :], s["u1"], s["u2"], ALU.subtract)
        k.tt("dve", s["u1"], LKr[:, kk, :], s["kim"], ALU.mult)
        k.tt("dve", s["u2"], LKi[:, kk, :], s["kre"], ALU.mult)
        k.tt("dve", GKi[:, kk, :], s["u1"], s["u2"], ALU.add)
    k.ts("dve", nGKi, GKi, -1.0, ALU.mult)
    COST = P.sb_at(PB0 + 65536 + 4096, [8, NCH], F32)
    SINT = P.sb_at(PB0 + 65536 + 4096 + 8192, [8, NCH], F32)
    assert PB0 + 65536 + 4096 + 16384 <= PB1
    r8 = s["r8"]
    ff = A.sb([8], F32)
    ffi = A.sb([8], I32)
    chix = A.sb([NCH], F32)
    Gt = A.sb([8, NCH], F32)
    Gi = A.sb([8, NCH], I32)
    Gf = A.sb([8, NCH], F32)
    P.dma(chix, d_chix, P.newsem())
    k.act(r8, s["a"], AF.Exp, scale=float(T8))
    k.ts("dve", ff, s["th"], float(T8) / (2 * PI), ALU.mult)
    k.cp("dve", ffi, ff)
    k.cp("dve", s["u1"], ffi)
    k.tt("dve", ff, ff, s["u1"], ALU.subtract)
    k.tt("dve", Gt, ff.unsqueeze(2).broadcast_to([128, 8, NCH]), chix.unsqueeze(1).broadcast_to([128, 8, NCH]), ALU.mult)
    for shift, dstT in ((0.0, SINT), (0.25, COST)):
        src = Gt
        if shift != 0.0:
            k.ts("dve", Gf, Gt, shift, ALU.add)
            src = Gf
        k.cp("dve", Gi, src)
        k.cp("pool", dstT, Gi)
        k.tt("dve", dstT, src, dstT, ALU.subtract)
        k.ts("dve", dstT, dstT, 2 * PI, ALU.mult, 3.1415925, ALU.min)
        k.ts("dve", dstT, dstT, -3.1415925, ALU.max)
        k.act(dstT, dstT, AF.Sin)
    nb_ = 0
    for kk in range(T8):
        for ct in range(2):
            kp = P.ps(ct, (128,))
            for i in range(4 * ct, 4 * ct + 4):
                br_, bi_ = bts[nb_ % 2]
                nb_ += 1
                gr = GKr[:, kk, i:i + 1]
                gi = GKi[:, kk, i:i + 1]
                ngi = nGKi[:, kk, i:i + 1]
                k.ts("dve", br_, BTr[:, i, :], gr, ALU.mult)
                k.stt(br_, BTi[:, i, :], ngi, br_, ALU.mult, ALU.add)
                k.ts("dve", bi_, BTr[:, i, :], gi, ALU.mult)
                k.stt(bi_, BTi[:, i, :], gr, bi_, ALU.mult, ALU.add)
                for x_, src in ((0, br_), (1, bi_)):
                    pt = P.ps(4 + (2 * nb_ + x_) % 4, (128,))
                    k.tr(pt, src, identf)
                    k.cp("act", W1[:, i, T8 - 1 - kk, x_, :], pt)
                k.mm(kp, br_, CTr[:, i, :], i == 4 * ct, False)
                k.mm(kp, bi_, nCTi[:, i, :], False, i == 4 * ct + 3)
            if kk == 0:
                k.tt("dve", Kt[:, ct, kk, :], kp, dbf[:, ct, :], ALU.add)
            else:
                k.cp("act", Kt[:, ct, kk, :], kp)
    for i in range(8):
        for j in range(T8):
            lr = LKr[:, j + 1, i:i + 1]
            nli = nLKi[:, j + 1, i:i + 1]
            nlr = nLKr[:, j + 1, i:i + 1]
            t_ = w2t[(i * T8 + j) % 2]
            k.ts("pool", t_, CTr[:, i, :], lr, ALU.mult, 0.0, ALU.add)
            k.stt(W2[:, i, j, 0, :], CTi[:, i, :], nli, t_, ALU.mult, ALU.add)
            t2_ = w2t[(i * T8 + j + 1) % 2]
            k.act(t2_, CTr[:, i, :], AF.Copy, scale=nli)
            k.stt(W2[:, i, j, 1, :], CTi[:, i, :], nlr, t2_, ALU.mult, ALU.add)

    import os
    PHS = int(os.environ.get("PHS_VARIANT", "99"))
    A = Arena()
    xstS = A.sb([L], F32)
    wstS = [A.sb([8, 128], F32) for _ in range(2)]
    wbfS = [A.sb([8, 128], BF16) for _ in range(2)]
    Lr = A.sb([8, NCH], F32)
    Li = A.sb([8, NCH], F32)
    Br = A.sb([8, NCH], F32)
    Bi = A.sb([8, NCH], F32)
    Spr = A.sb([8, NCH], BF16)
    Spi = A.sb([8, NCH], BF16)
    uTd = A.sb([2, T8, NCH], BF16)
    ygb = P.sb_at(ARENA0, [2, L], BF16)
    osb = [P.sb_at(ARENA0 + 8192 + 4096 * q_, [L], BF16) for q_ in range(2)]
    assert 8192 + 8192 <= 8192 + 2 * 4096 + 2 * 2048
    k.memset("pool", Spr[:, :, 0:1], 0.0)
    k.memset("pool", Spi[:, :, 0:1], 0.0)
    def xload_S(bb):
        for kk in range(8):
            P.dma(xstS, xT[bb, 128 * kk:128 * kk + 128, :], s_xs)
            k.cp("dve" if kk % 2 == 0 else "act", XC[:, kk, :], xstS)
    xload_S(0)
    for b in range(BPC if PHS > 10 else 0):
        for ct in range(2):
            P.dma(wstS[ct], wfm[9 + ct].rearrange("(k p) c -> p k c", p=128), s_ws[ct])
            k.cp("act", wbfS[ct], wstS[ct])
            for c in range(4):
                ps = P.ps(nextbank(0, 4))
                for kk in range(8):
                    k.mm(ps, wbfS[ct][:, kk, :], XC[:, kk, 512 * c:512 * c + 512], kk == 0, kk == 7)
                k.cp("act", uT[:, ct, 512 * c:512 * c + 512], ps)
        xload_S(b + 1 if b + 1 < BPC else 0)
        for ct in range(2):
            k.cp("dve", uTd[:, ct, :, :], uT[:, ct, :].rearrange("p (c j) -> p j c", j=T8))
        for i in range(8):
            ct = i // 4
            for x_, dstL in ((0, Lr), (1, Li)):
                q_ = 2 * i + x_
                psL = P.ps(4 + q_ % 4, (NCH,))
                for j in range(T8):
                    k.mm(psL, W1[:, i, j, x_, :], uTd[:, ct, j, :], j == 0, j == T8 - 1, skip=True)
                k.cp("act", dstL[:, i, :], psL)
        T1, T2 = Br, Bi
        k.tt("dve", T1, COST, Lr, ALU.mult)
        k.tt("dve", T2, SINT, Li, ALU.mult)
        k.tt("dve", T1, T1, T2, ALU.add)
        k.tt("dve", T2, SINT, Lr, ALU.mult)
        k.tt("dve", Li, COST, Li, ALU.mult)
        k.tt("dve", Li, Li, T2, ALU.subtract)
        for i in range(8):
            d0 = r8[:, i:i + 1].broadcast_to([128, NCH])
            P.op("dve", lambda e, o_=Lr[:, i, :], d0=d0, d1=T1[:, i, :]: e.tensor_tensor_scan(out=o_, data0=d0, data1=d1, initial=0.0, op0=ALU.mult, op1=ALU.add),
                 [T1[:, i, :], r8], [Lr[:, i, :]])
            P.op("dve", lambda e, o_=T2[:, i, :], d0=d0, d1=Li[:, i, :]: e.tensor_tensor_scan(out=o_, data0=d0, data1=d1, initial=0.0, op0=ALU.mult, op1=ALU.add),
                 [Li[:, i, :], r8], [T2[:, i, :]])
        k.tt("dve", T1, COST, Lr, ALU.mult)
        k.tt("dve", Li, SINT, T2, ALU.mult)
        k.tt("dve", T1, T1, Li, ALU.subtract)
        k.tt("dve", Li, SINT, Lr, ALU.mult)
        k.tt("dve", T2, COST, T2, ALU.mult)
        k.tt("dve", Li, Li, T2, ALU.add)
        for i in range(8):
            k.cp("act", Spr[:, i, 1:NCH], T1[:, i, 0:NCH - 1])
            k.cp("act", Spi[:, i, 1:NCH], Li[:, i, 0:NCH - 1])
        if PHS <= 20:
            continue
        y = P.sb_at(ARENA0 + 20480, [L], F32)
        t1 = P.sb_at(ARENA0 + 20480 + 8192, [L], F32)
        t2 = P.sb_at(ARENA0 + 20480 + 16384, [L], F32)
        sg = P.sb_at(ARENA0 + 20480 + 24576, [L], F32)
        for ct in range(2):
            for c4 in range(4):
                yp = P.ps(c4, (64, T8))
                uv = uT[:, ct, 512 * c4:512 * c4 + 512].rearrange("p (c j) -> p c j", j=T8)
                for tau in range(T8):
                    k.mm(yp[:, :, tau:T8], Kt[:, ct, tau, :], uv[:, :, 0:T8 - tau], tau == 0, False, skip=True)
                for i in range(4 * ct, 4 * ct + 4):
                    for j in range(T8):
                        k.mm(yp[:, :, j], W2[:, i, j, 0, :], Spr[:, i, 64 * c4:64 * c4 + 64], False, False, skip=True)
                        k.mm(yp[:, :, j], W2[:, i, j, 1, :], Spi[:, i, 64 * c4:64 * c4 + 64], False,
                             (i == 4 * ct + 3 and j == T8 - 1), skip=True)
            for c in range(4):
                sl = slice(512 * c, 512 * c + 512)
                k.cp("act", y[:, sl], P.ps(c))
            if debug and b == 0:
                dump("y%d" % ct, y, [128, L])
            k.tt("dve", t1, y, y, ALU.mult)
            k.ts("dve", t1, t1, 0.044715, ALU.mult, 1.0, ALU.add)
            k.tt("dve", t1, t1, y, ALU.mult)
            k.act(t2, t1, AF.Sigmoid, scale=2.0 * 0.7978845608028654)
            k.tt("dve", ygb[:, ct, :], y, t2, ALU.mult)
        for et in range(2):
            for c in range(4):
                sl = slice(512 * c, 512 * c + 512)
                ps = P.ps(nextbank(4, 8))
                for kc in range(2):
                    k.mm(ps, wglub[:, kc, 128 * et:128 * et + 128], ygb[:, kc, sl], kc == 0, kc == 1)
                k.act(sg[:, sl], ps, AF.Sigmoid, bias=bglu[:, et:et + 1])
                k.tt("dve", osb[et][:, sl], ygb[:, et, sl], sg[:, sl], ALU.mult)
            P.dma(ssc[b, et], osb[et], s_ss[et])
    A = Arena()
    stg = A.sb([8, 1024], F32)
    P.dma(lng, d_lng, P.newsem())
    P.dma(lnb, d_lnb, P.newsem())
    P.dma(stg, wout.rearrange("(k p) c -> p k c", p=128), s_stg)
    for kk in range(8):
        k.cp("dve" if kk % 2 == 0 else "act", woutb[:, kk, :], stg[:, kk, :])

    for b in range(BPC):
        A = Arena()
        xst = A.sb([L], F32)
        xst2 = A.sb([L], F32)
        wst = [A.sb([8, 128], F32) for _ in range(2)]
        wbf = [A.sb([8, 128], BF16) for _ in range(2)]
        tabC = A.sb([L], F32)
        tabS = A.sb([L], F32)
        tmpA = [A.sb([512], F32) for _ in range(3)]
        zbf = [A.sb([512], BF16) for _ in range(3)]
        kirT = A.sb([L], BF16)
        tmp2 = [A.sb([512], F32) for _ in range(2)]
        gst = [A.sb([L], BF16) for _ in range(1)]
        for kk in range(8 if b > 0 else 0):
            xs_ = xst if kk % 2 == 0 else xst2
            P.dma(xs_, xT[b, 128 * kk:128 * kk + 128, :], s_x if kk % 2 == 0 else s_x2)
            k.cp("dve" if kk % 2 == 0 else "act", XC[:, kk, :], xs_)

        widx = {}

        def load_w(m):
            widx[m] = len(widx) % 2
            P.dma(wst[widx[m]], wfm[m].rearrange("(k p) c -> p k c", p=128), s_w[widx[m]])
            k.cp("act", wbf[widx[m]], wst[widx[m]])

        def proj(m, c):
            bank = nextbank()
            ps = P.ps(bank)
            for kk in range(8):
                k.mm(ps, wbf[widx[m]][:, kk, :], XC[:, kk, 512 * c:512 * c + 512], kk == 0, kk == 7)
            return ps
        dests = {}
        for t in range(2):
            dests[t] = ("rope", qiT[:, t, :], 0)
        dests[2] = ("rope", kirT, 0)
        for t in range(4):
            dests[3 + t] = ("rope", qT[:, t, :], 1)
        for g in range(2):
            dests[7 + g] = ("rope", kT[:, g, :], 1)
        dests[11] = ("plain", qmT[:, 0, :], None)
        dests[12] = ("plain", qmT[:, 1, :], None)
        for t in range(8):
            dests[13 + t] = ("gate", t, None)
        order = [m for m in range(NFM) if m not in (9, 10)]
        nta = 0
        pend = [None]
        for oi, m in enumerate(order):
            if m == 0:
                P.dma(tabC, d_c32, s_tab[0])
                P.dma(tabS, d_s32, s_tab[1])
                load_w(0)
            if m == 3:
                if pend[0] is not None:
                    pend[0]()
                    pend[0] = None
                for v_ in range(4):
                    k.ts("dve", kiT4[:, v_, :], kirT, vmask[:, v_:v_ + 1], ALU.mult)
                P.dma(tabC, d_c64, s_tab[0])
                P.dma(tabS, d_s64, s_tab[1])
            if oi + 1 < len(order):
                load_w(order[oi + 1])
            kind, dst, pq = dests[m]
            if kind == "rope":
                for c in range(4):
                    sl = slice(512 * c, 512 * c + 512)
                    ps = proj(m, c)
                    if pend[0] is not None:
                        pend[0]()
                        pend[0] = None
                    ta = tmpA[nta % 3][:, 0:512]
                    zb = zbf[nta % 3]
                    nta += 1
                    k.cp("act", zb, ps)
                    k.tt("dve", ta, ps, tabC[:, sl], ALU.mult)

                    def fin(zb=zb, ta=ta, sl=sl, dst=dst, pq=pq, c=c):
                        ps2 = P.ps(nextbank())
                        k.mm(ps2, pmb[:, pq, :], zb, True, True)
                        t2 = tmp2[c % 2]
                        k.tt("dve", t2, ps2, tabS[:, sl], ALU.mult)
                        k.tt("pool", dst[:, sl], t2, ta, ALU.add)
                    pend[0] = fin
            elif kind == "plain":
                for c in range(4):
                    ps = proj(m, c)
                    if pend[0] is not None:
                        pend[0]()
                        pend[0] = None
                    k.cp("act", dst[:, 512 * c:512 * c + 512], ps)
            else:
                gt = gst[0]
                for c in range(4):
                    ps = proj(m, c)
                    k.act(gt[:, 512 * c:512 * c + 512], ps, AF.Silu)
                P.dma(gsc[b, dst], gt, s_g[0], eng="act")
        for i in range(NB):
            bank = nextbank()
            ps = P.ps(bank, (136,))
            for kk in range(8):
                k.mm(ps, XC[:, kk, 128 * i:128 * i + 128], wtmb[:, kk, :], kk == 0, kk == 7)
            k.cp("act", vaug[:, i, :, 0:64], ps[:, 0:128].rearrange("p (g d) -> p g d", g=2))
            k.act(sgnw[:, i, :], ps[:, 128:136], AF.Sign)
            k.tt("dve", absw[:, i, :], ps[:, 128:136], sgnw[:, i, :], ALU.mult)
            k.ts("dve", absw[:, i, :], absw[:, i, :], 1.0 / 16.0, ALU.mult)
        if b == 0:
            k.memset("pool", vaug[:, :, :, 64:65], 1.0)
            k.memset("pool", mvaug[:, :, :, 64:65], 1.0)
        if debug and b == 0:
            dump("qT", qT, [128, 4, L], BF16)
            dump("kT", kT, [128, 2, L], BF16)
            dump("qiT", qiT, [128, 2, L], BF16)
            dump("kiT4", kiT4, [128, 4, L], BF16)
            dump("vaug", vaug, [128, NB, 2, 65], BF16)
            dump("absw", absw, [128, NB, 8])
            dump("sgnw", sgnw, [128, NB, 8])
            dump("qmT", qmT, [128, 2, L], BF16)
        if stop <= 1:
            continue

        A = Arena()
        mst = A.sb([8, 256], F32)
        mbf = A.sb([8, 256], BF16)
        wms = A.sb([8, 512], F32)
        wmb = A.sb([8, 512], BF16)
        P.dma(mst, memT[b].rearrange("(k p) c -> p k c", p=128), s_misc)
        P.dma(wms, wmem.rearrange("(k p) c -> p k c", p=128), s_misc2)
        k.cp("dve", mbf, mst)
        k.cp("pool", wmb, wms)
        for t in range(2):
            ps = P.ps(nextbank(), (256,))
            for kk in range(8):
                k.mm(ps, wmb[:, kk, 128 * t:128 * t + 128], mbf[:, kk, :], kk == 0, kk == 7)
            k.cp("act", mkT[:, t, :], ps)
        for mb_ in range(2):
            ps = P.ps(nextbank(), (256,))
            for kk in range(8):
                k.mm(ps, mbf[:, kk, 128 * mb_:128 * mb_ + 128], wmb[:, kk, 256:512], kk == 0, kk == 7)
            k.cp("act", mvaug[:, mb_, :, 0:64], ps.rearrange("p (h d) -> p h d", h=4))
        if debug and b == 0:
            dump("mkT", mkT, [128, 2, 256], BF16)
            dump("mvaug", mvaug, [128, 2, 4, 65], BF16)

        A = Arena()
        em = [A.sb([2, 512], BF16) for _ in range(2)]
        otm = [A.sb([128], BF16) for _ in range(2)]
        rd = A.sb([16], F32)
        nrd = 0
        for hp in range(2):
            for c in range(4):
                e_h = []
                for hh in range(2):
                    h = 2 * hp + hh
                    e = em[hh]
                    for mb_ in range(2):
                        ps = P.ps(nextbank(0, 4))
                        k.mm(ps, mkT[64 * hh:64 * hh + 64, hp, 128 * mb_:128 * mb_ + 128],
                             qmT[64 * hh:64 * hh + 64, hp, 512 * c:512 * c + 512], True, True)
                        k.act(e[:, mb_, :], ps, AF.Exp, scale=0.125)
                    e_h.append(e)
                for tb in range(4):
                    i = 4 * c + tb
                    o = otm[i % 2]
                    po = P.ps(4 + (i % 2), (2, 65))
                    for hh in range(2):
                        h = 2 * hp + hh
                        for mb_ in range(2):
                            k.mm(po[:, hh, :], e_h[hh][:, mb_, 128 * tb:128 * tb + 128], mvaug[:, mb_, h, :],
                                 (hh == 0 and mb_ == 0), mb_ == 1, skip=True)
                    r_ = rd[:, 2 * (nrd % 8):2 * (nrd % 8) + 2]
                    nrd += 1
                    P.op("dve", lambda e, r_=r_, po=po: e.reciprocal(out=r_, in_=po[:, :, 64]), [po], [r_])
                    k.tt("dve", o.rearrange("p (h d) -> p h d", h=2), po[:, :, 0:64],
                         r_.unsqueeze(2).broadcast_to([128, 2, 64]), ALU.mult)
                    pt = P.ps(6 + (i % 2), (128,), BF16)
                    k.tr(pt, o, identb)
                    k.cp("act", XC[:, 6 + hp, 128 * i:128 * i + 128], pt)
        if stop <= 4:
            continue

        A = Arena()
        NSC = 3
        sc = [A.sb([L], F32) for _ in range(NSC)]
        rt = [A.sb([512], F32) for _ in range(4)]
        junk = {"dve": A.sb([L], BF16), "act": A.sb([L], BF16)}
        mbk = [A.sb([L], BF16) for _ in range(NSC)]
        eg = [A.sb([4, 128], BF16) for _ in range(4)]
        oat = [A.sb([512], BF16) for _ in range(2)]
        scal = A.sb([NB, 32], F32)
        rda = A.sb([16], F32)
        nrt = [0]
        neg = [0]
        W0 = 32.0
        NIT = NBIS - 2

        def g_scores(i):
            S = 128 * (i + 1)
            s_ = sc[i % NSC]
            for c in range((S + 511) // 512):
                wc = min(512, S - 512 * c)
                sl = slice(512 * c, 512 * c + wc)
                for h in range(8):
                    ps = P.ps(nextbank(0, 4))[:, 0:wc]
                    k.mm(ps, qiT[:, h // 4, 128 * i:128 * i + 128], kiT4[:, h % 4, sl], True, True)
                    r_ = rt[nrt[0] % 4][:, 0:wc]
                    nrt[0] += 1
                    k.act(r_, ps, AF.Relu, scale=absw[:, i, h:h + 1])
                    if h == 0:
                        k.ts("dve", s_[:, sl], r_, sgnw[:, i, h:h + 1], ALU.mult)
                    else:
                        k.stt(s_[:, sl], r_, sgnw[:, i, h:h + 1], s_[:, sl], ALU.mult, ALU.add)
                    yield
            k.tt("pool", s_[:, 128 * i:128 * i + 128], s_[:, 128 * i:128 * i + 128], causal, ALU.add)
            yield

        def g_bisect(i):
            S = 128 * (i + 1)
            s_ = sc[i % NSC][:, 0:S]
            m = scal[:, i, 0:1]
            nm = scal[:, i, 1:2]
            c_ = scal[:, i, 2:3]
            a_ = scal[:, i, 3:4]
            eng = "dve" if i in (3, 5, 8, 10, 13, 15) else "act"
            if i < 2:
                k.memset("dve", m, -64.0)
                yield
            elif eng == "dve":
                k.memset("dve", m, 0.0)
                w = W0
                for it in range(NIT):
                    k.ts("dve", junk["dve"][:, 0:S], s_, m, ALU.is_ge, 0.0, ALU.add, accum_out=c_)
                    k.ts("dve", a_, c_, TOPK - 0.5, ALU.is_ge, w / 2, ALU.mult)
                    k.stt(m, a_, -w / 4, m, ALU.add, ALU.add)
                    w = w / 2
                    yield
                k.ts("dve", m, m, -w / 2, ALU.add)
            else:
                k.memset("dve", nm, 0.0)
                w = W0
                for it in range(NIT):
                    k.act(junk["act"][:, 0:S], s_, AF.Sign, bias=nm, accum_out=c_)
                    k.ts("dve", a_, c_, float(2 * TOPK - 1 - S), ALU.is_ge, -w / 2, ALU.mult)
                    k.stt(nm, a_, w / 4, nm, ALU.add, ALU.add)
                    w = w / 2
                    yield
                k.ts("dve", m, nm, -1.0, ALU.mult, -w / 2, ALU.add)
            k.ts("dve", mbk[i % NSC][:, 0:S], s_, m, ALU.is_lt, MASKNEG, ALU.mult)
            yield

        def g_attend(i):
            mb_ = mbk[i % NSC]
            po = [P.ps(6, (4, 65)), P.ps(7, (4, 65))]
            pend = [None]

            def av_step(j, par, e):
                for q_ in range(4):
                    h = par + 2 * q_
                    g = h // 4
                    k.mm(po[g][:, h % 4, :], e[:, q_, :], vaug[:, j, g, :], (j == 0 and par == 0 and h in (0, 4)), j == i, skip=True)
            for j in range(i + 1):
                for par in range(2):
                    lg = P.ps(4 + par, (4, 128))
                    k.mm(lg.rearrange("p r t -> p (r t)"), mb_[:, 128 * j:128 * j + 128], ident4.rearrange("p r t -> p (r t)"),
                         True, False, skip=True)
                    ro = 64 * par
                    for q_ in range(4):
                        h = par + 2 * q_
                        g = h // 4
                        k.mm(lg[:, q_, :], kT[ro:ro + 64, g, 128 * j:128 * j + 128], qT[ro:ro + 64, h // 2, 128 * i:128 * i + 128],
                             False, q_ == 3, skip=True)
                    e = eg[neg[0] % 4]
                    neg[0] += 1
                    k.act(e, lg, AF.Exp, scale=0.125)
                    if pend[0] is not None:
                        av_step(*pend[0])
                    pend[0] = (j, par, e)
                    yield
            av_step(*pend[0])
            yield

        def normalize(i):
            po = [P.ps(6, (4, 65)), P.ps(7, (4, 65))]
            o = oat[i % 2]
            for g in range(2):
                r_ = rda[:, 4 * ((2 * i + g) % 4):4 * ((2 * i + g) % 4) + 4]
                P.op("dve", lambda e, r_=r_, pg=po[g]: e.reciprocal(out=r_, in_=pg[:, :, 64]), [po[g]], [r_])
                k.tt("dve", o[:, 256 * g:256 * g + 256].rearrange("p (h d) -> p h d", h=4), po[g][:, :, 0:64],
                     r_.unsqueeze(2).broadcast_to([128, 4, 64]), ALU.mult)
            for t in range(4):
                pt = P.ps(nextbank(0, 4), (128,), BF16)
                k.tr(pt, o[:, 128 * t:128 * t + 128], identb)
                k.cp("act", XC[:, t, 128 * i:128 * i + 128], pt)

        def nsteps_bisect(i):
            return 1 if i < 2 else NIT + 1

        def run_interleaved(tasks):
            order = []
            for ti, (g_, n) in enumerate(tasks):
                for q_ in range(n):
                    order.append(((q_ + 0.5) / n, ti))
            order.sort()
            for _, ti in order:
                try:
                    next(tasks[ti][0])
                except StopIteration:
                    pass

        def drain(g_):
            for _ in g_:
                pass

        def nsteps_scores(i):
            return sum(8 for _ in range((128 * (i + 1) + 511) // 512)) + 1
        bis = {}
        for i in range(3):
            drain(g_scores(i))
        for n in (0, 1):
            bis[n] = g_bisect(n)
        drain(bis[0])
        half = lambda n: (nsteps_bisect(n) + 1) // 2
        run_interleaved([(bis[1], half(1))])
        for i in range(NB):
            tasks = []
            if i + 3 < NB:
                tasks.append((g_scores(i + 3), nsteps_scores(i + 3)))
            if i + 2 < NB:
                bis[i + 2] = g_bisect(i + 2)
                tasks.append((bis[i + 2], half(i + 2)))
            if i + 1 < NB:
                tasks.append((bis[i + 1], nsteps_bisect(i + 1) - half(i + 1) + 1))
            tasks.append((g_attend(i), 2 * (i + 1) + 1))
            run_interleaved(tasks)
            if i + 1 < NB:
                drain(bis[i + 1])
            normalize(i)
        if debug and b == 0:
            dump("catT", XC, [128, 8, L], BF16)
        if stop <= 5:
            continue

        A = Arena()
        gl = [A.sb([L], BF16) for _ in range(2)]
        xt = [A.sb([1024], F32) for _ in range(2)]
        hb = [A.sb([1024], F32) for _ in range(2)]
        ob = [A.sb([1024], F32) for _ in range(2)]
        st6 = A.sb([2, 32], F32)
        mv2 = A.sb([2, 32], F32)
        rs = A.sb([2, 32], F32)
        for et in range(2):
            P.dma(XC[:, 4 + et, :], ssc[b, et], s_ssl, after=[(s_ss[0], 16 * P.dma_cnt[s_ss[0]]), (s_ss[1], 16 * P.dma_cnt[s_ss[1]])])
        for kk in range(8):
            P.dma(gl[kk % 2], gsc[b, kk], s_gl[kk % 2], after=[(s_g[0], 16 * P.dma_cnt[s_g[0]])])
            k.tt("dve", XC[:, kk, :], XC[:, kk, :], gl[kk % 2], ALU.mult)
        P.dma(xt[0], xtm[b, 0:128, :], s_xt[0])
        for i in range(NB):
            if i + 1 < NB:
                P.dma(xt[(i + 1) % 2], xtm[b, 128 * (i + 1):128 * (i + 1) + 128, :], s_xt[(i + 1) % 2])
            h_ = hb[i % 2]
            for hf in range(2):
                ps = P.ps(nextbank(0, 8))
                for kk in range(8):
                    k.mm(ps, XC[:, kk, 128 * i:128 * i + 128], woutb[:, kk, 512 * hf:512 * hf + 512], kk == 0, kk == 7)
                k.stt(h_[:, 512 * hf:512 * hf + 512], xt[i % 2][:, 512 * hf:512 * hf + 512], DN_ALPHA, ps, ALU.mult, ALU.add)
                P.op("dve", lambda e, o_=st6[:, i % 2, 6 * hf:6 * hf + 6], a_=h_[:, 512 * hf:512 * hf + 512]: e.bn_stats(out=o_, in_=a_),
                     [h_[:, 512 * hf:512 * hf + 512]], [st6[:, i % 2, 6 * hf:6 * hf + 6]])
            mv = mv2[:, i % 2, 0:2]
            P.op("dve", lambda e, o_=mv, a_=st6[:, i % 2, 0:12]: e.bn_aggr(out=o_, in_=a_),
                 [st6[:, i % 2, 0:12]], [mv])
            r4 = rs[:, i % 2, 0:4]
            k.ts("dve", r4[:, 0:1], mv[:, 1:2], LN_EPS, ALU.add)
            k.act(r4[:, 1:2], r4[:, 0:1], AF.Sqrt)
            P.op("dve", lambda e, o_=r4[:, 2:3], a_=r4[:, 1:2]: e.reciprocal(out=o_, in_=a_), [r4[:, 1:2]], [r4[:, 2:3]])
            k.ts("dve", r4[:, 3:4], mv[:, 0:1], -1.0, ALU.mult, r4[:, 2:3], ALU.mult)
            o_ = ob[i % 2]
            k.act(o_, h_, AF.Identity, bias=r4[:, 3:4], scale=r4[:, 2:3])
            k.tt("dve", o_, o_, lng, ALU.mult)
            k.tt("pool", o_, o_, lnb, ALU.add)
            P.dma(out[b, 128 * i:128 * i + 128, :], o_, s_out[i % 2], is_output=True, eng="pool")

    P.emit()
    return nc, dbg


_CACHE = {}


def kernel(x, mem, w_in, w_mem_kv, lam_re, lam_im, log_dt, b_re, b_im, c_re, c_im, d_skip,
           w_glu, b_glu, w_out, ln_g, ln_b):
    x = np.asarray(x, np.float32)
    mem = np.asarray(mem, np.float32)
    f = lambda a: np.asarray(a, np.float32)
    hw = host_weights(f(w_in), f(w_mem_kv), f(lam_re), f(lam_im), f(log_dt), f(b_re), f(b_im), f(c_re), f(c_im),
                      f(d_skip), f(w_glu), f(b_glu), f(w_out), f(ln_g), f(ln_b))
    hc = host_consts()
    if "nc" not in _CACHE:
        _CACHE["nc"] = build()[0]
    nc = _CACHE["nc"]
    in_maps = []
    for c in range(8):
        xb = x[BPC * c:BPC * c + BPC]
        m = dict(hw)
        m.update(hc)
        m["x"] = np.ascontiguousarray(xb)
        m["xT"] = np.ascontiguousarray(xb.transpose(0, 2, 1))
        m["memT"] = np.ascontiguousarray(mem[BPC * c:BPC * c + BPC].transpose(0, 2, 1))
        in_maps.append(m)
    res = run_bass_kernel_spmd(nc, in_maps, core_ids=list(range(8)))
    return np.concatenate([r["out"] for r in res.results], axis=0).astype(np.float32)
```

```python
import contextlib
import math
import numpy as np
import concourse.bass as bass
import concourse.mybir as mybir
from concourse.bass_utils import run_bass_kernel_spmd

DT = mybir.dt
F32, BF16, I32 = DT.float32, DT.bfloat16, DT.int32
ALU = mybir.AluOpType
AF = mybir.ActivationFunctionType
ESIZE = {F32: 4, BF16: 2, I32: 4}

ENGS = ["pe", "act", "dve", "pool", "sp"]
SB_BYTES = 206 * 1024
PS_BYTES = 16 * 1024
SLOT = 128
NDMA = 60

L = 2048
NB = 16
BPC = 2
TOPK = 256
NBIS = 21
DN_ALPHA = 2.0 ** 0.25
LN_EPS = 1e-5
BIG = 1.0e30
MASKNEG = -30000.0
PI = math.pi


class Prog:
    def __init__(self, nc):
        self.nc = nc
        self.ops = {e: [] for e in ENGS}
        self.kinds = ENGS + ["d%d" % i for i in range(NDMA)]
        self.kidx = {k: i for i, k in enumerate(self.kinds)}
        nk = len(self.kinds)
        self.nslots = {"sb": SB_BYTES // SLOT, "ps": PS_BYTES // SLOT}
        self.wk = {s: np.full(n, -1, np.int64) for s, n in self.nslots.items()}
        self.wv = {s: np.zeros(n, np.int64) for s, n in self.nslots.items()}
        self.rv = {s: np.zeros((nk, n), np.int64) for s, n in self.nslots.items()}
        self.seen = {e: np.zeros(nk, np.int64) for e in ENGS}
        self.dma_cnt = [0] * NDMA
        self.arena = nc.alloc_sbuf_tensor("arena", [128, SB_BYTES // 4], F32)
        self.psum = nc.alloc_psum_tensor("psum_all", [128, PS_BYTES // 4], F32)
        self.sb_off = 0
        self.out_dma = []
        self.nsem = 0
        self._dummy = self.sb([8], F32)
        self._dummy_act = self.sb([8], F32)

    def newsem(self):
        self.nsem += 1
        assert self.nsem <= NDMA
        return self.nsem - 1

    def sb(self, shape, dtype):
        n = int(np.prod(shape)) * ESIZE[dtype]
        n = (n + SLOT - 1) // SLOT * SLOT
        off = self.sb_off
        self.sb_off += n
        assert self.sb_off <= SB_BYTES, "SBUF arena overflow %d" % self.sb_off
        return self.sb_at(off, shape, dtype)

    def sb_at(self, off, shape, dtype):
        assert off % 4 == 0
        nel = int(np.prod(shape))
        nb = nel * ESIZE[dtype]
        assert off + nb <= SB_BYTES, "SBUF arena overflow (at) %d" % (off + nb)
        v = self.arena[:, off // 4:(off + nb + 3) // 4]
        if dtype != F32:
            v = v.bitcast(dtype)[:, 0:nel]
        if len(shape) > 1:
            names = " ".join("a%d" % i for i in range(len(shape)))
            kw = {"a%d" % i: int(s) for i, s in enumerate(shape)}
            v = v.rearrange("p (%s) -> p %s" % (names, names), **kw)
        return v

    def ps(self, bank, shape=(512,), dtype=F32, off=0):
        nel = int(np.prod(shape))
        nb = nel * ESIZE[dtype]
        b0 = bank * 2048 + off
        assert off + nb <= 2048
        v = self.psum[:, b0 // 4:(b0 + nb + 3) // 4]
        if dtype != F32:
            v = v.bitcast(dtype)[:, 0:nel]
        if len(shape) > 1:
            names = " ".join("a%d" % i for i in range(len(shape)))
            kw = {"a%d" % i: int(s) for i, s in enumerate(shape)}
            v = v.rearrange("p (%s) -> p %s" % (names, names), **kw)
        return v

    def _range(self, ap):
        sp = str(ap.space).lower()
        if "sb" in sp or "state" in sp:
            space, pitch = "sb", SB_BYTES
        elif "psum" in sp:
            space, pitch = "ps", PS_BYTES
        else:
            return None
        es = ESIZE[ap.dtype]
        off = (ap.offset * es) % pitch
        ext = 1
        for st, cnt in list(ap.ap)[1:]:
            ext += (cnt - 1) * abs(st)
        lo = off // SLOT
        hi = (off + ext * es - 1) // SLOT + 1
        if space == "ps":
            per = 2048 // SLOT
            lo = lo // per * per
            hi = (hi + per - 1) // per * per
        return space, lo, hi

    def _deps(self, eng, reads, writes):
        need = {}
        myk = self.kidx[eng]
        myseq = len(self.ops[eng]) + 1

        def add(k, v):
            if k < 0 or v <= 0:
                return
            if k == myk:
                if eng == "pe" or v < myseq - 1:
                    return
            if need.get(k, 0) < v:
                need[k] = v
        for ap in reads:
            r = self._range(ap)
            if r is None:
                continue
            s, lo, hi = r
            wk, wv = self.wk[s][lo:hi], self.wv[s][lo:hi]
            for k in np.unique(wk):
                if k >= 0:
                    add(int(k), int(wv[wk == k].max()))
            if s == "ps":
                rm = self.rv[s][:, lo:hi].max(axis=1)
                for k in np.nonzero(rm)[0]:
                    if int(k) != myk:
                        add(int(k), int(rm[k]))
        for ap in writes:
            r = self._range(ap)
            if r is None:
                continue
            s, lo, hi = r
            wk, wv = self.wk[s][lo:hi], self.wv[s][lo:hi]
            for k in np.unique(wk):
                if k >= 0:
                    add(int(k), int(wv[wk == k].max()))
            rm = self.rv[s][:, lo:hi].max(axis=1)
            for k in np.nonzero(rm)[0]:
                add(int(k), int(rm[k]))
        waits = {}
        seen = self.seen[eng]
        for k, v in need.items():
            if k == myk or seen[k] < v:
                waits[k] = v
                if k != myk:
                    seen[k] = v
        return waits

    def _mark(self, kind, val, reads, writes):
        for ap in reads:
            r = self._range(ap)
            if r is None:
                continue
            s, lo, hi = r
            self.rv[s][kind, lo:hi] = val
        for ap in writes:
            r = self._range(ap)
            if r is None:
                continue
            s, lo, hi = r
            self.wk[s][lo:hi] = kind
            self.wv[s][lo:hi] = val
            self.rv[s][:, lo:hi] = 0

    def op(self, eng, fn, reads=(), writes=(), accum=False):
        waits = self._deps(eng, reads, writes)
        self.ops[eng].append((fn, waits, ("accum", eng) if accum else None))
        self._mark(self.kidx[eng], len(self.ops[eng]), reads, writes)

    def dma(self, out, in_, sem, eng="sp", is_output=False, after=(), **kw):
        waits = self._deps(eng, [in_], [out])
        for s_, v_ in after:
            kk_ = self.kidx["d%d" % s_]
            if waits.get(kk_, 0) < v_:
                waits[kk_] = v_
        self.dma_cnt[sem] += 1
        val = 16 * self.dma_cnt[sem]
        fn = lambda e, out=out, in_=in_, kw=kw: e.dma_start(out=out, in_=in_, **kw)
        self.ops[eng].append((fn, waits, sem))
        self._mark(self.kidx["d%d" % sem], val, [in_], [out])
        if is_output:
            self.out_dma.append(sem)

    def emit(self):
        nc = self.nc
        waited = {e: set() for e in ENGS}
        selfw = {e: set() for e in ENGS}
        for e in ENGS:
            for fn, waits, dsem in self.ops[e]:
                for k, v in waits.items():
                    if k < len(ENGS):
                        waited[ENGS[k]].add(v)
                        if ENGS[k] == e:
                            selfw[e].add(v)
        rank = {e: {s: i + 1 for i, s in enumerate(sorted(waited[e]))} for e in ENGS}
        print("ops", {e: len(self.ops[e]) for e in ENGS}, "sem max", {e: len(rank[e]) for e in ENGS}, "dma", max(self.dma_cnt) * 16)
        with contextlib.ExitStack() as st:
            sems = {e: st.enter_context(nc.semaphore("s_" + e)) for e in ENGS}
            dsems = [st.enter_context(nc.semaphore("sd%d" % i)) for i in range(max(self.nsem, 1))]
            block = st.enter_context(nc.Block())

            def run(engname):
                def body(engine):
                    seq = 0
                    for fn, waits, dsem in self.ops[engname]:
                        seq += 1
                        for k, v in sorted(waits.items()):
                            if k < len(ENGS):
                                engine.wait_ge(sems[ENGS[k]], rank[ENGS[k]][v])
                            else:
                                engine.wait_ge(dsems[k - len(ENGS)], v)
                        ins = fn(engine)
                        if isinstance(dsem, tuple):
                            if seq in rank[engname] and seq not in selfw[engname]:
                                ins.then_inc(sems[engname], 1)
                            elif seq in rank[engname]:
                                if engname == "dve":
                                    ins = engine.memset(self._dummy[:, 0:1], 0.0)
                                else:
                                    ins = engine.activation(out=self._dummy_act[:, 0:1], in_=self._dummy_act[:, 1:2], func=AF.Copy)
                                ins.then_inc(sems[engname], 1)
                        elif dsem is not None:
                            ins.then_inc(dsems[dsem], 16)
                        elif seq in rank[engname]:
                            ins.then_inc(sems[engname], 1)
                    if engname == "sp":
                        for s in sorted(set(self.out_dma)):
                            engine.wait_ge(dsems[s], 16 * self.dma_cnt[s])
                return body
            block.tensor(run("pe"))
            block.scalar(run("act"))
            block.vector(run("dve"))
            block.gpsimd(run("pool"))
            block.sync(run("sp"))


def _aps(*xs):
    return [x for x in xs if not isinstance(x, (int, float)) and x is not None]


class K:
    def __init__(self, P):
        self.P = P

    def tt(self, eng, out, a, b, op):
        self.P.op(eng, lambda e: e.tensor_tensor(out=out, in0=a, in1=b, op=op), [a, b], [out])

    def ts(self, eng, out, a, s1, op0, s2=None, op1=None, accum_out=None):
        def fn(e):
            if op1 is None:
                return e.tensor_scalar(out=out, in0=a, scalar1=s1, scalar2=None, op0=op0)
            if accum_out is not None:
                return e.tensor_scalar(out=out, in0=a, scalar1=s1, scalar2=s2, op0=op0, op1=op1, accum_out=accum_out)
            return e.tensor_scalar(out=out, in0=a, scalar1=s1, scalar2=s2, op0=op0, op1=op1)
        self.P.op(eng, fn, [a] + _aps(s1, s2), [out] + _aps(accum_out), accum=accum_out is not None)

    def stt(self, out, a, s, b, op0, op1):
        self.P.op("dve", lambda e: e.scalar_tensor_tensor(out=out, in0=a, scalar=s, in1=b, op0=op0, op1=op1),
                  [a, b] + _aps(s), [out])

    def act(self, out, a, func, bias=None, scale=1.0, accum_out=None):
        def fn(e):
            kw = {}
            if bias is not None:
                kw["bias"] = bias
            if accum_out is not None:
                kw["accum_out"] = accum_out
            return e.activation(out=out, in_=a, func=func, scale=scale, **kw)
        self.P.op("act", fn, [a] + _aps(bias, scale), [out] + _aps(accum_out), accum=accum_out is not None)

    def cp(self, eng, out, a):
        if eng == "act":
            self.P.op("act", lambda e: e.activation(out=out, in_=a, func=AF.Copy), [a], [out])
        else:
            self.P.op(eng, lambda e: e.tensor_copy(out=out, in_=a), [a], [out])

    def memset(self, eng, out, val):
        self.P.op(eng, lambda e: e.memset(out, val), [], [out])

    def mm(self, out, lhsT, rhs, start, stop, skip=False):
        self.P.op("pe", lambda e: e.matmul(out, lhsT=lhsT, rhs=rhs, start=start, stop=stop, skip_group_check=skip),
                  [lhsT, rhs], [out])

    def tr(self, out, a, ident):
        self.P.op("pe", lambda e: e.transpose(out, a, ident), [a, ident], [out])


SPLITS = [512, 128, 128, 256, 32, 8, 512, 256, 256, 256, 256]
NFM = 21


def _swap_perm(width, hd, half):
    perm = np.arange(width)
    for c in range(width):
        d = c % hd
        if d < half:
            perm[c] = c + half
        elif d < 2 * half:
            perm[c] = c - half
    return perm


def _rope_tables(hd, reps):
    r = hd // 4
    half = r // 2
    inv = (np.float32(500000.0) ** (-np.arange(0, half, dtype=np.float32) * np.float32(2.0) / np.float32(r))).astype(np.float32)
    pos = np.arange(L, dtype=np.float32)
    ang = (pos[:, None] * inv[None, :]).astype(np.float32)
    cos = np.cos(ang).astype(np.float32).T
    sin = np.sin(ang).astype(np.float32).T
    C = np.ones((hd, L), np.float32)
    S = np.zeros((hd, L), np.float32)
    C[0:half] = cos
    C[half:2 * half] = cos
    S[0:half] = -sin
    S[half:2 * half] = sin
    return np.tile(C, (reps, 1)), np.tile(S, (reps, 1))


def host_consts():
    c64, s64 = _rope_tables(64, 2)
    c32, s32 = _rope_tables(32, 4)
    ident = np.eye(128, dtype=np.float32)
    causal = np.where(np.arange(128)[None, :] <= np.arange(128)[:, None], 0.0, -BIG).astype(np.float32)
    chix = np.ascontiguousarray(np.repeat(np.arange(256, dtype=np.float32)[None, :], 128, axis=0))
    pm64 = np.zeros((128, 128), np.float32)
    pm64[_swap_perm(128, 64, 8), np.arange(128)] = 1.0
    pm32 = np.zeros((128, 128), np.float32)
    pm32[_swap_perm(128, 32, 4), np.arange(128)] = 1.0
    vmask = (np.arange(128)[:, None] // 32 == np.arange(4)[None, :]).astype(np.float32)
    return dict(c64=c64, s64=s64, c32=c32, s32=s32, ident=ident, causal=causal, chix=chix, pm64=pm64, pm32=pm32, vmask=vmask)


def host_weights(w_in, w_mem_kv, lam_re, lam_im, log_dt, b_re, b_im, c_re, c_im, d_skip, w_glu, b_glu, w_out, ln_g, ln_b):
    sp = np.concatenate([[0], np.cumsum(SPLITS)])
    col = lambda i: w_in[:, sp[i]:sp[i + 1]]
    q_c, k_c, v_c, qi_c, ki_c, wi_c, gatt_c, u_c, gssm_c, qm_c, gmem_c = [col(i) for i in range(11)]
    p64_512 = _swap_perm(512, 64, 8)
    p64_128 = _swap_perm(128, 64, 8)
    p32_256 = _swap_perm(256, 32, 4)
    p32_32 = _swap_perm(32, 32, 4)
    tiles = []
    for t in range(2):
        tiles += [qi_c[:, 128 * t:128 * t + 128]]
    tiles += [np.concatenate([ki_c] * 4, axis=1)]
    for t in range(4):
        tiles += [q_c[:, 128 * t:128 * t + 128]]
    for g in range(2):
        tiles += [np.concatenate([k_c[:, 64 * g:64 * g + 64]] * 2, axis=1)]
    tiles += [u_c[:, 0:128], u_c[:, 128:256], qm_c[:, 0:128], qm_c[:, 128:256]]
    gates = np.concatenate([gatt_c, gssm_c, gmem_c], axis=1)
    for t in range(8):
        tiles.append(gates[:, 128 * t:128 * t + 128])
    assert len(tiles) == NFM
    wfm = np.ascontiguousarray(np.stack(tiles, 0)).astype(np.float32)
    wtm = np.ascontiguousarray(np.concatenate([v_c, wi_c], axis=1)).astype(np.float32)
    bre = np.zeros((8, 128, 128), np.float32)
    bim = np.zeros((8, 128, 128), np.float32)
    cre = np.zeros((8, 128, 128), np.float32)
    cim = np.zeros((8, 128, 128), np.float32)
    for i in range(8):
        for gl in range(2):
            g = 2 * i + gl
            g8 = g % 8
            bre[i, 16 * g8:16 * g8 + 16, 64 * gl:64 * gl + 64] = b_re[g].T
            bim[i, 16 * g8:16 * g8 + 16, 64 * gl:64 * gl + 64] = b_im[g].T
            cre[i, 64 * gl:64 * gl + 64, 16 * g8:16 * g8 + 16] = c_re[g].T
            cim[i, 64 * gl:64 * gl + 64, 16 * g8:16 * g8 + 16] = c_im[g].T
    dblk = np.zeros((2, 128, 128), np.float32)
    dflat = d_skip.reshape(256)
    for ct in range(2):
        dblk[ct][np.arange(128), np.arange(128)] = dflat[128 * ct:128 * ct + 128]

    def st(a):
        return np.ascontiguousarray(a.reshape(8, 2, 64).transpose(1, 2, 0).reshape(128, 8)).astype(np.float32)
    lamre = st(lam_re)
    lamim = st(lam_im)
    logdt = st(np.repeat(log_dt[:, None], 64, axis=1))
    bglu = np.ascontiguousarray(b_glu.reshape(2, 128).T).astype(np.float32)
    lng = np.ascontiguousarray(np.repeat(ln_g[None, :], 128, axis=0)).astype(np.float32)
    lnb = np.ascontiguousarray(np.repeat(ln_b[None, :], 128, axis=0)).astype(np.float32)
    bre = np.ascontiguousarray(bre.transpose(0, 2, 1))
    bim = np.ascontiguousarray(bim.transpose(0, 2, 1))
    return dict(wfm=wfm, wtm=wtm, wmem=np.ascontiguousarray(w_mem_kv), wglu=np.ascontiguousarray(w_glu),
                wout=np.ascontiguousarray(w_out), bre=bre, bim=bim, cre=cre, cim=cim, dblk=dblk,
                lamre=lamre, lamim=lamim, logdt=logdt, bglu=bglu, lng=lng, lnb=lnb)


def build(stop=99, debug=False):
    nc = bass.Bass("TRN2", target_bir_lowering=False)

    def din(name, shape):
        return nc.dram_tensor(name, list(shape), F32, kind="ExternalInput").ap()
    xT = din("xT", [BPC, 1024, L])
    xtm = din("x", [BPC, L, 1024])
    memT = din("memT", [BPC, 1024, 256])
    wfm = din("wfm", [NFM, 1024, 128])
    wtm = din("wtm", [1024, 136])
    wmem = din("wmem", [1024, 512])
    wglu = din("wglu", [256, 256])
    wout = din("wout", [1024, 1024])
    d_c64, d_s64, d_c32, d_s32 = [din(n, [128, L]) for n in ("c64", "s64", "c32", "s32")]
    d_bre, d_bim, d_cre, d_cim = [din(n, [8, 128, 128]) for n in ("bre", "bim", "cre", "cim")]
    d_dblk = din("dblk", [2, 128, 128])
    d_lamre, d_lamim, d_logdt = [din(n, [128, 8]) for n in ("lamre", "lamim", "logdt")]
    d_bglu = din("bglu", [128, 2])
    d_lng = din("lng", [128, 1024])
    d_lnb = din("lnb", [128, 1024])
    d_ident = din("ident", [128, 128])
    d_causal = din("causal", [128, 128])
    d_chix = din("chix", [128, 256])
    d_pm64 = din("pm64", [128, 128])
    d_pm32 = din("pm32", [128, 128])
    d_vmask = din("vmask", [128, 4])
    out = nc.dram_tensor("out", [BPC, L, 1024], F32, kind="ExternalOutput").ap()
    gsc = nc.dram_tensor("gsc", [BPC, 8, 128, L], BF16).ap()
    ssc = nc.dram_tensor("ssc", [BPC, 2, 128, L], BF16).ap()

    P = Prog(nc)
    k = K(P)
    dbg = {}

    def dump(name, src, shape, dtype=F32):
        if not debug:
            return
        t = nc.dram_tensor("dbg_" + name, list(shape), dtype, kind="ExternalOutput").ap()
        P.dma(t, src, P.newsem(), is_output=True)
        dbg[name] = (shape, dtype)

    identf = P.sb([128], F32)
    identb = P.sb([128], BF16)
    ident4 = P.sb([4, 128], BF16)
    causal = P.sb([128], F32)
    pmb = P.sb([2, 128], BF16)
    vmask = P.sb([4], F32)
    wglub = P.sb([2, 256], BF16)
    bglu = P.sb([2], F32)
    wtmb = P.sb([8, 136], BF16)
    sc8 = {n: P.sb([8], F32) for n in ("lre", "lim", "ldt", "dt", "a", "th", "rho", "t1", "t2", "sin", "cos", "lbr", "lbi",
                                       "nre", "nim", "den", "kre", "kim", "nkim", "u1", "u2", "r8")}
    sc8i = P.sb([8], I32)
    mure = P.sb([11, 8], F32)
    muim = P.sb([11, 8], F32)
    nmuim = P.sb([11, 8], F32)
    uT = P.sb([2, L], BF16)
    PB0 = P.sb_off
    qT = P.sb([4, L], BF16)
    kT = P.sb([2, L], BF16)
    qiT = P.sb([2, L], BF16)
    kiT4 = P.sb([4, L], BF16)
    vaug = P.sb([NB, 2, 65], BF16)
    absw = P.sb([NB, 8], F32)
    sgnw = P.sb([NB, 8], F32)
    qmT = P.sb([2, L], BF16)
    mkT = P.sb([2, 256], BF16)
    mvaug = P.sb([2, 4, 65], BF16)
    woutb = P.sb([8, 1024], BF16)
    lng = P.sb([1024], F32)
    lnb = P.sb([1024], F32)
    PB1 = P.sb_off
    XC = P.sb([8, L], BF16)
    small = P.sb([64], F32)
    ARENA0 = P.sb_off
    ARENA_SZ = SB_BYTES - ARENA0
    print("resident bytes", ARENA0, "arena", ARENA_SZ)

    class Arena:
        def __init__(self):
            self.off = ARENA0

        def sb(self, shape, dtype):
            n = int(np.prod(shape)) * ESIZE[dtype]
            n = (n + SLOT - 1) // SLOT * SLOT
            o = self.off
            self.off += n
            assert self.off <= SB_BYTES, "phase arena overflow %d" % (self.off - ARENA0)
            return P.sb_at(o, shape, dtype)

    A = Arena()
    stg = A.sb([8, 1024], F32)
    P.dma(identf, d_ident, P.newsem())
    P.dma(causal, d_causal, P.newsem())
    P.dma(bglu, d_bglu, P.newsem())
    for n, d in (("lre", d_lamre), ("lim", d_lamim), ("ldt", d_logdt)):
        P.dma(sc8[n], d, P.newsem())
    k.cp("dve", identb, identf)
    P.dma(vmask, d_vmask, P.newsem())
    for q_, dsrc in enumerate((d_pm32, d_pm64)):
        P.dma(stg[:, q_, 0:128], dsrc, P.newsem())
        k.cp("dve", pmb[:, q_, :], stg[:, q_, 0:128])
    for r in range(4):
        k.cp("dve", ident4[:, r, :], identf)
    s_stg = P.newsem()
    v = stg[:, 0, 0:512].rearrange("p (i c) -> p i c", i=2)
    P.dma(v, wglu.rearrange("(k p) c -> p k c", p=128), s_stg)
    k.cp("dve", wglub, v)
    v = stg[:, 0:2, :].rearrange("p a b -> p (a b)")[:, 0:8 * 136].rearrange("p (i c) -> p i c", i=8)
    P.dma(v, wtm.rearrange("(k p) c -> p k c", p=128), s_stg)
    k.cp("dve", wtmb, v)

    s = sc8
    k.act(s["dt"], s["ldt"], AF.Exp)
    k.tt("dve", s["a"], s["lre"], s["dt"], ALU.mult)
    k.tt("dve", s["th"], s["lim"], s["dt"], ALU.mult)
    k.act(s["rho"], s["a"], AF.Exp)

    def sin_of(dst, src, shift):
        k.ts("dve", s["t1"], src, shift, ALU.add, 1.0 / (2 * PI), ALU.mult)
        k.cp("dve", sc8i, s["t1"])
        k.cp("dve", s["t2"], sc8i)
        k.ts("dve", s["t1"], src, shift, ALU.add)
        k.stt(s["t1"], s["t2"], -2 * PI, s["t1"], ALU.mult, ALU.add)
        k.ts("dve", s["t1"], s["t1"], 3.1415925, ALU.min, -3.1415925, ALU.max)
        k.act(dst, s["t1"], AF.Sin)
    sin_of(s["sin"], s["th"], 0.0)
    sin_of(s["cos"], s["th"], PI / 2)
    k.tt("dve", s["lbr"], s["rho"], s["cos"], ALU.mult)
    k.tt("dve", s["lbi"], s["rho"], s["sin"], ALU.mult)
    k.ts("dve", s["t1"], s["lbr"], -1.0, ALU.add)
    k.tt("dve", s["nre"], s["t1"], s["lre"], ALU.mult)
    k.tt("dve", s["u1"], s["lbi"], s["lim"], ALU.mult)
    k.tt("dve", s["nre"], s["nre"], s["u1"], ALU.add)
    k.tt("dve", s["nim"], s["lbi"], s["lre"], ALU.mult)
    k.tt("dve", s["u1"], s["t1"], s["lim"], ALU.mult)
    k.tt("dve", s["nim"], s["nim"], s["u1"], ALU.subtract)
    k.tt("dve", s["den"], s["lre"], s["lre"], ALU.mult)
    k.tt("dve", s["u1"], s["lim"], s["lim"], ALU.mult)
    k.tt("dve", s["den"], s["den"], s["u1"], ALU.add)
    P.op("dve", lambda e: e.reciprocal(out=s["u2"], in_=s["den"]), [s["den"]], [s["u2"]])
    k.tt("dve", s["kre"], s["nre"], s["u2"], ALU.mult)
    k.tt("dve", s["kim"], s["nim"], s["u2"], ALU.mult)
    k.ts("dve", s["nkim"], s["kim"], -1.0, ALU.mult)
    k.cp("dve", mure[:, 0, :], s["lbr"])
    k.cp("dve", muim[:, 0, :], s["lbi"])
    for lv in range(1, 11):
        k.tt("dve", s["u1"], mure[:, lv - 1, :], mure[:, lv - 1, :], ALU.mult)
        k.tt("dve", s["u2"], muim[:, lv - 1, :], muim[:, lv - 1, :], ALU.mult)
        k.tt("dve", mure[:, lv, :], s["u1"], s["u2"], ALU.subtract)
        k.tt("dve", s["u1"], mure[:, lv - 1, :], muim[:, lv - 1, :], ALU.mult)
        k.ts("dve", muim[:, lv, :], s["u1"], 2.0, ALU.mult)
    k.ts("dve", nmuim, muim, -1.0, ALU.mult)
    if debug:
        dump("mure", mure, [128, 11, 8])
        dump("muim", muim, [128, 11, 8])
        dump("kre", s["kre"], [128, 8])
        dump("kim", s["kim"], [128, 8])

    s_x = P.newsem()
    s_x2 = P.newsem()
    s_tab = [P.newsem() for _ in range(2)]
    s_w = [P.newsem() for _ in range(2)]
    s_g = [P.newsem() for _ in range(2)]
    s_misc = P.newsem()
    s_misc2 = P.newsem()
    s_out = [P.newsem() for _ in range(2)]
    s_xt = [P.newsem() for _ in range(2)]
    s_gl = [P.newsem() for _ in range(2)]
    psrot = [0]

    def nextbank(lo=0, hi=8):
        b = lo + psrot[0] % (hi - lo)
        psrot[0] += 1
        return b


    T8 = 8
    NCH = L // T8
    s_ss = [P.newsem() for _ in range(2)]
    s_ssl = P.newsem()
    s_bt = [P.newsem() for _ in range(5)]
    s_xs = P.newsem()
    s_ws = [P.newsem() for _ in range(2)]
    W1 = P.sb_at(PB0, [8, T8, 2, 128], BF16)
    W2 = P.sb_at(PB0 + 32768, [8, T8, 2, 128], BF16)
    Kt = P.sb_at(PB0 + 65536, [2, T8, 128], BF16)
    assert PB0 + 65536 + 4096 <= PB1
    XS_OFF = SB_BYTES - 8192
    xstS = P.sb_at(XS_OFF, [L], F32)

    def xload_S(bb):
        for kk in range(8):
            P.dma(xstS, xT[bb, 128 * kk:128 * kk + 128, :], s_xs)
            k.cp("dve" if kk % 2 == 0 else "act", XC[:, kk, :], xstS)
    xload_S(0)
    A = Arena()
    BTr = A.sb([8, 128], F32)
    BTi = A.sb([8, 128], F32)
    CTr = A.sb([8, 128], F32)
    CTi = A.sb([8, 128], F32)
    nCTi = A.sb([8, 128], F32)
    dbf = A.sb([2, 128], F32)
    bts = [[A.sb([128], F32) for _ in range(2)] for _ in range(2)]
    w2t = [A.sb([128], F32) for _ in range(2)]
    LKr, LKi, nLKr, nLKi = [A.sb([T8 + 1, 8], F32) for _ in range(4)]
    GKr, GKi, nGKi = [A.sb([T8, 8], F32) for _ in range(3)]
    for dsrc, dst, sm in ((d_bre, BTr, 0), (d_bim, BTi, 1), (d_cre, CTr, 2), (d_cim, CTi, 3)):
        P.dma(dst, dsrc.rearrange("i p c -> p i c"), s_bt[sm])
    P.dma(dbf, d_dblk.rearrange("i p c -> p i c"), s_bt[4])
    k.ts("pool", nCTi, CTi, -1.0, ALU.mult, 0.0, ALU.add)
    k.cp("dve", LKr[:, 1, :], s["lbr"])
    k.cp("dve", LKi[:, 1, :], s["lbi"])
    for kk in range(2, T8 + 1):
        k.tt("dve", s["u1"], LKr[:, kk - 1, :], s["lbr"], ALU.mult)
        k.tt("dve", s["u2"], LKi[:, kk - 1, :], s["lbi"], ALU.mult)
        k.tt("dve", LKr[:, kk, :], s["u1"], s["u2"], ALU.subtract)
        k.tt("dve", s["u1"], LKr[:, kk - 1, :], s["lbi"], ALU.mult)
        k.tt("dve", s["u2"], LKi[:, kk - 1, :], s["lbr"], ALU.mult)
        k.tt("dve", LKi[:, kk, :], s["u1"], s["u2"], ALU.add)
    k.ts("dve", nLKr[:, 1:, :], LKr[:, 1:, :], -1.0, ALU.mult)
    k.ts("dve", nLKi[:, 1:, :], LKi[:, 1:, :], -1.0, ALU.mult)
    k.cp("dve", GKr[:, 0, :], s["kre"])
    k.cp("dve", GKi[:, 0, :], s["kim"])
    for kk in range(1, T8):
        k.tt("dve", s["u1"], LKr[:, kk, :], s["kre"], ALU.mult)
        k.tt("dve", s["u2"], LKi[:, kk, :], s["kim"], ALU.mult)
        k.tt("dve", GKr[:, kk, :], s["u1"], s["u2"], ALU.subtract)
        k.tt("dve", s["u1"], LKr[:, kk, :], s["kim"], ALU.mult)
        k.tt("dve", s["u2"], LKi[:, kk, :], s["kre"], ALU.mult)
        k.tt("dve", GKi[:, kk, :], s["u1"], s["u2"], ALU.add)
    k.ts("dve", nGKi, GKi, -1.0, ALU.mult)
    COST = P.sb_at(PB0 + 65536 + 4096, [8, NCH], F32)
    SINT = P.sb_at(PB0 + 65536 + 4096 + 8192, [8, NCH], F32)
    assert PB0 + 65536 + 4096 + 16384 <= PB1
    r8 = s["r8"]
    ff = A.sb([8], F32)
    ffi = A.sb([8], I32)
    chix = A.sb([NCH], F32)
    Gt = A.sb([8, NCH], F32)
    Gi = A.sb([8, NCH], I32)
    Gf = A.sb([8, NCH], F32)
    P.dma(chix, d_chix, P.newsem())
    k.act(r8, s["a"], AF.Exp, scale=float(T8))
    k.ts("dve", ff, s["th"], float(T8) / (2 * PI), ALU.mult)
    k.cp("dve", ffi, ff)
    k.cp("dve", s["u1"], ffi)
    k.tt("dve", ff, ff, s["u1"], ALU.subtract)
    k.tt("dve", Gt, ff.unsqueeze(2).broadcast_to([128, 8, NCH]), chix.unsqueeze(1).broadcast_to([128, 8, NCH]), ALU.mult)
    for shift, dstT in ((0.0, SINT), (0.25, COST)):
        src = Gt
        if shift != 0.0:
            k.ts("dve", Gf, Gt, shift, ALU.add)
            src = Gf
        k.cp("dve", Gi, src)
        k.cp("pool", dstT, Gi)
        k.tt("dve", dstT, src, dstT, ALU.subtract)
        k.ts("dve", dstT, dstT, 2 * PI, ALU.mult, 3.1415925, ALU.min)
        k.ts("dve", dstT, dstT, -3.1415925, ALU.max)
        k.act(dstT, dstT, AF.Sin)
    nb_ = 0
    for kk in range(T8):
        for ct in range(2):
            kp = P.ps(ct, (128,))
            for i in range(4 * ct, 4 * ct + 4):
                br_, bi_ = bts[nb_ % 2]
                nb_ += 1
                gr = GKr[:, kk, i:i + 1]
                gi = GKi[:, kk, i:i + 1]
                ngi = nGKi[:, kk, i:i + 1]
                k.ts("dve", br_, BTr[:, i, :], gr, ALU.mult)
                k.stt(br_, BTi[:, i, :], ngi, br_, ALU.mult, ALU.add)
                k.ts("dve", bi_, BTr[:, i, :], gi, ALU.mult)
                k.stt(bi_, BTi[:, i, :], gr, bi_, ALU.mult, ALU.add)
                for x_, src in ((0, br_), (1, bi_)):
                    pt = P.ps(4 + (2 * nb_ + x_) % 4, (128,))
                    k.tr(pt, src, identf)
                    k.cp("act", W1[:, i, T8 - 1 - kk, x_, :], pt)
                k.mm(kp, br_, CTr[:, i, :], i == 4 * ct, False)
                k.mm(kp, bi_, nCTi[:, i, :], False, i == 4 * ct + 3)
            if kk == 0:
                k.tt("dve", Kt[:, ct, kk, :], kp, dbf[:, ct, :], ALU.add)
            else:
                k.cp("act", Kt[:, ct, kk, :], kp)
    for i in range(8):
        for j in range(T8):
            lr = LKr[:, j + 1, i:i + 1]
            nli = nLKi[:, j + 1, i:i + 1]
            nlr = nLKr[:, j + 1, i:i + 1]
            t_ = w2t[(i * T8 + j) % 2]
            k.ts("pool", t_, CTr[:, i, :], lr, ALU.mult, 0.0, ALU.add)
            k.stt(W2[:, i, j, 0, :], CTi[:, i, :], nli, t_, ALU.mult, ALU.add)
            t2_ = w2t[(i * T8 + j + 1) % 2]
            k.act(t2_, CTr[:, i, :], AF.Copy, scale=nli)
            k.stt(W2[:, i, j, 1, :], CTi[:, i, :], nlr, t2_, ALU.mult, ALU.add)

    import os
    PHS = int(os.environ.get("PHS_VARIANT", "99"))
    assert A.off <= XS_OFF, "setup transients overlap the x staging"
    A = Arena()
    wstS = [A.sb([8, 128], F32) for _ in range(2)]
    wbfS = [A.sb([8, 128], BF16) for _ in range(2)]
    Lr = A.sb([8, NCH], F32)
    Li = A.sb([8, NCH], F32)
    Br = A.sb([8, NCH], F32)
    Bi = A.sb([8, NCH], F32)
    Spr = A.sb([8, NCH], BF16)
    Spi = A.sb([8, NCH], BF16)
    uTd = A.sb([2, T8, NCH], BF16)
    assert A.off <= XS_OFF, "phase-S arena overlaps the x staging"
    ygb = P.sb_at(XS_OFF, [2, L], BF16)
    osb = [P.sb_at(ARENA0 + 4096 * q_, [L], BF16) for q_ in range(2)]
    k.memset("pool", Spr[:, :, 0:1], 0.0)
    k.memset("pool", Spi[:, :, 0:1], 0.0)
    for b in range(BPC if PHS > 10 else 0):
        for ct in range(2):
            P.dma(wstS[ct], wfm[9 + ct].rearrange("(k p) c -> p k c", p=128), s_ws[ct])
            k.cp("act", wbfS[ct], wstS[ct])
            for c in range(4):
                ps = P.ps(nextbank(0, 4))
                for kk in range(8):
                    k.mm(ps, wbfS[ct][:, kk, :], XC[:, kk, 512 * c:512 * c + 512], kk == 0, kk == 7)
                k.cp("act", uT[:, ct, 512 * c:512 * c + 512], ps)
        xload_S(b + 1 if b + 1 < BPC else 0)
        for ct in range(2):
            k.cp("dve", uTd[:, ct, :, :], uT[:, ct, :].rearrange("p (c j) -> p j c", j=T8))
        for i in range(8):
            ct = i // 4
            for x_, dstL in ((0, Lr), (1, Li)):
                q_ = 2 * i + x_
                psL = P.ps(4 + q_ % 4, (NCH,))
                for j in range(T8):
                    k.mm(psL, W1[:, i, j, x_, :], uTd[:, ct, j, :], j == 0, j == T8 - 1, skip=True)
                k.cp("act", dstL[:, i, :], psL)
        T1, T2 = Br, Bi
        k.tt("dve", T1, COST, Lr, ALU.mult)
        k.tt("dve", T2, SINT, Li, ALU.mult)
        k.tt("dve", T1, T1, T2, ALU.add)
        k.tt("dve", T2, SINT, Lr, ALU.mult)
        k.tt("dve", Li, COST, Li, ALU.mult)
        k.tt("dve", Li, Li, T2, ALU.subtract)
        for i in range(8):
            d0 = r8[:, i:i + 1].broadcast_to([128, NCH])
            P.op("dve", lambda e, o_=Lr[:, i, :], d0=d0, d1=T1[:, i, :]: e.tensor_tensor_scan(out=o_, data0=d0, data1=d1, initial=0.0, op0=ALU.mult, op1=ALU.add),
                 [T1[:, i, :], r8], [Lr[:, i, :]])
            P.op("dve", lambda e, o_=T2[:, i, :], d0=d0, d1=Li[:, i, :]: e.tensor_tensor_scan(out=o_, data0=d0, data1=d1, initial=0.0, op0=ALU.mult, op1=ALU.add),
                 [Li[:, i, :], r8], [T2[:, i, :]])
        k.tt("dve", T1, COST, Lr, ALU.mult)
        k.tt("dve", Li, SINT, T2, ALU.mult)
        k.tt("dve", T1, T1, Li, ALU.subtract)
        k.tt("dve", Li, SINT, Lr, ALU.mult)
        k.tt("dve", T2, COST, T2, ALU.mult)
        k.tt("dve", Li, Li, T2, ALU.add)
        for i in range(8):
            k.cp("act", Spr[:, i, 1:NCH], T1[:, i, 0:NCH - 1])
            k.cp("act", Spi[:, i, 1:NCH], Li[:, i, 0:NCH - 1])
        if PHS <= 20:
            continue
        y = P.sb_at(ARENA0 + 12288, [L], F32)
        t1 = P.sb_at(ARENA0 + 12288 + 8192, [L], F32)
        t2 = P.sb_at(ARENA0 + 12288 + 16384, [L], F32)
        sg = P.sb_at(ARENA0 + 12288 + 24576, [L], F32)
        for ct in range(2):
            for c4 in range(4):
                yp = P.ps(c4, (64, T8))
                uv = uT[:, ct, 512 * c4:512 * c4 + 512].rearrange("p (c j) -> p c j", j=T8)
                for tau in range(T8):
                    k.mm(yp[:, :, tau:T8], Kt[:, ct, tau, :], uv[:, :, 0:T8 - tau], tau == 0, False, skip=True)
                for i in range(4 * ct, 4 * ct + 4):
                    for j in range(T8):
                        k.mm(yp[:, :, j], W2[:, i, j, 0, :], Spr[:, i, 64 * c4:64 * c4 + 64], False, False, skip=True)
                        k.mm(yp[:, :, j], W2[:, i, j, 1, :], Spi[:, i, 64 * c4:64 * c4 + 64], False,
                             (i == 4 * ct + 3 and j == T8 - 1), skip=True)
            for c in range(4):
                sl = slice(512 * c, 512 * c + 512)
                k.cp("act", y[:, sl], P.ps(c))
            if debug and b == 0:
                dump("y%d" % ct, y, [128, L])
            k.tt("dve", t1, y, y, ALU.mult)
            k.ts("dve", t1, t1, 0.044715, ALU.mult, 1.0, ALU.add)
            k.tt("dve", t1, t1, y, ALU.mult)
            k.act(t2, t1, AF.Sigmoid, scale=2.0 * 0.7978845608028654)
            k.tt("dve", ygb[:, ct, :], y, t2, ALU.mult)
        for et in range(2):
            for c in range(4):
                sl = slice(512 * c, 512 * c + 512)
                ps = P.ps(nextbank(4, 8))
                for kc in range(2):
                    k.mm(ps, wglub[:, kc, 128 * et:128 * et + 128], ygb[:, kc, sl], kc == 0, kc == 1)
                k.act(sg[:, sl], ps, AF.Sigmoid, bias=bglu[:, et:et + 1])
                k.tt("dve", osb[et][:, sl], ygb[:, et, sl], sg[:, sl], ALU.mult)
            P.dma(ssc[b, et], osb[et], s_ss[et])
    A = Arena()
    stg = A.sb([8, 1024], F32)
    P.dma(lng, d_lng, P.newsem())
    P.dma(lnb, d_lnb, P.newsem())
    P.dma(stg, wout.rearrange("(k p) c -> p k c", p=128), s_stg)
    for kk in range(8):
        k.cp("dve" if kk % 2 == 0 else "act", woutb[:, kk, :], stg[:, kk, :])

    for b in range(BPC):
        A = Arena()
        xst = A.sb([L], F32)
        xst2 = A.sb([L], F32)
        wst = [A.sb([8, 128], F32) for _ in range(2)]
        wbf = [A.sb([8, 128], BF16) for _ in range(2)]
        tabC = A.sb([L], F32)
        tabS = A.sb([L], F32)
        tmpA = [A.sb([512], F32) for _ in range(3)]
        zbf = [A.sb([512], BF16) for _ in range(3)]
        kirT = A.sb([L], BF16)
        tmp2 = [A.sb([512], F32) for _ in range(2)]
        gst = [A.sb([L], BF16) for _ in range(1)]
        for kk in range(8 if b > 0 else 0):
            xs_ = xst if kk % 2 == 0 else xst2
            P.dma(xs_, xT[b, 128 * kk:128 * kk + 128, :], s_x if kk % 2 == 0 else s_x2)
            k.cp("dve" if kk % 2 == 0 else "act", XC[:, kk, :], xs_)

        widx = {}

        def load_w(m):
            widx[m] = len(widx) % 2
            P.dma(wst[widx[m]], wfm[m].rearrange("(k p) c -> p k c", p=128), s_w[widx[m]])
            k.cp("act", wbf[widx[m]], wst[widx[m]])

        def proj(m, c):
            bank = nextbank()
            ps = P.ps(bank)
            for kk in range(8):
                k.mm(ps, wbf[widx[m]][:, kk, :], XC[:, kk, 512 * c:512 * c + 512], kk == 0, kk == 7)
            return ps
        dests = {}
        for t in range(2):
            dests[t] = ("rope", qiT[:, t, :], 0)
        dests[2] = ("rope", kirT, 0)
        for t in range(4):
            dests[3 + t] = ("rope", qT[:, t, :], 1)
        for g in range(2):
            dests[7 + g] = ("rope", kT[:, g, :], 1)
        dests[11] = ("plain", qmT[:, 0, :], None)
        dests[12] = ("plain", qmT[:, 1, :], None)
        for t in range(8):
            dests[13 + t] = ("gate", t, None)
        order = [m for m in range(NFM) if m not in (9, 10)]
        nta = 0
        pend = [None]
        for oi, m in enumerate(order):
            if m == 0:
                P.dma(tabC, d_c32, s_tab[0])
                P.dma(tabS, d_s32, s_tab[1])
                load_w(0)
            if m == 3:
                if pend[0] is not None:
                    pend[0]()
                    pend[0] = None
                for v_ in range(4):
                    k.ts("dve", kiT4[:, v_, :], kirT, vmask[:, v_:v_ + 1], ALU.mult)
                P.dma(tabC, d_c64, s_tab[0])
                P.dma(tabS, d_s64, s_tab[1])
            if oi + 1 < len(order):
                load_w(order[oi + 1])
            kind, dst, pq = dests[m]
            if kind == "rope":
                for c in range(4):
                    sl = slice(512 * c, 512 * c + 512)
                    ps = proj(m, c)
                    if pend[0] is not None:
                        pend[0]()
                        pend[0] = None
                    ta = tmpA[nta % 3][:, 0:512]
                    zb = zbf[nta % 3]
                    nta += 1
                    k.cp("act", zb, ps)
                    k.tt("dve", ta, ps, tabC[:, sl], ALU.mult)

                    def fin(zb=zb, ta=ta, sl=sl, dst=dst, pq=pq, c=c):
                        ps2 = P.ps(nextbank())
                        k.mm(ps2, pmb[:, pq, :], zb, True, True)
                        t2 = tmp2[c % 2]
                        k.tt("dve", t2, ps2, tabS[:, sl], ALU.mult)
                        k.tt("pool", dst[:, sl], t2, ta, ALU.add)
                    pend[0] = fin
            elif kind == "plain":
                for c in range(4):
                    ps = proj(m, c)
                    if pend[0] is not None:
                        pend[0]()
                        pend[0] = None
                    k.cp("act", dst[:, 512 * c:512 * c + 512], ps)
            else:
                gt = gst[0]
                for c in range(4):
                    ps = proj(m, c)
                    k.act(gt[:, 512 * c:512 * c + 512], ps, AF.Silu)
                P.dma(gsc[b, dst], gt, s_g[0], eng="act")
        for i in range(NB):
            bank = nextbank()
            ps = P.ps(bank, (136,))
            for kk in range(8):
                k.mm(ps, XC[:, kk, 128 * i:128 * i + 128], wtmb[:, kk, :], kk == 0, kk == 7)
            k.cp("act", vaug[:, i, :, 0:64], ps[:, 0:128].rearrange("p (g d) -> p g d", g=2))
            k.act(sgnw[:, i, :], ps[:, 128:136], AF.Sign)
            k.tt("dve", absw[:, i, :], ps[:, 128:136], sgnw[:, i, :], ALU.mult)
            k.ts("dve", absw[:, i, :], absw[:, i, :], 1.0 / 16.0, ALU.mult)
        if b == 0:
            k.memset("pool", vaug[:, :, :, 64:65], 1.0)
            k.memset("pool", mvaug[:, :, :, 64:65], 1.0)
        if debug and b == 0:
            dump("qT", qT, [128, 4, L], BF16)
            dump("kT", kT, [128, 2, L], BF16)
            dump("qiT", qiT, [128, 2, L], BF16)
            dump("kiT4", kiT4, [128, 4, L], BF16)
            dump("vaug", vaug, [128, NB, 2, 65], BF16)
            dump("absw", absw, [128, NB, 8])
            dump("sgnw", sgnw, [128, NB, 8])
            dump("qmT", qmT, [128, 2, L], BF16)
        if stop <= 1:
            continue

        A = Arena()
        mst = A.sb([8, 256], F32)
        mbf = A.sb([8, 256], BF16)
        wms = A.sb([8, 512], F32)
        wmb = A.sb([8, 512], BF16)
        P.dma(mst, memT[b].rearrange("(k p) c -> p k c", p=128), s_misc)
        P.dma(wms, wmem.rearrange("(k p) c -> p k c", p=128), s_misc2)
        k.cp("dve", mbf, mst)
        k.cp("pool", wmb, wms)
        for t in range(2):
            ps = P.ps(nextbank(), (256,))
            for kk in range(8):
                k.mm(ps, wmb[:, kk, 128 * t:128 * t + 128], mbf[:, kk, :], kk == 0, kk == 7)
            k.cp("act", mkT[:, t, :], ps)
        for mb_ in range(2):
            ps = P.ps(nextbank(), (256,))
            for kk in range(8):
                k.mm(ps, mbf[:, kk, 128 * mb_:128 * mb_ + 128], wmb[:, kk, 256:512], kk == 0, kk == 7)
            k.cp("act", mvaug[:, mb_, :, 0:64], ps.rearrange("p (h d) -> p h d", h=4))
        if debug and b == 0:
            dump("mkT", mkT, [128, 2, 256], BF16)
            dump("mvaug", mvaug, [128, 2, 4, 65], BF16)

        A = Arena()
        em = [A.sb([2, 512], BF16) for _ in range(2)]
        otm = [A.sb([128], BF16) for _ in range(2)]
        rd = A.sb([16], F32)
        nrd = 0
        for hp in range(2):
            for c in range(4):
                e_h = []
                for hh in range(2):
                    h = 2 * hp + hh
                    e = em[hh]
                    for mb_ in range(2):
                        ps = P.ps(nextbank(0, 4))
                        k.mm(ps, mkT[64 * hh:64 * hh + 64, hp, 128 * mb_:128 * mb_ + 128],
                             qmT[64 * hh:64 * hh + 64, hp, 512 * c:512 * c + 512], True, True)
                        k.act(e[:, mb_, :], ps, AF.Exp, scale=0.125)
                    e_h.append(e)
                for tb in range(4):
                    i = 4 * c + tb
                    o = otm[i % 2]
                    po = P.ps(4 + (i % 2), (2, 65))
                    for hh in range(2):
                        h = 2 * hp + hh
                        for mb_ in range(2):
                            k.mm(po[:, hh, :], e_h[hh][:, mb_, 128 * tb:128 * tb + 128], mvaug[:, mb_, h, :],
                                 (hh == 0 and mb_ == 0), mb_ == 1, skip=True)
                    r_ = rd[:, 2 * (nrd % 8):2 * (nrd % 8) + 2]
                    nrd += 1
                    P.op("dve", lambda e, r_=r_, po=po: e.reciprocal(out=r_, in_=po[:, :, 64]), [po], [r_])
                    k.tt("dve", o.rearrange("p (h d) -> p h d", h=2), po[:, :, 0:64],
                         r_.unsqueeze(2).broadcast_to([128, 2, 64]), ALU.mult)
                    pt = P.ps(6 + (i % 2), (128,), BF16)
                    k.tr(pt, o, identb)
                    k.cp("act", XC[:, 6 + hp, 128 * i:128 * i + 128], pt)
        if stop <= 4:
            continue

        A = Arena()
        NSC = 3
        sc = [A.sb([L], F32) for _ in range(NSC)]
        rt = [A.sb([512], F32) for _ in range(4)]
        junk = {"dve": A.sb([L], BF16), "act": A.sb([L], BF16)}
        mbk = [A.sb([L], BF16) for _ in range(NSC)]
        eg = [A.sb([4, 128], BF16) for _ in range(4)]
        oat = [A.sb([512], BF16) for _ in range(2)]
        scal = A.sb([NB, 32], F32)
        rda = A.sb([16], F32)
        nrt = [0]
        neg = [0]
        W0 = 32.0
        NIT = NBIS - 2

        def g_scores(i):
            S = 128 * (i + 1)
            s_ = sc[i % NSC]
            for c in range((S + 511) // 512):
                wc = min(512, S - 512 * c)
                sl = slice(512 * c, 512 * c + wc)
                for h in range(8):
                    ps = P.ps(nextbank(0, 4))[:, 0:wc]
                    k.mm(ps, qiT[:, h // 4, 128 * i:128 * i + 128], kiT4[:, h % 4, sl], True, True)
                    r_ = rt[nrt[0] % 4][:, 0:wc]
                    nrt[0] += 1
                    k.act(r_, ps, AF.Relu, scale=absw[:, i, h:h + 1])
                    if h == 0:
                        k.ts("dve", s_[:, sl], r_, sgnw[:, i, h:h + 1], ALU.mult)
                    else:
                        k.stt(s_[:, sl], r_, sgnw[:, i, h:h + 1], s_[:, sl], ALU.mult, ALU.add)
                    yield
            k.tt("pool", s_[:, 128 * i:128 * i + 128], s_[:, 128 * i:128 * i + 128], causal, ALU.add)
            yield

        def g_bisect(i):
            S = 128 * (i + 1)
            s_ = sc[i % NSC][:, 0:S]
            m = scal[:, i, 0:1]
            nm = scal[:, i, 1:2]
            c_ = scal[:, i, 2:3]
            a_ = scal[:, i, 3:4]
            eng = "dve" if i in (3, 5, 8, 10, 13, 15) else "act"
            if i < 2:
                k.memset("dve", m, -64.0)
                yield
            elif eng == "dve":
                k.memset("dve", m, 0.0)
                w = W0
                for it in range(NIT):
                    k.ts("dve", junk["dve"][:, 0:S], s_, m, ALU.is_ge, 0.0, ALU.add, accum_out=c_)
                    k.ts("dve", a_, c_, TOPK - 0.5, ALU.is_ge, w / 2, ALU.mult)
                    k.stt(m, a_, -w / 4, m, ALU.add, ALU.add)
                    w = w / 2
                    yield
                k.ts("dve", m, m, -w / 2, ALU.add)
            else:
                k.memset("dve", nm, 0.0)
                w = W0
                for it in range(NIT):
                    k.act(junk["act"][:, 0:S], s_, AF.Sign, bias=nm, accum_out=c_)
                    k.ts("dve", a_, c_, float(2 * TOPK - 1 - S), ALU.is_ge, -w / 2, ALU.mult)
                    k.stt(nm, a_, w / 4, nm, ALU.add, ALU.add)
                    w = w / 2
                    yield
                k.ts("dve", m, nm, -1.0, ALU.mult, -w / 2, ALU.add)
            k.ts("dve", mbk[i % NSC][:, 0:S], s_, m, ALU.is_lt, MASKNEG, ALU.mult)
            yield

        def g_attend(i):
            mb_ = mbk[i % NSC]
            po = [P.ps(6, (4, 65)), P.ps(7, (4, 65))]
            pend = [None]

            def av_step(j, par, e):
                for q_ in range(4):
                    h = par + 2 * q_
                    g = h // 4
                    k.mm(po[g][:, h % 4, :], e[:, q_, :], vaug[:, j, g, :], (j == 0 and par == 0 and h in (0, 4)), j == i, skip=True)
            for j in range(i + 1):
                for par in range(2):
                    lg = P.ps(4 + par, (4, 128))
                    k.mm(lg.rearrange("p r t -> p (r t)"), mb_[:, 128 * j:128 * j + 128], ident4.rearrange("p r t -> p (r t)"),
                         True, False, skip=True)
                    ro = 64 * par
                    for q_ in range(4):
                        h = par + 2 * q_
                        g = h // 4
                        k.mm(lg[:, q_, :], kT[ro:ro + 64, g, 128 * j:128 * j + 128], qT[ro:ro + 64, h // 2, 128 * i:128 * i + 128],
                             False, q_ == 3, skip=True)
                    e = eg[neg[0] % 4]
                    neg[0] += 1
                    k.act(e, lg, AF.Exp, scale=0.125)
                    if pend[0] is not None:
                        av_step(*pend[0])
                    pend[0] = (j, par, e)
                    yield
            av_step(*pend[0])
            yield

        def normalize(i):
            po = [P.ps(6, (4, 65)), P.ps(7, (4, 65))]
            o = oat[i % 2]
            for g in range(2):
                r_ = rda[:, 4 * ((2 * i + g) % 4):4 * ((2 * i + g) % 4) + 4]
                P.op("dve", lambda e, r_=r_, pg=po[g]: e.reciprocal(out=r_, in_=pg[:, :, 64]), [po[g]], [r_])
                k.tt("dve", o[:, 256 * g:256 * g + 256].rearrange("p (h d) -> p h d", h=4), po[g][:, :, 0:64],
                     r_.unsqueeze(2).broadcast_to([128, 4, 64]), ALU.mult)
            for t in range(4):
                pt = P.ps(nextbank(0, 4), (128,), BF16)
                k.tr(pt, o[:, 128 * t:128 * t + 128], identb)
                k.cp("act", XC[:, t, 128 * i:128 * i + 128], pt)

        def nsteps_bisect(i):
            return 1 if i < 2 else NIT + 1

        def run_interleaved(tasks):
            order = []
            for ti, (g_, n) in enumerate(tasks):
                for q_ in range(n):
                    order.append(((q_ + 0.5) / n, ti))
            order.sort()
            for _, ti in order:
                try:
                    next(tasks[ti][0])
                except StopIteration:
                    pass

        def drain(g_):
            for _ in g_:
                pass

        def nsteps_scores(i):
            return sum(8 for _ in range((128 * (i + 1) + 511) // 512)) + 1
        bis = {}
        for i in range(3):
            drain(g_scores(i))
        for n in (0, 1):
            bis[n] = g_bisect(n)
        drain(bis[0])
        half = lambda n: (nsteps_bisect(n) + 1) // 2
        run_interleaved([(bis[1], half(1))])
        for i in range(NB):
            tasks = []
            if i + 3 < NB:
                tasks.append((g_scores(i + 3), nsteps_scores(i + 3)))
            if i + 2 < NB:
                bis[i + 2] = g_bisect(i + 2)
                tasks.append((bis[i + 2], half(i + 2)))
            if i + 1 < NB:
                tasks.append((bis[i + 1], nsteps_bisect(i + 1) - half(i + 1) + 1))
            tasks.append((g_attend(i), 2 * (i + 1) + 1))
            run_interleaved(tasks)
            if i + 1 < NB:
                drain(bis[i + 1])
            normalize(i)
        if debug and b == 0:
            dump("catT", XC, [128, 8, L], BF16)
        if stop <= 5:
            continue

        A = Arena()
        gl = [A.sb([L], BF16) for _ in range(2)]
        xt = [A.sb([1024], F32) for _ in range(2)]
        hb = [A.sb([1024], F32) for _ in range(2)]
        ob = [A.sb([1024], F32) for _ in range(2)]
        st6 = A.sb([2, 32], F32)
        mv2 = A.sb([2, 32], F32)
        rs = A.sb([2, 32], F32)
        for et in range(2):
            P.dma(XC[:, 4 + et, :], ssc[b, et], s_ssl, after=[(s_ss[0], 16 * P.dma_cnt[s_ss[0]]), (s_ss[1], 16 * P.dma_cnt[s_ss[1]])])
        for kk in range(8):
            P.dma(gl[kk % 2], gsc[b, kk], s_gl[kk % 2], after=[(s_g[0], 16 * P.dma_cnt[s_g[0]])])
            k.tt("dve", XC[:, kk, :], XC[:, kk, :], gl[kk % 2], ALU.mult)
        P.dma(xt[0], xtm[b, 0:128, :], s_xt[0])
        for i in range(NB):
            if i + 1 < NB:
                P.dma(xt[(i + 1) % 2], xtm[b, 128 * (i + 1):128 * (i + 1) + 128, :], s_xt[(i + 1) % 2])
            h_ = hb[i % 2]
            for hf in range(2):
                ps = P.ps(nextbank(0, 8))
                for kk in range(8):
                    k.mm(ps, XC[:, kk, 128 * i:128 * i + 128], woutb[:, kk, 512 * hf:512 * hf + 512], kk == 0, kk == 7)
                k.stt(h_[:, 512 * hf:512 * hf + 512], xt[i % 2][:, 512 * hf:512 * hf + 512], DN_ALPHA, ps, ALU.mult, ALU.add)
                P.op("dve", lambda e, o_=st6[:, i % 2, 6 * hf:6 * hf + 6], a_=h_[:, 512 * hf:512 * hf + 512]: e.bn_stats(out=o_, in_=a_),
                     [h_[:, 512 * hf:512 * hf + 512]], [st6[:, i % 2, 6 * hf:6 * hf + 6]])
            mv = mv2[:, i % 2, 0:2]
            P.op("dve", lambda e, o_=mv, a_=st6[:, i % 2, 0:12]: e.bn_aggr(out=o_, in_=a_),
                 [st6[:, i % 2, 0:12]], [mv])
            r4 = rs[:, i % 2, 0:4]
            k.ts("dve", r4[:, 0:1], mv[:, 1:2], LN_EPS, ALU.add)
            k.act(r4[:, 1:2], r4[:, 0:1], AF.Sqrt)
            P.op("dve", lambda e, o_=r4[:, 2:3], a_=r4[:, 1:2]: e.reciprocal(out=o_, in_=a_), [r4[:, 1:2]], [r4[:, 2:3]])
            k.ts("dve", r4[:, 3:4], mv[:, 0:1], -1.0, ALU.mult, r4[:, 2:3], ALU.mult)
            o_ = ob[i % 2]
            k.act(o_, h_, AF.Identity, bias=r4[:, 3:4], scale=r4[:, 2:3])
            k.tt("dve", o_, o_, lng, ALU.mult)
            k.tt("pool", o_, o_, lnb, ALU.add)
            P.dma(out[b, 128 * i:128 * i + 128, :], o_, s_out[i % 2], is_output=True, eng="pool")

    P.emit()
    return nc, dbg


_CACHE = {}


def kernel(x, mem, w_in, w_mem_kv, lam_re, lam_im, log_dt, b_re, b_im, c_re, c_im, d_skip,
           w_glu, b_glu, w_out, ln_g, ln_b):
    x = np.asarray(x, np.float32)
    mem = np.asarray(mem, np.float32)
    f = lambda a: np.asarray(a, np.float32)
    hw = host_weights(f(w_in), f(w_mem_kv), f(lam_re), f(lam_im), f(log_dt), f(b_re), f(b_im), f(c_re), f(c_im),
                      f(d_skip), f(w_glu), f(b_glu), f(w_out), f(ln_g), f(ln_b))
    hc = host_consts()
    if "nc" not in _CACHE:
        _CACHE["nc"] = build()[0]
    nc = _CACHE["nc"]
    in_maps = []
    for c in range(8):
        xb = x[BPC * c:BPC * c + BPC]
        m = dict(hw)
        m.update(hc)
        m["x"] = np.ascontiguousarray(xb)
        m["xT"] = np.ascontiguousarray(xb.transpose(0, 2, 1))
        m["memT"] = np.ascontiguousarray(mem[BPC * c:BPC * c + BPC].transpose(0, 2, 1))
        in_maps.append(m)
    res = run_bass_kernel_spmd(nc, in_maps, core_ids=list(range(8)))
    return np.concatenate([r["out"] for r in res.results], axis=0).astype(np.float32)
```
